# Optimizing a Trainium2 kernel written in Bass

```python
import math, functools
import jax, jax.numpy as jnp
from jax import lax
import numpy as np

D_MODEL = 1024
BATCH = 8
SEQ = 2048
DEPTH = 1
DEC_BATCH = 32
DEC_SEQ = 1
PAST_LEN = 16384
PAGE_SIZE = 128

MIX_WIDTH = D_MODEL
HG_WIDTH = MIX_WIDTH // 2
HG_HEADS = 4
HG_DK = HG_WIDTH // HG_HEADS
HG_DV = HG_WIDTH // HG_HEADS
HG_CHUNK = 64
DA_WIDTH = MIX_WIDTH - HG_WIDTH
DA_HEADS = 4
DA_DV = DA_WIDTH // DA_HEADS
DA_DH = DA_DV // 2
DA_DK = 2 * DA_DH
Q_BLOCK = 128
ROPE_THETA = 10000.0
N_MEM = 256
MEM_HEADS = 4
MEM_DH = D_MODEL // MEM_HEADS
D_FF = -(-8 * D_MODEL // (3 * 256)) * 256
IN_COLS = 4 * HG_WIDTH + 3 * DA_WIDTH
EPS = 1e-6

kernel_name = "hymba_hgrn2_diffattn_decode_step"

F32 = jnp.float32


def rmsnorm(x, g):
    x32 = x.astype(F32)
    y = x32 * lax.rsqrt(jnp.mean(x32 * x32, axis=-1, keepdims=True) + EPS)
    return (y * g.astype(F32)).astype(x.dtype)


def rope(x, pos):
    d = x.shape[-1]
    inv = ROPE_THETA ** (-jnp.arange(0, d, 2, dtype=F32) / d)
    ang = pos.astype(F32)[:, None] * inv[None, :]
    cos = jnp.cos(ang)[:, None, :]
    sin = jnp.sin(ang)[:, None, :]
    x32 = x.astype(F32)
    x1, x2 = x32[..., : d // 2], x32[..., d // 2:]
    return jnp.concatenate([x1 * cos - x2 * sin, x1 * sin + x2 * cos], axis=-1).astype(x.dtype)


def mixer_inputs(h, pos, w_in_l, lb_l):
    B, L, _ = h.shape
    z = h @ w_in_l
    sizes = [HG_WIDTH] * 4 + [DA_WIDTH] * 3
    splits = [int(s) for s in np.cumsum(sizes)[:-1]]
    hq, hf, hi, hg, dq, dk, dv = jnp.split(z, splits, axis=-1)
    lb = lb_l.reshape(HG_HEADS, HG_DK)
    zf = hf.astype(F32).reshape(B, L, HG_HEADS, HG_DK)
    logf = jnp.log(lb + (1.0 - lb) * jax.nn.sigmoid(zf))
    k_hg = (1.0 - lb) * jax.nn.sigmoid(-zf)
    q_hg = hq.astype(F32).reshape(B, L, HG_HEADS, HG_DK)
    v_hg = hi.astype(F32).reshape(B, L, HG_HEADS, HG_DV)
    q_da = rope(dq.reshape(B, L, DA_HEADS * 2, DA_DH), pos).reshape(B, L, DA_HEADS, 2, DA_DH)
    k_da = rope(dk.reshape(B, L, DA_HEADS * 2, DA_DH), pos).reshape(B, L, DA_HEADS, 2, DA_DH)
    v_da = dv.reshape(B, L, DA_HEADS, DA_DV)
    return q_hg, k_hg, logf, v_hg, hg, q_da, k_da, v_da


def hgrn2_recurrence(q, k, logf, v, s0):
    B, L, H, DK = q.shape
    DV = v.shape[-1]
    C = HG_CHUNK if L % HG_CHUNK == 0 else L
    n = L // C

    def to_chunks(a):
        return a.reshape(B, n, C, H, a.shape[-1]).transpose(1, 0, 3, 2, 4)

    qc, kc, gc, vc = to_chunks(q), to_chunks(k), to_chunks(logf), to_chunks(v)
    causal = jnp.tril(jnp.ones((C, C), dtype=bool))

    def step(S, inp):
        qb, kb, gb, vb = inp
        b = jnp.cumsum(gb, axis=2)
        diff = b[:, :, :, None, :] - b[:, :, None, :, :]
        decay = jnp.exp(jnp.where(causal[:, :, None], diff, -jnp.inf))
        A = jnp.einsum('bhtk,bhsk,bhtsk->bhts', qb, kb, decay)
        o = jnp.einsum('bhts,bhsv->bhtv', A, vb) + jnp.einsum('bhtk,bhkv->bhtv', qb * jnp.exp(b), S)
        b_last = b[:, :, -1:, :]
        S_new = jnp.exp(b_last[:, :, 0, :])[..., None] * S + jnp.einsum(
            'bhsk,bhsv->bhkv', kb * jnp.exp(b_last - b), vb)
        return S_new, o

    S, o = lax.scan(step, s0.astype(F32), (qc, kc, gc, vc))
    o = o.transpose(1, 0, 3, 2, 4).reshape(B, L, H, DV)
    return o, S


def diff_weights(s, lam):
    p = jax.nn.softmax(s.astype(F32), axis=-1)
    return p[:, :, 0] - lam * p[:, :, 1]


def diff_attn_prompt(q, k, v, lam):
    B, S, H, _, DH = q.shape
    nb = S // Q_BLOCK
    scale = DH ** -0.5
    qb = q.reshape(B, nb, Q_BLOCK, H, 2, DH).transpose(1, 0, 2, 3, 4, 5)
    kpos = jnp.arange(S)

    def block(args):
        qi, i = args
        qpos = i * Q_BLOCK + jnp.arange(Q_BLOCK)
        s = jnp.einsum('bqhmd,bkhmd->bhmqk', qi, k).astype(F32) * scale
        s = jnp.where((kpos[None, :] <= qpos[:, None])[None, None, None], s, -jnp.inf)
        w = diff_weights(s, lam)
        return jnp.einsum('bhqk,bkhv->bqhv', w.astype(v.dtype), v)

    o = lax.map(block, (qb, jnp.arange(nb)))
    return o.transpose(1, 0, 2, 3, 4).reshape(B, S, H, v.shape[-1])


def diff_attn_sample(q, k, v, lam, k_past, v_past):
    T = q.shape[1]
    P = k_past.shape[1]
    scale = q.shape[-1] ** -0.5
    s_past = jnp.einsum('bqhmd,bkhmd->bhmqk', q, k_past).astype(F32) * scale
    s_new = jnp.einsum('bqhmd,bkhmd->bhmqk', q, k).astype(F32) * scale
    s_new = jnp.where(jnp.tril(jnp.ones((T, T), dtype=bool))[None, None, None], s_new, -jnp.inf)
    w = diff_weights(jnp.concatenate([s_past, s_new], axis=-1), lam).astype(v.dtype)
    return (jnp.einsum('bhqk,bkhv->bqhv', w[..., :P], v_past)
            + jnp.einsum('bhqk,bkhv->bqhv', w[..., P:], v))


def mem_kv(mem, g, wk, wv):
    B = mem.shape[0]
    m = rmsnorm(mem, g)
    k = (m @ wk).reshape(B, -1, MEM_HEADS, MEM_DH)
    v = (m @ wv).reshape(B, -1, MEM_HEADS, MEM_DH)
    return k, v


def cross_attn(h, mk, mv, wq, wo):
    B, L, _ = h.shape
    q = (h @ wq).reshape(B, L, MEM_HEADS, MEM_DH)
    s = jnp.einsum('blhd,bnhd->bhln', q, mk).astype(F32) * (MEM_DH ** -0.5)
    p = jax.nn.softmax(s, axis=-1).astype(mv.dtype)
    o = jnp.einsum('bhln,bnhd->blhd', p, mv).reshape(B, L, MEM_HEADS * MEM_DH)
    return o @ wo


def swiglu(h, wg, wu, wd):
    return (jax.nn.silu(h @ wg) * (h @ wu)) @ wd


def setup_inputs(seed: int = 0) -> dict:
    key = jax.random.key(seed)
    ks = jax.random.split(key, 32)
    n_pages = PAST_LEN // PAGE_SIZE
    n_used = DEC_BATCH * n_pages
    n_phys = n_used + n_used // 4
    nrm = lambda k, shape, s=1.0: jax.random.normal(k, shape, dtype=F32) * s
    gain = lambda k, shape: 1.0 + 0.02 * jax.random.normal(k, shape, dtype=F32)
    page_table = jax.random.permutation(ks[0], n_phys)[:n_used].reshape(DEC_BATCH, n_pages).astype(jnp.int32)
    return {
        "x_prompt": nrm(ks[1], (BATCH, SEQ, D_MODEL)),
        "x_sample": nrm(ks[2], (DEC_BATCH, DEC_SEQ, D_MODEL)),
        "mem_prompt": nrm(ks[3], (BATCH, N_MEM, D_MODEL)),
        "cache_k": nrm(ks[4], (DEPTH, n_phys, PAGE_SIZE, DA_HEADS, DA_DK)),
        "cache_v": nrm(ks[5], (DEPTH, n_phys, PAGE_SIZE, DA_HEADS, DA_DV)),
        "cache_mem_k": nrm(ks[6], (DEPTH, DEC_BATCH, N_MEM, MEM_HEADS, MEM_DH)),
        "cache_mem_v": nrm(ks[7], (DEPTH, DEC_BATCH, N_MEM, MEM_HEADS, MEM_DH)),
        "state_hgrn": nrm(ks[8], (DEPTH, DEC_BATCH, HG_HEADS, HG_DK, HG_DV), 0.5),
        "page_table": page_table,
        "norm_mix": gain(ks[9], (DEPTH, D_MODEL)),
        "w_in": nrm(ks[10], (DEPTH, D_MODEL, IN_COLS), D_MODEL ** -0.5),
        "hg_lb": nrm(ks[11], (DEPTH + 1, HG_WIDTH), 0.5),
        "hg_onorm": gain(ks[12], (DEPTH, HG_DV)),
        "da_lambda": nrm(ks[13], (DEPTH, 4, DA_DH), 0.1),
        "da_onorm": gain(ks[14], (DEPTH, DA_DV)),
        "w_out": nrm(ks[15], (DEPTH, MIX_WIDTH, D_MODEL), MIX_WIDTH ** -0.5),
        "norm_mem_q": gain(ks[16], (DEPTH, D_MODEL)),
        "norm_mem_kv": gain(ks[17], (DEPTH, D_MODEL)),
        "w_mq": nrm(ks[18], (DEPTH, D_MODEL, MEM_HEADS * MEM_DH), D_MODEL ** -0.5),
        "w_mk": nrm(ks[19], (DEPTH, D_MODEL, MEM_HEADS * MEM_DH), D_MODEL ** -0.5),
        "w_mv": nrm(ks[20], (DEPTH, D_MODEL, MEM_HEADS * MEM_DH), D_MODEL ** -0.5),
        "w_mo": nrm(ks[21], (DEPTH, MEM_HEADS * MEM_DH, D_MODEL), (MEM_HEADS * MEM_DH) ** -0.5),
        "norm_ffn": gain(ks[22], (DEPTH, D_MODEL)),
        "w_gate": nrm(ks[23], (DEPTH, D_MODEL, D_FF), D_MODEL ** -0.5),
        "w_up": nrm(ks[24], (DEPTH, D_MODEL, D_FF), D_MODEL ** -0.5),
        "w_down": nrm(ks[25], (DEPTH, D_FF, D_MODEL), D_FF ** -0.5),
        "norm_final": gain(ks[26], (D_MODEL,)),
    }


def reference(x_prompt, x_sample, mem_prompt, cache_k, cache_v, cache_mem_k, cache_mem_v, state_hgrn,
              page_table, norm_mix, w_in, hg_lb, hg_onorm, da_lambda, da_onorm, w_out, norm_mem_q,
              norm_mem_kv, w_mq, w_mk, w_mv, w_mo, norm_ffn, w_gate, w_up, w_down, norm_final):
    lb_all = jnp.cumsum(jax.nn.softmax(hg_lb.astype(F32), axis=0), axis=0)
    dec_b = x_sample.shape[0]
    past_len = page_table.shape[1] * cache_k.shape[2]
    pos_p = jnp.arange(x_prompt.shape[1])
    pos_s = past_len + jnp.arange(x_sample.shape[1])

    def run_layer(x, pos, hg_s0, attend, mk, mv, l):
        B, L, _ = x.shape
        h = rmsnorm(x, norm_mix[l])
        q_hg, k_hg, logf, v_hg, g_hg, q_da, k_da, v_da = mixer_inputs(h, pos, w_in[l], lb_all[l])
        o_hg, s_hg = hgrn2_recurrence(q_hg, k_hg, logf, v_hg, hg_s0)
        o_hg = rmsnorm(o_hg, hg_onorm[l]) * jax.nn.silu(g_hg.astype(F32)).reshape(B, L, HG_HEADS, HG_DV)
        lam_init = 0.8 - 0.6 * math.exp(-0.3 * l)
        lp = da_lambda[l].astype(F32)
        lam = jnp.exp(jnp.sum(lp[0] * lp[1])) - jnp.exp(jnp.sum(lp[2] * lp[3])) + lam_init
        o_da = attend(q_da, k_da, v_da, lam)
        o_da = rmsnorm(o_da, da_onorm[l]) * (1.0 - lam_init)
        mix = jnp.concatenate([o_hg.reshape(B, L, HG_WIDTH).astype(x.dtype),
                               o_da.reshape(B, L, DA_WIDTH).astype(x.dtype)], axis=-1)
        x = x + mix @ w_out[l]
        x = x + cross_attn(rmsnorm(x, norm_mem_q[l]), mk, mv, w_mq[l], w_mo[l])
        x = x + swiglu(rmsnorm(x, norm_ffn[l]), w_gate[l], w_up[l], w_down[l])
        return x, s_hg.astype(x.dtype), k_da.reshape(B, L, DA_HEADS, DA_DK), v_da

    xp, xs = x_prompt, x_sample
    hsp, kp_l, vp_l, mkp_l, mvp_l, hss, ks_l, vs_l = [], [], [], [], [], [], [], []
    for l in range(DEPTH):
        mk, mv = mem_kv(mem_prompt, norm_mem_kv[l], w_mk[l], w_mv[l])
        s0 = jnp.zeros((xp.shape[0], HG_HEADS, HG_DK, HG_DV), F32)
        xp, s_p, k_p, v_p = run_layer(xp, pos_p, s0, diff_attn_prompt, mk, mv, l)
        hsp.append(s_p); kp_l.append(k_p); vp_l.append(v_p); mkp_l.append(mk); mvp_l.append(mv)
        k_past = cache_k[l][page_table].reshape(dec_b, past_len, DA_HEADS, 2, DA_DH)
        v_past = cache_v[l][page_table].reshape(dec_b, past_len, DA_HEADS, DA_DV)
        attend_s = functools.partial(diff_attn_sample, k_past=k_past, v_past=v_past)
        xs, s_s, k_s, v_s = run_layer(xs, pos_s, state_hgrn[l], attend_s, cache_mem_k[l], cache_mem_v[l], l)
        hss.append(s_s); ks_l.append(k_s); vs_l.append(v_s)

    y_prompt = rmsnorm(xp, norm_final)
    y_sample = rmsnorm(xs, norm_final)
    return (y_prompt, y_sample, jnp.stack(hsp), jnp.stack(kp_l), jnp.stack(vp_l), jnp.stack(mkp_l),
            jnp.stack(mvp_l), jnp.stack(hss), jnp.stack(ks_l), jnp.stack(vs_l))
```

```python
import os
import numpy as np
from contextlib import ExitStack
import concourse.bass as bass
import concourse.mybir as mybir
from concourse.bass_utils import run_bass_kernel_spmd

F32 = mybir.dt.float32
BF16 = mybir.dt.bfloat16
I32 = mybir.dt.int32
AF = mybir.ActivationFunctionType
ALU = mybir.AluOpType
AX = mybir.AxisListType

D = 1024
SEQ = 2048
NT = SEQ // 128
NS = 4
DFF = 2816
NPAGES = 128
EPS = 1e-6
LAM_INIT = 0.2
RING = 16
SL = int(os.environ.get("SL", "9"))
NPG = int(os.environ.get("NPG", "128"))
DL = int(os.environ.get("DL", "9"))
CL = int(os.environ.get("CL", "9"))


class Prog:
    ENGS = ("pe", "act", "dve", "pool", "sp")

    def __init__(self):
        self.ops = {e: [] for e in self.ENGS}
        self.res = {}
        self.dma_n = {"sp": 0, "pool": 0}

    def _deps(self, r, w):
        deps = set()
        for k in r:
            st = self.res.get(k)
            if st and st[0] is not None:
                deps.add(st[0])
        for k in w:
            st = self.res.get(k)
            if st:
                if st[0] is not None:
                    deps.add(st[0])
                deps.update(st[1])
        return deps

    def _update(self, r, w, ev):
        for k in r:
            self.res.setdefault(k, [None, []])[1].append(ev)
        for k in w:
            self.res[k] = [ev, []]

    def op(self, eng, fn, r=(), w=()):
        deps = self._deps(r, w)
        ev = ("c", eng, len(self.ops[eng]))
        self.ops[eng].append((fn, deps, ev))
        self._update(r, w, ev)

    def dma(self, q, fn, r=(), w=()):
        n = self.dma_n[q]
        self.dma_n[q] += 1
        ring, val = n % RING, 16 * (n // RING + 1)
        deps = self._deps(r, w)
        if n >= RING:
            deps.add(("d", q, ring, val - 16))
        ev = ("d", q, ring, val)
        self.ops[q].append((fn, deps, ev))
        self._update(r, w, ev)

    def emit(self, eng, h, sems, dsems):
        seen = {}
        for fn, deps, ev in self.ops[eng]:
            for d in sorted(deps):
                if d[0] == "c":
                    if d[1] == eng and eng == "pe":
                        continue
                    if seen.get(d[1], -1) >= d[2]:
                        continue
                    seen[d[1]] = d[2]
                    h.wait_ge(sems[d[1]], d[2] + 1)
                else:
                    key = (d[1], d[2])
                    if seen.get(key, 0) >= d[3]:
                        continue
                    seen[key] = d[3]
                    h.wait_ge(dsems[d[1]][d[2]], d[3])
            ins = fn(h)
            if ev[0] == "c":
                ins.then_inc(sems[eng], 1)
            else:
                ins.then_inc(dsems[ev[1]][ev[2]], 16)
        if eng == "sp":
            for e2 in self.ENGS:
                if e2 != "sp" and self.ops[e2]:
                    ncomp = sum(1 for o in self.ops[e2] if o[2][0] == "c")
                    if ncomp:
                        h.wait_ge(sems[e2], ncomp)
            for q in ("sp", "pool"):
                n = self.dma_n[q]
                for ring in range(min(n, RING)):
                    cnt = (n - ring + RING - 1) // RING
                    h.wait_ge(dsems[q][ring], 16 * cnt)


def build(nphys, stage=99):
    nc = bass.Bass("TRN2", target_bir_lowering=False)
    P = Prog()
    es = ExitStack()

    def din(name, shape, dt=F32):
        return nc.dram_tensor(name, list(shape), dt, kind="ExternalInput").ap()

    def dout(name, shape, dt=F32):
        return nc.dram_tensor(name, list(shape), dt, kind="ExternalOutput").ap()

    def dscr(name, shape, dt=F32):
        return nc.dram_tensor(name, list(shape), dt, kind="Internal").ap()

    cnt = [0]

    def sb(shape, dt=F32, name=None):
        cnt[0] += 1
        return es.enter_context(nc.sbuf_tensor(name or f"sb{cnt[0]}", list(shape), dt))

    def ps(shape, dt=F32, name=None):
        cnt[0] += 1
        return es.enter_context(nc.psum_tensor(name or f"ps{cnt[0]}", list(shape), dt))

    xp = din("xp", [SEQ, D]); xs = din("xs", [NS, D]); mem = din("mem", [256, D])
    cmk = din("cmk", [NS * 256, D]); cmv = din("cmv", [NS * 256, D])
    sh = din("sh", [NS * 4 * 128, 128]); pt = din("pt", [NS, NPAGES], I32)
    w_in = din("w_in", [D, 3584]); w_out = din("w_out", [D, D])
    w_mq = din("w_mq", [D, D]); w_mk = din("w_mk", [D, D]); w_mv = din("w_mv", [D, D]); w_mo = din("w_mo", [D, D])
    w_gate = din("w_gate", [D, DFF]); w_up = din("w_up", [D, DFF]); w_down = din("w_down", [DFF, D])
    g_mix = din("g_mix", [128, 8]); g_mq = din("g_mq", [128, 8]); g_mkv = din("g_mkv", [128, 8]); g_ffn = din("g_ffn", [128, 8])
    g_fin = din("g_fin", [1, D]); hg_lb = din("hg_lb", [2, 512]); hg_on = din("hg_on", [1, 128]); da_on = din("da_on", [1, 128])
    da_lam = din("da_lam", [1, 256])
    c_ident = din("c_ident", [128, 128]); c_caus = din("c_caus", [128, 128]); c_u1 = din("c_u1", [128, 128]); c_u2 = din("c_u2", [128, 128])
    c_am = din("c_am", [128, 128]); c_ind = din("c_ind", [128, 2])
    c_i4 = din("c_i4", [1, 16]); c_iota = din("c_iota", [128, 1]); c_cpos = din("c_cpos", [8, 4]); c_cneg = din("c_cneg", [8, 4]); c_bm = din("c_bm", [4, 512])
    ck = din("ck", [nphys * 64, 1024]); cv = din("cv", [nphys * 64, 1024])
    c_cos = din("c_cos", [SEQ, 32]); c_sin = din("c_sin", [SEQ, 32]); c_coss = din("c_coss", [1, 32]); c_sins = din("c_sins", [1, 32])

    yp = dout("yp", [SEQ, D]); ys = dout("ys", [NS, D]); hsp = dout("hsp", [512, 128]); kp = dout("kp", [SEQ, 512]); vp = dout("vp", [SEQ, 512])
    mkp = dout("mkp", [256, D]); mvp = dout("mvp", [256, D]); hss = dout("hss", [NS * 512, 128]); ks = dout("ks", [NS, 512]); vs = dout("vs", [NS, 512])

    xres = dscr("xres", [SEQ + 128, D])
    facc = dscr("facc", [SEQ + 128, D])

    ident_f = sb([128, 128]); ident_b = sb([128, 128], BF16); caus_b = sb([128, 128], BF16)
    u1 = sb([128, 128]); u2 = sb([128, 128]); am = sb([128, 128]); ind = sb([128, 2]); ones_b = sb([128, 128], BF16); ones_f = sb([128, 128])
    gcol = {k: sb([128, 8], name="gc_" + k) for k in ("mix", "mq", "mkv", "ffn")}
    lb_b = sb([128, 512]); oml_b = sb([128, 512]); lbt = sb([128, 1024], name="mix")
    hgon_b = sb([128, 128]); daon_b = sb([128, 128]); lam_t = sb([128, 256]); lamc = sb([128, 4]); neglam = sb([128, 1])
    cos_t = sb([128, NT, 32]); sin_t = sb([128, NT, 32]); coss_t = sb([128, 32]); sins_t = sb([128, 32])

    def ld(dst, src, keys_w, q="sp", keys_r=()):
        P.dma(q, lambda h, d=dst, s=src: h.dma_start(out=d, in_=s), r=keys_r, w=keys_w)

    def bc(ap, n):
        return ap.broadcast(0, n) if hasattr(ap, "broadcast") else ap

    ld(ident_f[:], c_ident, ["ident_f"]); ld(u1[:], c_u1, ["u1"]); ld(u2[:], c_u2, ["u2"]); ld(am[:], c_am, ["am"]); ld(ind[:], c_ind, ["ind"])
    ld(lbt[:, 0:128], c_caus, ["mix"])
    P.op("dve", lambda h: h.tensor_copy(out=ident_b[:], in_=ident_f[:]), r=["ident_f"], w=["ident_b"])
    P.op("dve", lambda h: h.tensor_copy(out=caus_b[:], in_=lbt[:, 0:128]), r=["mix"], w=["caus_b"])
    P.op("dve", lambda h: h.memset(ones_b[:], 1.0), w=["ones_b"])
    P.op("dve", lambda h: h.memset(ones_f[:], 1.0), w=["ones_f"])
    for k, g in (("mix", g_mix), ("mq", g_mq), ("mkv", g_mkv), ("ffn", g_ffn)):
        ld(gcol[k][:], g, ["gc_" + k])
    ld(hgon_b[:], hg_on.partition_broadcast(128), ["hgon_b"]); ld(daon_b[:], da_on.partition_broadcast(128), ["daon_b"])
    ld(lam_t[:], da_lam.partition_broadcast(128), ["lam_t"])
    ld(cos_t[:], c_cos.rearrange("(t p) f -> p t f", p=128), ["cos_t"]); ld(sin_t[:], c_sin.rearrange("(t p) f -> p t f", p=128), ["sin_t"])
    ld(coss_t[:], c_coss.partition_broadcast(128), ["coss_t"]); ld(sins_t[:], c_sins.partition_broadcast(128), ["sins_t"])
    ld(lbt[:, 0:512], hg_lb[0:1, :].partition_broadcast(128), ["mix"], keys_r=["caus_b"])
    ld(lbt[:, 512:1024], hg_lb[1:2, :].partition_broadcast(128), ["lbt2"], keys_r=["caus_b"])
    P.op("dve", lambda h: h.tensor_tensor(out=lb_b[:], in0=lbt[:, 0:512], in1=lbt[:, 512:1024], op=ALU.subtract), r=["mix", "lbt2"], w=["lb_b"])
    P.op("act", lambda h: h.activation(out=lb_b[:], in_=lb_b[:], func=AF.Sigmoid), r=["lb_b"], w=["lb_b"])
    P.op("dve", lambda h: h.tensor_scalar(out=oml_b[:], in0=lb_b[:], scalar1=-1.0, scalar2=1.0, op0=ALU.mult, op1=ALU.add), r=["lb_b"], w=["oml_b"])
    P.op("dve", lambda h: h.tensor_tensor(out=lam_t[:, 0:64], in0=lam_t[:, 0:64], in1=lam_t[:, 64:128], op=ALU.mult), r=["lam_t"], w=["lam_t"])
    P.op("dve", lambda h: h.tensor_tensor(out=lam_t[:, 128:192], in0=lam_t[:, 128:192], in1=lam_t[:, 192:256], op=ALU.mult), r=["lam_t"], w=["lam_t"])
    P.op("dve", lambda h: h.reduce_sum(out=lamc[:, 0:1], in_=lam_t[:, 0:64], axis=AX.X), r=["lam_t"], w=["lamc"])
    P.op("dve", lambda h: h.reduce_sum(out=lamc[:, 1:2], in_=lam_t[:, 128:192], axis=AX.X), r=["lam_t", "lamc"], w=["lamc"])
    P.op("act", lambda h: h.activation(out=lamc[:, 2:4], in_=lamc[:, 0:2], func=AF.Exp), r=["lamc"], w=["lamc"])
    P.op("dve", lambda h: h.tensor_tensor(out=neglam[:], in0=lamc[:, 3:4], in1=lamc[:, 2:3], op=ALU.subtract), r=["lamc"], w=["neglam"])
    P.op("dve", lambda h: h.tensor_scalar(out=neglam[:], in0=neglam[:], scalar1=-LAM_INIT, scalar2=None, op0=ALU.add), r=["neglam"], w=["neglam"])

    psA = [ps([128, 512], name=f"psA{i}") for i in range(2)]
    psT = ps([128, 1024], BF16, name="psT")
    psF = ps([128, 512], name="psF")
    psS = [ps([128, 512], name=f"psS{i}") for i in range(2)]
    psO = [ps([128, 512], name=f"psO{i}") for i in range(2)]
    rr = {"A": 0, "S": 0, "O": 0}

    def nxt(kind):
        rr[kind] ^= 1
        lst = {"A": psA, "S": psS, "O": psO}[kind]
        return lst[rr[kind]], f"ps{kind}{rr[kind]}"

    stg = [sb([128, 2, 512], name=f"stg{i}") for i in range(2)]
    stg_i = [0]

    def load_weight(wd, dst, ncols, gk=None, rows=D, key=None, col0=0):
        nk = rows // 128
        wv = wd.rearrange("(k p) n -> p k n", p=128)
        for c0 in range(0, ncols, 512):
            cw = min(512, ncols - c0)
            for k0 in range(0, nk, 2):
                kw = min(2, nk - k0)
                s = stg_i[0] & 1
                stg_i[0] += 1
                st = stg[s]
                P.dma("sp", lambda h, st=st, k0=k0, kw=kw, c0=c0, cw=cw: h.dma_start(out=st[:, 0:kw, 0:cw], in_=wv[:, k0:k0 + kw, col0 + c0:col0 + c0 + cw]),
                      w=[f"stg{s}"])
                for kq in range(kw):
                    if gk is None:
                        P.op("act", lambda h, st=st, k0=k0, kq=kq, c0=c0, cw=cw: h.activation(out=dst[:, k0 + kq, c0:c0 + cw], in_=st[:, kq, 0:cw], func=AF.Copy),
                             r=[f"stg{s}"], w=[key])
                    else:
                        g = gcol[gk]
                        P.op("act", lambda h, st=st, k0=k0, kq=kq, c0=c0, cw=cw, g=g: h.activation(out=dst[:, k0 + kq, c0:c0 + cw], in_=st[:, kq, 0:cw], func=AF.Copy,
                                                                                              scale=g[:, k0 + kq:k0 + kq + 1]),
                             r=[f"stg{s}", "gc_" + gk], w=[key])

    xt = [sb([128, D], name=f"xt{i}") for i in range(2)]
    xo = sb([128, D], name="xo"); kvtmp = xo
    hb = sb([128, D], BF16, name="hb"); junk = sb([128, D], BF16, name="junk")
    hT = sb([128, 8, 128], BF16, name="hT")
    ssq = sb([128, 4], name="ssq")
    xt_i = [0]

    def load_rows(src_ap, T, rk=()):
        s = xt_i[0] & 1
        xt_i[0] += 1
        P.dma("sp", lambda h: h.dma_start(out=xt[s][0:T, :], in_=src_ap), r=list(rk), w=[f"xt{s}"])
        return xt[s], f"xt{s}"

    def rstd_of(x_ap, T, xkey, width=D, col=0):
        P.op("act", lambda h: h.activation(out=junk[0:T, 0:width], in_=x_ap, func=AF.Square, accum_out=ssq[0:T, col:col + 1]), r=[xkey], w=["junk", "ssq"])
        P.op("act", lambda h: h.activation(out=ssq[0:T, col:col + 1], in_=ssq[0:T, col:col + 1], func=AF.Sqrt, scale=1.0 / width, bias=EPS), r=["ssq"], w=["ssq"])
        P.op("dve", lambda h: h.reciprocal(out=ssq[0:T, col:col + 1], in_=ssq[0:T, col:col + 1]), r=["ssq"], w=["ssq"])

    def transpose_to(dst, dkey, src_bf, skey, T, nchunks=8):
        for c in range(nchunks):
            P.op("pe", lambda h, c=c: h.transpose(out=psT[:, c * 128:c * 128 + T], in_=src_bf[0:T, c * 128:(c + 1) * 128], identity=ident_b[0:T, 0:T]),
                 r=[skey, "ident_b"], w=["psT"])
        P.op("act", lambda h: h.activation(out=dst[:, 0:nchunks, 0:T], in_=psT[:, 0:nchunks * 128].rearrange("p (c t) -> p c t", t=128)[:, :, 0:T], func=AF.Copy),
             r=["psT"], w=[dkey])

    def norm_T(x_ap, xkey, T):
        rstd_of(x_ap, T, xkey)
        P.op("dve", lambda h: h.tensor_scalar(out=hb[0:T, :], in0=x_ap, scalar1=ssq[0:T, 0:1], scalar2=None, op0=ALU.mult), r=[xkey, "ssq"], w=["hb"])
        transpose_to(hT, "hT", hb, "hb", T)

    def linear(lT, lkey, T, Wb, wkey, c0, cw, nk=8):
        pt_, pkey = nxt("A")
        for k in range(nk):
            P.op("pe", lambda h, k=k: h.matmul(pt_[0:T, 0:cw], lhsT=lT[:, k, 0:T], rhs=Wb[:, k, c0:c0 + cw], start=(k == 0), stop=(k == nk - 1)),
                 r=[lkey, wkey], w=[pkey])
        return pt_, pkey

    wbig = sb([128, 8, 3584], BF16, name="wbig")
    memKT = sb([128, 8, 256], BF16, name="memKT")
    memV = sb([128, 2, D], BF16, name="memV")
    kvb = sb([128, D], BF16, name="kvb")

    def prep_mem_k(k_ap, kkey, nt_, KT, ktkey):
        P.op("dve", lambda h: h.tensor_copy(out=kvb[:], in_=k_ap), r=[kkey], w=["kvb"])
        for c in range(8):
            P.op("pe", lambda h, c=c: h.transpose(out=psT[:, c * 128:(c + 1) * 128], in_=kvb[:, c * 128:(c + 1) * 128], identity=ident_b[:]),
                 r=["kvb", "ident_b"], w=["psT"])
        P.op("act", lambda h: h.activation(out=KT[:, :, nt_ * 128:(nt_ + 1) * 128], in_=psT[:].rearrange("p (c t) -> p c t", t=128), func=AF.Copy),
             r=["psT"], w=[ktkey])

    load_weight(w_mk, wbig, D, gk="mkv", key="wbig")
    load_weight(w_mv, wbig[:, :, 1024:2048], D, gk="mkv", key="wbig")
    for nt_ in range(2):
        x_, xk = load_rows(mem[nt_ * 128:(nt_ + 1) * 128, :], 128)
        norm_T(x_[:], xk, 128)
        for which in range(2):
            for cb in range(2):
                p_, pk = linear(hT, "hT", 128, wbig, "wbig", which * 1024 + cb * 512, 512)
                P.op("dve", lambda h, p_=p_, cb=cb: h.tensor_copy(out=kvtmp[:, cb * 512:(cb + 1) * 512], in_=p_[:]), r=[pk], w=["xo"])
            outd = (mkp, mvp)[which]
            P.dma("sp", lambda h, outd=outd, nt_=nt_: h.dma_start(out=outd[nt_ * 128:(nt_ + 1) * 128, :], in_=kvtmp[:]), r=["xo"], w=["out_mkv"])
            if which == 0:
                prep_mem_k(kvtmp[:], "xo", nt_, memKT, "memKT")
            else:
                P.op("dve", lambda h, nt_=nt_: h.tensor_copy(out=memV[:, nt_, :], in_=kvtmp[:]), r=["xo"], w=["memV"])


    wout_b = sb([128, 8, D], BF16, name="wout_b")
    load_weight(w_in, wbig, 3584, gk="mix", key="wbig")
    load_weight(w_out, wout_b, D, key="wout_b")
    big2 = sb([128, 16640], BF16, name="big2")
    KT = big2[:, 0:8192].rearrange("p (c t) -> p c t", t=SEQ)
    Vaug = big2[:, 8192:8192 + 8448].rearrange("p (t g f) -> p t g f", g=4, f=132)
    P.op("pool", lambda h: h.memset(big2[:, 8192:16640], 1.0), w=["Vaug"])
    S = sb([128, 4, 128], name="S"); S_b = sb([128, 4, 128], BF16, name="S_b")
    P.op("dve", lambda h: h.memset(S[:], 0.0), w=["S"])
    P.op("dve", lambda h: h.memset(S_b[:], 0.0), w=["S_b"])
    q_hg = sb([128, 512], name="q_hg"); logf = sb([128, 512], name="logf"); kk = sb([128, 512], name="kk"); tmpf = sb([128, 512], name="tmpf")
    v_hg = sb([128, 512], name="v_hg"); v_hgb = sb([128, 512], BF16, name="v_hgb"); gs = sb([128, 512], name="gs")
    qr = sb([128, 512], name="qr"); kr = sb([128, 512], name="kr"); vr = sb([128, 512], name="vr")
    qrb = sb([128, 512], BF16, name="qrb"); krb = sb([128, 512], BF16, name="krb")
    r1 = sb([128, 256], name="r1"); r2 = sb([128, 256], name="r2")
    ebuf = sb([128, 512], name="ebuf"); qe = sb([128, 512], BF16, name="qe"); kinv = sb([128, 512], BF16, name="kinv"); kdec = sb([128, 512], BF16, name="kdec")
    gam = sb([128, 8], name="gam")
    qeT = sb([128, 4, 128], BF16, name="qeT"); qeF = sb([128, 4, 128], BF16, name="qeF"); kinvT = sb([128, 4, 128], BF16, name="kinvT"); kdT0 = sb([128, 4, 64], BF16, name="kdT0")
    AT = sb([128, 128], BF16, name="AT")
    qT1 = sb([128, 4, 128], BF16, name="qT1"); qT2 = sb([128, 4, 128], BF16, name="qT2"); qTm = [qT1, qT2]
    P.op("pool", lambda h: h.memset(qT1[:], 0.0), w=["qT"])
    P.op("pool", lambda h: h.memset(qT2[:], 0.0), w=["qT"])
    Eb = [sb([128, 256], BF16, name=f"Eb{i}") for i in range(2)]
    rl = sb([128, 4], name="rl"); od = sb([128, 512], name="od"); tda = sb([128, 128], name="tda")
    mix = lbt; mixb = sb([128, D], BF16, name="mixb"); mixT = sb([128, 8, 128], BF16, name="mixT")
    daon08 = sb([128, 128], name="daon08")
    P.op("dve", lambda h: h.tensor_scalar(out=daon08[:], in0=daon_b[:], scalar1=1.0 - LAM_INIT, scalar2=None, op0=ALU.mult), r=["daon_b"], w=["daon08"])

    def rope(p_, pk, T, dst, dkey, cosap, sinap):
        pv = p_[0:T, :].rearrange("p (g two f) -> p g two f", two=2, f=32)
        dv_ = dst[0:T, :].rearrange("p (g two f) -> p g two f", two=2, f=32)
        cb_ = cosap.unsqueeze(1).to_broadcast([T, 8, 32]); sb_ = sinap.unsqueeze(1).to_broadcast([T, 8, 32])
        r1v = r1[0:T, :].rearrange("p (g f) -> p g f", f=32); r2v = r2[0:T, :].rearrange("p (g f) -> p g f", f=32)
        P.op("dve", lambda h: h.tensor_tensor(out=r1v, in0=pv[:, :, 0, :], in1=cb_, op=ALU.mult), r=[pk, "cos_t", "coss_t"], w=["r1"])
        P.op("dve", lambda h: h.tensor_tensor(out=r2v, in0=pv[:, :, 1, :], in1=sb_, op=ALU.mult), r=[pk, "sin_t", "sins_t"], w=["r2"])
        P.op("dve", lambda h: h.tensor_tensor(out=dv_[:, :, 0, :], in0=r1v, in1=r2v, op=ALU.subtract), r=["r1", "r2"], w=[dkey])
        P.op("dve", lambda h: h.tensor_tensor(out=r1v, in0=pv[:, :, 0, :], in1=sb_, op=ALU.mult), r=[pk, dkey], w=["r1"])
        P.op("dve", lambda h: h.tensor_tensor(out=r2v, in0=pv[:, :, 1, :], in1=cb_, op=ALU.mult), r=[pk, dkey], w=["r2"])
        P.op("dve", lambda h: h.tensor_tensor(out=dv_[:, :, 1, :], in0=r1v, in1=r2v, op=ALU.add), r=["r1", "r2", dkey], w=[dkey])

    def mixer_inputs(T, cosap, sinap):
        for cb in range(7):
            p_, pk = linear(hT, "hT", T, wbig, "wbig", cb * 512, 512)
            pp = p_[0:T, :]
            if cb == 0:
                P.op("act", lambda h, pp=pp: h.activation(out=q_hg[0:T, :], in_=pp, func=AF.Copy), r=[pk], w=["q_hg"])
            elif cb == 1:
                P.op("act", lambda h, pp=pp: h.activation(out=tmpf[0:T, :], in_=pp, func=AF.Sigmoid), r=[pk], w=["tmpf"])
                P.op("dve", lambda h: h.tensor_tensor(out=tmpf[0:T, :], in0=tmpf[0:T, :], in1=oml_b[0:T, :], op=ALU.mult), r=["tmpf", "oml_b"], w=["tmpf"])
                P.op("dve", lambda h: h.tensor_tensor(out=tmpf[0:T, :], in0=tmpf[0:T, :], in1=lb_b[0:T, :], op=ALU.add), r=["tmpf", "lb_b"], w=["tmpf"])
                P.op("act", lambda h: h.activation(out=logf[0:T, :], in_=tmpf[0:T, :], func=AF.Ln), r=["tmpf"], w=["logf"])
                P.op("dve", lambda h: h.tensor_scalar(out=kk[0:T, :], in0=tmpf[0:T, :], scalar1=-1.0, scalar2=1.0, op0=ALU.mult, op1=ALU.add), r=["tmpf"], w=["kk"])
            elif cb == 2:
                P.op("act", lambda h, pp=pp: h.activation(out=v_hg[0:T, :], in_=pp, func=AF.Copy), r=[pk], w=["v_hg"])
                P.op("dve", lambda h: h.tensor_copy(out=v_hgb[0:T, :], in_=v_hg[0:T, :]), r=["v_hg"], w=["v_hgb"])
            elif cb == 3:
                P.op("act", lambda h, pp=pp: h.activation(out=gs[0:T, :], in_=pp, func=AF.Silu), r=[pk], w=["gs"])
                P.op("dve", lambda h: h.tensor_tensor(out=gs[0:T, :].rearrange("p (g f) -> p g f", f=128), in0=gs[0:T, :].rearrange("p (g f) -> p g f", f=128),
                                                      in1=hgon_b[0:T, :].unsqueeze(1).to_broadcast([T, 4, 128]), op=ALU.mult), r=["gs", "hgon_b"], w=["gs"])
            elif cb == 4:
                rope(p_, pk, T, qr, "qr", cosap, sinap)
                P.op("dve", lambda h: h.tensor_scalar(out=qrb[0:T, :], in0=qr[0:T, :], scalar1=0.125, scalar2=None, op0=ALU.mult), r=["qr"], w=["qrb"])
            elif cb == 5:
                rope(p_, pk, T, kr, "kr", cosap, sinap)
                P.op("pool", lambda h: h.tensor_copy(out=krb[0:T, :], in_=kr[0:T, :]), r=["kr"], w=["krb"])
            else:
                P.op("act", lambda h, pp=pp: h.activation(out=vr[0:T, :], in_=pp, func=AF.Copy), r=[pk], w=["vr"])

    def head_norm(src_ap_fn, skeys, T, dst_col0, scale_ap_fn, scale_key, col):
        for hh in range(4):
            P.op("act", lambda h, hh=hh: h.activation(out=junk[0:T, 0:128], in_=src_ap_fn(hh), func=AF.Square, accum_out=ssq[0:T, col + hh:col + hh + 1]), r=skeys, w=["junk", "ssq"])
        P.op("act", lambda h: h.activation(out=ssq[0:T, col:col + 4], in_=ssq[0:T, col:col + 4], func=AF.Sqrt, scale=1.0 / 128, bias=EPS), r=["ssq"], w=["ssq"])
        P.op("dve", lambda h: h.reciprocal(out=ssq[0:T, col:col + 4], in_=ssq[0:T, col:col + 4]), r=["ssq"], w=["ssq"])
        for hh in range(4):
            P.op("dve", lambda h, hh=hh: h.scalar_tensor_tensor(out=mix[0:T, dst_col0 + hh * 128:dst_col0 + (hh + 1) * 128], in0=src_ap_fn(hh), scalar=ssq[0:T, col + hh:col + hh + 1],
                                                                in1=scale_ap_fn(hh), op0=ALU.mult, op1=ALU.mult), r=skeys + ["ssq", scale_key], w=["mix"])

    def out_proj(T, x_ap, xkey, row0):
        P.op("act", lambda h: h.activation(out=mixb[0:T, :], in_=mix[0:T, :], func=AF.Copy), r=["mix"], w=["mixb"])
        transpose_to(mixT, "mixT", mixb, "mixb", T)
        for cb in range(2):
            p_, pk = linear(mixT, "mixT", T, wout_b, "wout_b", cb * 512, 512)
            P.op("dve", lambda h, p_=p_, cb=cb: h.tensor_tensor(out=xo[0:T, cb * 512:(cb + 1) * 512], in0=p_[0:T, :], in1=x_ap[:, cb * 512:(cb + 1) * 512], op=ALU.add), r=[pk, xkey], w=["xo"])
        P.dma("sp", lambda h: h.dma_start(out=xres[row0:row0 + T, :], in_=xo[0:T, :]), r=["xo"], w=[f"xres{row0 // 128}"])

    ssq8 = 0
    for i in range(NT if stage >= 1 else 0):
        x_, xk = load_rows(xp[i * 128:(i + 1) * 128, :], 128)
        norm_T(x_[:], xk, 128)
        mixer_inputs(128, cos_t[:, i, :], sin_t[:, i, :])
        P.dma("sp", lambda h, i=i: h.dma_start(out=kp[i * 128:(i + 1) * 128, :], in_=kr[:]), r=["kr"], w=["out_kp"])
        P.dma("sp", lambda h, i=i: h.dma_start(out=vp[i * 128:(i + 1) * 128, :], in_=vr[:]), r=["vr"], w=["out_vp"])
        P.op("pool", lambda h, i=i: h.tensor_copy(out=Vaug[:, i, :, 0:128], in_=vr[:].rearrange("p (g f) -> p g f", f=128)), r=["vr"], w=["Vaug"])
        if stage < 2:
            continue
        pb, pbk = nxt("A")
        P.op("pe", lambda h, pb=pb: h.matmul(pb[:], lhsT=u1[:], rhs=logf[:], start=True, stop=True), r=["u1", "logf"], w=[pbk])
        P.op("act", lambda h, pb=pb: h.activation(out=ebuf[:], in_=pb[:], func=AF.Exp), r=[pbk], w=["ebuf"])
        P.op("dve", lambda h: h.tensor_tensor(out=qe[:], in0=q_hg[:], in1=ebuf[:], op=ALU.mult), r=["q_hg", "ebuf"], w=["qe"])
        P.op("act", lambda h, pb=pb: h.activation(out=ebuf[:], in_=pb[:], func=AF.Exp, scale=-1.0), r=[pbk, "qe"], w=["ebuf"])
        P.op("dve", lambda h: h.tensor_tensor(out=kinv[:], in0=kk[:], in1=ebuf[:], op=ALU.mult), r=["kk", "ebuf"], w=["kinv"])
        pd, pdk = nxt("A")
        P.op("pe", lambda h, pd=pd: h.matmul(pd[:], lhsT=u2[:], rhs=logf[:], start=True, stop=True), r=["u2", "logf"], w=[pdk])
        P.op("act", lambda h, pd=pd: h.activation(out=ebuf[:], in_=pd[:], func=AF.Exp), r=[pdk, "kinv"], w=["ebuf"])
        P.op("dve", lambda h: h.tensor_tensor(out=kdec[:], in0=kk[:], in1=ebuf[:], op=ALU.mult), r=["kk", "ebuf"], w=["kdec"])
        for hh in range(4):
            P.op("pe", lambda h, hh=hh: h.matmul(psF[:, hh * 2:hh * 2 + 2], lhsT=logf[:, hh * 128:(hh + 1) * 128], rhs=ind[:], start=True, stop=True), r=["logf", "ind"], w=["psF"])
        P.op("act", lambda h: h.activation(out=gam[:], in_=psF[:, 0:8], func=AF.Exp), r=["psF"], w=["gam"])
        transpose_to(qeT, "qeT", qe, "qe", 128, nchunks=4)
        transpose_to(kinvT, "kinvT", kinv, "kinv", 128, nchunks=4)
        for hh in range(4):
            P.op("dve", lambda h, hh=hh: h.tensor_copy(out=qeF[:, hh, 0:64], in_=qeT[:, hh, 0:64]), r=["qeT"], w=["qeF"])
            P.op("dve", lambda h, hh=hh: h.tensor_scalar(out=qeF[:, hh, 64:128], in0=qeT[:, hh, 64:128], scalar1=gam[:, 2 * hh:2 * hh + 1], scalar2=None, op0=ALU.mult), r=["qeT", "gam"], w=["qeF"])
            P.op("dve", lambda h, hh=hh: h.tensor_scalar(out=kdT0[:, hh, :], in0=kinvT[:, hh, 0:64], scalar1=gam[:, 2 * hh:2 * hh + 1], scalar2=None, op0=ALU.mult), r=["kinvT", "gam"], w=["kdT0"])
        po, pok = nxt("O")
        for hh in range(4):
            pa, pak = nxt("S")
            P.op("pe", lambda h, hh=hh, pa=pa: h.matmul(pa[:, 0:128], lhsT=kinvT[:, hh, :], rhs=qeT[:, hh, :], start=True, stop=True), r=["kinvT", "qeT"], w=[pak])
            P.op("pe", lambda h, hh=hh, pa=pa: h.matmul(pa[0:64, 128:192], lhsT=kdT0[:, hh, :], rhs=qeT[:, hh, 64:128], start=True, stop=True), r=["kdT0", "qeT"], w=[pak])
            P.op("dve", lambda h, pa=pa: h.tensor_tensor(out=AT[:], in0=pa[:, 0:128], in1=am[:], op=ALU.mult), r=[pak, "am"], w=["AT"])
            P.op("dve", lambda h, pa=pa: h.tensor_copy(out=AT[0:64, 64:128], in_=pa[0:64, 128:192]), r=[pak, "AT"], w=["AT"])
            P.op("pe", lambda h, hh=hh, po=po: h.matmul(po[:, hh * 128:(hh + 1) * 128], lhsT=AT[:], rhs=v_hgb[:, hh * 128:(hh + 1) * 128], start=True, stop=False), r=["AT", "v_hgb"], w=[pok])
            P.op("pe", lambda h, hh=hh, po=po: h.matmul(po[:, hh * 128:(hh + 1) * 128], lhsT=qeF[:, hh, :], rhs=S_b[:, hh, :], start=False, stop=True), r=["qeF", "S_b"], w=[pok])
            pu, puk = nxt("A")
            P.op("pe", lambda h, hh=hh, pu=pu: h.matmul(pu[:, 0:128], lhsT=kdec[:, hh * 128:(hh + 1) * 128], rhs=v_hgb[:, hh * 128:(hh + 1) * 128], start=True, stop=True), r=["kdec", "v_hgb"], w=[puk])
            P.op("dve", lambda h, hh=hh, pu=pu: h.scalar_tensor_tensor(out=S[:, hh, :], in0=S[:, hh, :], scalar=gam[:, 2 * hh + 1:2 * hh + 2], in1=pu[:, 0:128], op0=ALU.mult, op1=ALU.add),
                 r=["S", "gam", puk], w=["S"])
        P.op("act", lambda h: h.activation(out=S_b[:], in_=S[:], func=AF.Copy), r=["S"], w=["S_b"])
        head_norm(lambda hh, po=po: po[:, hh * 128:(hh + 1) * 128], [pok], 128, 0, lambda hh: gs[:, hh * 128:(hh + 1) * 128], "gs", 0)
        if i == NT - 1:
            for hh in range(4):
                P.dma("sp", lambda h, hh=hh: h.dma_start(out=hsp[hh * 128:(hh + 1) * 128, :], in_=S[:, hh, :]), r=["S"], w=["out_hsp"])
        if stage < 3:
            continue
        for c_ in range(4):
            P.op("pe", lambda h, c_=c_: h.transpose(out=psT[:, c_ * 128:(c_ + 1) * 128], in_=qrb[:, c_ * 128:(c_ + 1) * 128], identity=ident_b[:]), r=["qrb", "ident_b"], w=["psT"])
        P.op("act", lambda h: h.activation(out=qT1[0:64, :, :], in_=psT[0:64, 0:512].rearrange("p (c t) -> p c t", t=128), func=AF.Copy), r=["psT"], w=["qT"])
        P.op("act", lambda h: h.activation(out=qT2[64:128, :, :], in_=psT[64:128, 0:512].rearrange("p (c t) -> p c t", t=128), func=AF.Copy), r=["psT", "qT"], w=["qT"])
        for c_ in range(4):
            P.op("pe", lambda h, c_=c_: h.transpose(out=psT[:, c_ * 128:(c_ + 1) * 128], in_=krb[:, c_ * 128:(c_ + 1) * 128], identity=ident_b[:]), r=["krb", "ident_b"], w=["psT"])
        P.op("act", lambda h, i=i: h.activation(out=KT[:, :, i * 128:(i + 1) * 128], in_=psT[:, 0:512].rearrange("p (c t) -> p c t", t=128), func=AF.Copy), r=["psT"], w=["KT"])
        for hh in range(4):
            for j in range(i + 1):
                pS, pSk = nxt("S")
                for m_ in range(2):
                    P.op("pe", lambda h, hh=hh, j=j, m_=m_, pS=pS: h.matmul(pS[:, m_ * 128:(m_ + 1) * 128], lhsT=KT[:, hh, j * 128:(j + 1) * 128],
                                                                          rhs=qTm[m_][:, hh, :], start=True, stop=True), r=["KT", "qT"], w=[pSk])
                E_ = Eb[j & 1]; ek = f"Eb{j & 1}"
                P.op("act", lambda h, pS=pS, E_=E_: h.activation(out=E_[:], in_=pS[:, 0:256], func=AF.Exp), r=[pSk], w=[ek])
                if j == i:
                    P.op("dve", lambda h, E_=E_: h.tensor_tensor(out=E_[:].rearrange("p (m q) -> p m q", m=2), in0=E_[:].rearrange("p (m q) -> p m q", m=2),
                                                               in1=caus_b[:].unsqueeze(1).to_broadcast([128, 2, 128]), op=ALU.mult), r=[ek, "caus_b"], w=[ek])
                for m_ in range(2):
                    P.op("pe", lambda h, hh=hh, j=j, m_=m_, E_=E_: h.matmul(psO[m_][:, 0:130], lhsT=E_[:, m_ * 128:(m_ + 1) * 128], rhs=Vaug[:, j, hh, 0:130],
                                                                          start=(j == 0), stop=(j == i)), r=[ek, "Vaug"], w=[f"psO{m_}"])
            P.op("dve", lambda h: h.reciprocal(out=rl[:, 0:1], in_=psO[0][:, 128:129]), r=["psO0"], w=["rl"])
            P.op("dve", lambda h: h.reciprocal(out=rl[:, 1:2], in_=psO[1][:, 128:129]), r=["psO1", "rl"], w=["rl"])
            P.op("dve", lambda h: h.tensor_tensor(out=rl[:, 2:3], in0=rl[:, 1:2], in1=neglam[:], op=ALU.mult), r=["rl", "neglam"], w=["rl"])
            P.op("dve", lambda h: h.tensor_scalar(out=tda[:], in0=psO[0][:, 0:128], scalar1=rl[:, 0:1], scalar2=None, op0=ALU.mult), r=["psO0", "rl"], w=["tda"])
            P.op("dve", lambda h, hh=hh: h.scalar_tensor_tensor(out=od[:, hh * 128:(hh + 1) * 128], in0=psO[1][:, 0:128], scalar=rl[:, 2:3], in1=tda[:], op0=ALU.mult, op1=ALU.add),
                 r=["psO1", "rl", "tda"], w=["od"])
        head_norm(lambda hh: od[:, hh * 128:(hh + 1) * 128], ["od"], 128, 512, lambda hh: daon08[:], "daon08", 0)
        if stage < 4:
            continue
        out_proj(128, x_[:], xk, i * 128)


    if stage >= 4:
        x_, xk = load_rows(xs[0:NS, :], NS)
        norm_T(x_[0:NS, :], xk, NS)
        mixer_inputs(NS, coss_t[0:NS, :], sins_t[0:NS, :])
        P.dma("sp", lambda h: h.dma_start(out=ks[0:NS, :], in_=kr[0:NS, :]), r=["kr"], w=["out_ks"])
        P.dma("sp", lambda h: h.dma_start(out=vs[0:NS, :], in_=vr[0:NS, :]), r=["vr"], w=["out_vs"])
        smalls = sb([128, 64], name="smalls"); i4b = sb([128, 16], name="i4b"); qmask = sb([128, 64], name="qmask"); colsT = sb([128, 48], name="colsT")
        iota_c = sb([128, 1], name="iota_c"); cpos = sb([8, 4], name="cpos"); cneg = sb([8, 4], name="cneg"); Cm = sb([8, 4], name="Cm")
        ld(i4b[:], c_i4.partition_broadcast(128), ["i4b"]); ld(iota_c[:], c_iota, ["iota_c"]); ld(cpos[:], c_cpos, ["cpos"]); ld(cneg[:], c_cneg, ["cneg"])
        P.op("dve", lambda h: h.scalar_tensor_tensor(out=Cm[:], in0=cneg[:], scalar=neglam[0:8, 0:1], in1=cpos[:], op0=ALU.mult, op1=ALU.add), r=["cneg", "cpos", "neglam"], w=["Cm"])
        Ss = big2[:, 0:4096].bitcast(F32).rearrange("p (g v) -> p g v", v=128)
        s_all = big2[:, 4096:6144].bitcast(F32)
        Kt = [big2[:, 6144 + r_ * 2048:6144 + (r_ + 1) * 2048].bitcast(F32) for r_ in range(2)]
        Vt = [big2[:, 10240 + r_ * 2048:10240 + (r_ + 1) * 2048].bitcast(F32) for r_ in range(2)]
        prodt = [big2[:, 0:2048].bitcast(F32)]
        selT = kvb[:].bitcast(F32)[0:4, :].rearrange("p (b m) -> p b m", m=128)
        P.op("dve", lambda h: h.tensor_copy(out=selT, in_=ident_f[0:4, 0:4].unsqueeze(2).to_broadcast([4, 4, 128])), r=["ident_f", "KT", "Vaug"], w=["kvb"])
        P.dma("sp", lambda h: h.dma_start(out=Ss, in_=sh.rearrange("(g p) v -> p g v", p=128)), r=["KT", "Vaug"], w=["Ss"])
        if SL >= 1:
            P.op("act", lambda h: h.activation(out=tmpf[0:4, :], in_=logf[0:4, :], func=AF.Exp), r=["logf"], w=["tmpf"])
            for wi, (src, skey) in enumerate(((tmpf, "tmpf"), (kk, "kk"), (q_hg, "q_hg"))):
                for hh in range(4):
                    P.op("pe", lambda h, wi=wi, hh=hh, src=src: h.transpose(out=psF[:, (wi * 4 + hh) * 4:(wi * 4 + hh) * 4 + 4], in_=src[0:4, hh * 128:(hh + 1) * 128], identity=ident_f[0:4, 0:4]),
                         r=[skey, "ident_f"], w=["psF"])
            P.op("act", lambda h: h.activation(out=colsT[:], in_=psF[:, 0:48], func=AF.Copy), r=["psF"], w=["colsT"])
            for b in range(NS):
                pvb, pvbk = nxt("A")
                P.op("pe", lambda h, b=b, pvb=pvb: h.matmul(pvb[:, 0:512], lhsT=selT[:, b, :], rhs=v_hg[0:4, :], start=True, stop=True), r=["kvb", "v_hg"], w=[pvbk])
                for hh in range(4):
                    g_ = b * 4 + hh
                    P.op("dve", lambda h, hh=hh, b=b, pvb=pvb: h.tensor_scalar(out=tda[:], in0=pvb[:, hh * 128:(hh + 1) * 128], scalar1=colsT[:, 16 + hh * 4 + b:16 + hh * 4 + b + 1], scalar2=None, op0=ALU.mult),
                         r=[pvbk, "colsT"], w=["tda"])
                    P.op("dve", lambda h, hh=hh, b=b, g_=g_: h.scalar_tensor_tensor(out=Ss[:, g_, :], in0=Ss[:, g_, :], scalar=colsT[:, hh * 4 + b:hh * 4 + b + 1], in1=tda[:], op0=ALU.mult, op1=ALU.add),
                         r=["Ss", "colsT", "tda"], w=["Ss"])
            P.dma("sp", lambda h: h.dma_start(out=hss.rearrange("(g p) v -> p g v", p=128), in_=Ss), r=["Ss"], w=["out_hss"])
            qm4 = qmask[:].rearrange("p (a b c) -> p a b c", b=4, c=4)
            for hh in range(4):
                P.op("dve", lambda h, hh=hh: h.tensor_tensor(out=qm4[:, hh, :, :], in0=colsT[:, 32 + hh * 4:32 + hh * 4 + 4].unsqueeze(1).to_broadcast([128, 4, 4]),
                                                             in1=i4b[:].rearrange("p (a b) -> p a b", b=4), op=ALU.mult), r=["colsT", "i4b"], w=["qmask"])
            po, pok = nxt("O")
            for hh in range(4):
                for b in range(NS):
                    P.op("pe", lambda h, hh=hh, b=b, po=po: h.matmul(po[0:4, hh * 128:(hh + 1) * 128], lhsT=qm4[:, hh, b, :], rhs=Ss[:, b * 4 + hh, :], start=(b == 0), stop=(b == NS - 1)),
                         r=["qmask", "Ss"], w=[pok])
            head_norm(lambda hh, po=po: po[0:4, hh * 128:(hh + 1) * 128], [pok], NS, 0, lambda hh: gs[0:4, hh * 128:(hh + 1) * 128], "gs", 0)

        if SL >= 2:
            pti_t = sb([128, 512], I32, name="pti_t"); pti = pti_t[:]
            ptv = pt.rearrange("(o b) (g two) -> o two b g", o=1, two=2)
            for h2 in range(2):
                for b_ in range(NS):
                    P.dma("sp", lambda h, h2=h2, b_=b_: h.dma_start(out=pti[h2 * 64:(h2 + 1) * 64, b_ * 64:(b_ + 1) * 64], in_=ptv[:, h2, b_, :].partition_broadcast(64), allow_slow_non_contiguous=True), w=["ebuf"])
            P.op("dve", lambda h: h.tensor_copy(out=tmpf[:, 0:256], in_=pti[:, 0:256]), r=["ebuf"], w=["tmpf"])
            P.op("dve", lambda h: h.tensor_scalar(out=tmpf[:, 0:256], in0=tmpf[:, 0:256], scalar1=64.0, scalar2=iota_c[:, 0:1], op0=ALU.mult, op1=ALU.add), r=["tmpf", "iota_c"], w=["tmpf"])
            P.op("dve", lambda h: h.tensor_copy(out=pti[:, 0:256], in_=tmpf[:, 0:256]), r=["tmpf"], w=["ebuf"])
            qs = logf
            P.op("dve", lambda h: h.tensor_scalar(out=qs[0:4, :], in0=qr[0:4, :], scalar1=0.125, scalar2=None, op0=ALU.mult), r=["qr", "logf"], w=["logf"])
            P.op("dve", lambda h: h.tensor_tensor(out=kk[0:4, :], in0=qs[0:4, :], in1=kr[0:4, :], op=ALU.mult), r=["logf", "kr", "kk"], w=["kk"])
            P.op("dve", lambda h: h.tensor_reduce(out=smalls[0:4, 0:8], in_=kk[0:4, :].rearrange("p (m d) -> p m d", d=64), axis=AX.X, op=ALU.add), r=["kk"], w=["smalls"])
            P.op("act", lambda h: h.activation(out=smalls[0:4, 8:16], in_=smalls[0:4, 0:8], func=AF.Exp), r=["smalls"], w=["smalls"])
            bm_t = v_hg
            P.dma("sp", lambda h: h.dma_start(out=bm_t[0:4, :], in_=c_bm), r=["v_hg"], w=["v_hg"])
        if SL == 2:
            P.dma("sp", lambda h: h.dma_start(out=kp[0:128, :], in_=tmpf[:]), r=["tmpf"], w=["out_kp"])
            P.dma("sp", lambda h: h.dma_start(out=vp[0:128, :], in_=ebuf[:]), r=["ebuf"], w=["out_vp"])
        if SL >= 3:
            s_alls = [s_all, big2[:, 2048:4096].bitcast(F32)]
            sm = [smalls, sb([128, 64], name="smalls2")]
            pov = povk = None
            for b in range(NS + 1):
                kb, vb = b, b - 1
                if kb < NS:
                    sa = s_alls[kb % 2]; sak = f"s_all{kb % 2}"; smk = sm[kb % 2]; smkk = f"sm{kb % 2}"
                    pq, pqk = nxt("A")
                    P.op("pe", lambda h, kb=kb, pq=pq: h.matmul(pq[:, 0:512], lhsT=selT[:, kb, :], rhs=qs[0:4, :], start=True, stop=True), r=["kvb", "logf"], w=[pqk])
                    P.op("act", lambda h, pq=pq: h.activation(out=q_hg[:], in_=pq[:, 0:512], func=AF.Copy), r=[pqk], w=["q_hg"])
                if vb >= 0:
                    sv = s_alls[vb % 2]; svk = f"s_all{vb % 2}"; smv = sm[vb % 2]; smvk = f"sm{vb % 2}"
                    pov, povk = nxt("O")
                for g in range(NPAGES // 2):
                    r_ = g % 2
                    if kb < NS:
                        P.dma("pool", lambda h, kb=kb, g=g, r_=r_: h.indirect_dma_start(out=Kt[r_], out_offset=None, in_=ck, in_offset=bass.IndirectOffsetOnAxis(ap=pti[:, kb * 64 + g:kb * 64 + g + 1], axis=0)),
                              r=["ebuf"], w=[f"Kt{r_}"])
                        P.op("dve", lambda h, r_=r_: h.tensor_tensor(out=prodt[0].rearrange("p (t f) -> p t f", t=2), in0=Kt[r_].rearrange("p (t f) -> p t f", t=2),
                                                                 in1=q_hg[:].unsqueeze(1).to_broadcast([128, 2, 512]), op=ALU.mult), r=[f"Kt{r_}", "q_hg"], w=["prodt0", "Ss"])
                        P.op("dve", lambda h, g=g, sa=sa: h.tensor_reduce(out=sa[:, g * 16:(g + 1) * 16], in_=prodt[0].rearrange("p (m d) -> p m d", d=64), axis=AX.X, op=ALU.add), r=["prodt0"], w=[sak])
                    if vb >= 0:
                        P.dma("pool", lambda h, vb=vb, g=g, r_=r_: h.indirect_dma_start(out=Vt[r_], out_offset=None, in_=cv, in_offset=bass.IndirectOffsetOnAxis(ap=pti[:, vb * 64 + g:vb * 64 + g + 1], axis=0)),
                              r=["ebuf"], w=[f"Vt{r_}"])
                        for t2 in range(2):
                            P.op("pe", lambda h, g=g, t2=t2, r_=r_, pov=pov, sv=sv: h.matmul(pov[0:8, 0:512], lhsT=sv[:, g * 16 + t2 * 8:g * 16 + t2 * 8 + 8], rhs=Vt[r_][:, t2 * 512:(t2 + 1) * 512],
                                                                                        start=(g == 0 and t2 == 0), stop=False), r=[svk, f"Vt{r_}"], w=[povk])
                if vb >= 0:
                    P.op("pe", lambda h, pov=pov, smv=smv: h.matmul(pov[0:8, 0:512], lhsT=smv[0:4, 16:24], rhs=vr[0:4, :], start=False, stop=True), r=[smvk, "vr"], w=[povk])
                    pol, polk = nxt("S")
                    P.op("pe", lambda h, pol=pol, smv=smv: h.matmul(pol[0:8, 0:1], lhsT=smv[:, 24:32], rhs=ones_f[:, 0:1], start=True, stop=False), r=[smvk, "ones_f"], w=[polk])
                    P.op("pe", lambda h, pol=pol, smv=smv: h.matmul(pol[0:8, 0:1], lhsT=smv[0:4, 16:24], rhs=ones_f[0:4, 0:1], start=False, stop=True), r=[smvk, "ones_f"], w=[polk])
                    P.op("dve", lambda h, pol=pol, smv=smv: h.reciprocal(out=smv[0:8, 32:33], in_=pol[0:8, 0:1]), r=[polk], w=[smvk])
                    P.op("dve", lambda h, pov=pov, smv=smv: h.tensor_scalar(out=kk[0:8, :], in0=pov[0:8, 0:512], scalar1=smv[0:8, 32:33], scalar2=None, op0=ALU.mult), r=[povk, smvk], w=["kk"])
                    pfin, pfink = nxt("A")
                    P.op("pe", lambda h, pfin=pfin: h.matmul(pfin[0:4, 0:512], lhsT=Cm[0:8, 0:4], rhs=kk[0:8, :], start=True, stop=True), r=["Cm", "kk"], w=[pfink])
                    P.op("dve", lambda h, pfin=pfin: h.tensor_tensor(out=tmpf[0:4, :], in0=pfin[0:4, 0:512], in1=bm_t[0:4, :], op=ALU.mult), r=[pfink, "v_hg"], w=["tmpf"])
                    P.op("pe", lambda h, vb=vb: h.matmul(psF[0:4, 0:512], lhsT=i4b[0:4, vb * 4:(vb + 1) * 4], rhs=tmpf[0:4, :], start=(vb == 0), stop=(vb == NS - 1)), r=["i4b", "tmpf"], w=["psF"])
                if kb < NS:
                    P.op("act", lambda h, sa=sa: h.activation(out=sa, in_=sa, func=AF.Exp), r=[sak], w=[sak])
                    P.op("dve", lambda h, sa=sa, smk=smk: h.tensor_reduce(out=smk[:, 24:32], in_=sa.rearrange("p (j m) -> p m j", m=8), axis=AX.X, op=ALU.add), r=[sak], w=[smkk])
                    P.op("dve", lambda h, kb=kb, smk=smk: h.tensor_scalar(out=smk[0:4, 16:24], in0=smalls[0:4, 8:16], scalar1=ident_f[0:4, kb:kb + 1], scalar2=None, op0=ALU.mult), r=["smalls", "ident_f", smkk], w=[smkk])
        if SL >= 4:
            head_norm(lambda hh: psF[0:4, hh * 128:(hh + 1) * 128], ["psF"], NS, 512, lambda hh: daon08[0:4, :], "daon08", 0)
        x_, xk = load_rows(xs[0:NS, :], NS)
        out_proj(NS, x_[0:NS, :], xk, SEQ)

    if stage >= 5:
        load_weight(w_mq, wbig, D, gk="mq", key="wbig")
        load_weight(w_mo, wbig[:, :, 1024:2048], D, key="wbig")
        qTc = big2[:, 14336:15360].rearrange("p (c t) -> p c t", t=128); oTc = big2[:, 15360:16384].rearrange("p (c t) -> p c t", t=128)
        ETc = big2[:, 16384:16640].rearrange("p (c t) -> p c t", t=128); rlc = tda
        def cross_core(c0, n):
            for hh in range(4):
                for nt_ in range(2):
                    pS, pSk = nxt("S")
                    for c_ in range(2):
                        P.op("pe", lambda h, hh=hh, nt_=nt_, c_=c_, pS=pS: h.matmul(pS[:, 0:n], lhsT=memKT[:, hh * 2 + c_, nt_ * 128:(nt_ + 1) * 128], rhs=qTc[:, hh * 2 + c_, c0:c0 + n],
                                                                                  start=(c_ == 0), stop=(c_ == 1)), r=["memKT", "qTc"], w=[pSk])
                    P.op("act", lambda h, nt_=nt_, pS=pS: h.activation(out=ETc[:, nt_, 0:n], in_=pS[:, 0:n], func=AF.Exp), r=[pSk], w=["ETc"])
                for nt_ in range(2):
                    P.op("pe", lambda h, nt_=nt_: h.matmul(psF[:, 0:n], lhsT=ones_b[:], rhs=ETc[:, nt_, 0:n], start=(nt_ == 0), stop=(nt_ == 1)), r=["ones_b", "ETc"], w=["psF"])
                P.op("dve", lambda h: h.reciprocal(out=rlc[:, 0:n], in_=psF[:, 0:n]), r=["psF"], w=["tda"])
                for c_ in range(2):
                    po_, pok_ = nxt("O")
                    for nt_ in range(2):
                        P.op("pe", lambda h, hh=hh, nt_=nt_, c_=c_, po_=po_: h.matmul(po_[:, 0:n], lhsT=memV[:, nt_, hh * 256 + c_ * 128:hh * 256 + (c_ + 1) * 128], rhs=ETc[:, nt_, 0:n],
                                                                                    start=(nt_ == 0), stop=(nt_ == 1)), r=["memV", "ETc"], w=[pok_])
                    P.op("dve", lambda h, hh=hh, c_=c_, po_=po_: h.tensor_tensor(out=oTc[:, hh * 2 + c_, c0:c0 + n], in0=po_[:, 0:n], in1=rlc[:, 0:n], op=ALU.mult), r=[pok_, "tda"], w=["oTc"])

        def cross_q(T):
            for oc in range(8):
                pq, pqk = nxt("A")
                for k in range(8):
                    P.op("pe", lambda h, k=k, oc=oc, pq=pq: h.matmul(pq[:, 0:T], lhsT=wbig[:, k, oc * 128:(oc + 1) * 128], rhs=hT[:, k, 0:T], start=(k == 0), stop=(k == 7)), r=["wbig", "hT"], w=[pqk])
                P.op("dve", lambda h, oc=oc, pq=pq: h.tensor_scalar(out=qTc[:, oc, 0:T], in0=pq[:, 0:T], scalar1=1.0 / 16, scalar2=None, op0=ALU.mult), r=[pqk], w=["qTc"])

        def cross_out(T, x_, xk, row0):
            for cb in range(2):
                p_, pk = linear(oTc, "oTc", T, wbig, "wbig", 1024 + cb * 512, 512)
                P.op("dve", lambda h, p_=p_, cb=cb: h.tensor_tensor(out=xo[0:T, cb * 512:(cb + 1) * 512], in0=p_[0:T, :], in1=x_[0:T, cb * 512:(cb + 1) * 512], op=ALU.add), r=[pk, xk], w=["xo"])
            P.dma("sp", lambda h: h.dma_start(out=xres[row0:row0 + T, :], in_=xo[0:T, :]), r=["xo"], w=[f"xres{row0 // 128}"])

        for i in range(NT):
            x_, xk = load_rows(xres[i * 128:(i + 1) * 128, :], 128, rk=[f"xres{i}"])
            norm_T(x_[0:128, :], xk, 128)
            cross_q(128)
            cross_core(0, 128)
            cross_out(128, x_, xk, i * 128)

        if CL >= 1:
            x_s, xk_s = load_rows(xres[SEQ:SEQ + NS, :], NS, rk=[f"xres{SEQ // 128}"])
            norm_T(x_s[0:NS, :], xk_s, NS)
            cross_q(NS)
            for b in range(NS):
                for nt_ in range(2):
                    kt_, ktk_ = load_rows(cmk[b * 256 + nt_ * 128:b * 256 + (nt_ + 1) * 128, :], 128)
                    prep_mem_k(kt_[:], ktk_, nt_, memKT, "memKT")
                    vt_, vtk_ = load_rows(cmv[b * 256 + nt_ * 128:b * 256 + (nt_ + 1) * 128, :], 128)
                    P.op("dve", lambda h, nt_=nt_, vt_=vt_: h.tensor_copy(out=memV[:, nt_, :], in_=vt_[:]), r=[vtk_], w=["memV"])
                if CL >= 3:
                    cross_core(b, 1)
            x_s, xk_s = load_rows(xres[SEQ:SEQ + NS, :], NS, rk=[f"xres{SEQ // 128}"])
            if CL >= 4:
                cross_out(NS, x_s, xk_s, SEQ)

    if stage >= 6:
        NTOK = SEQ + NS
        hTall = big2[:, 0:8 * NTOK].rearrange("p (k t) -> p k t", t=NTOK)
        hidT = wbig[:, 0:6, 2560:3072]
        sg = q_hg
        groups = [(0, 6), (6, 6), (12, 5), (17, 5)]
        ld(mix[:], g_fin.partition_broadcast(128), ["mix"])
        for i in range(NT + 1):
            T = 128 if i < NT else NS
            x_, xk = load_rows(xres[i * 128:i * 128 + T, :], T, rk=[f"xres{i}"])
            rstd_of(x_[0:T, :], T, xk)
            P.op("dve", lambda h, T=T, x_=x_: h.tensor_scalar(out=hb[0:T, :], in0=x_[0:T, :], scalar1=ssq[0:T, 0:1], scalar2=None, op0=ALU.mult), r=[xk, "ssq"], w=["hb"])
            for c in range(8):
                P.op("pe", lambda h, c=c, T=T: h.transpose(out=psT[:, c * 128:c * 128 + T], in_=hb[0:T, c * 128:(c + 1) * 128], identity=ident_b[0:T, 0:T]), r=["hb", "ident_b"], w=["psT"])
            P.op("act", lambda h, T=T, i=i: h.activation(out=hTall[:, :, i * 128:i * 128 + T], in_=psT[:, 0:1024].rearrange("p (c t) -> p c t", t=128)[:, :, 0:T], func=AF.Copy),
                 r=["psT"], w=["hTall"])
        blocks = [(0, 512), (512, 512), (1024, 512), (1536, 512), (SEQ, NS)]
        for gi, (f0, nf) in enumerate(groups):
            load_weight(w_gate, wbig[:, :, 0:768], nf * 128, gk="ffn", key="wbig", col0=f0 * 128)
            load_weight(w_up, wbig[:, :, 768:1536], nf * 128, gk="ffn", key="wbig", col0=f0 * 128)
            load_weight(w_down[f0 * 128:(f0 + nf) * 128, :], wbig[:, 0:nf, 1536:2560], D, key="wbig", rows=nf * 128)
            for (t0, n) in blocks:
                for f in range(nf):
                    pg, pgk = nxt("A")
                    for k in range(8):
                        P.op("pe", lambda h, k=k, f=f, pg=pg, t0=t0, n=n: h.matmul(pg[:, 0:n], lhsT=wbig[:, k, f * 128:(f + 1) * 128], rhs=hTall[:, k, t0:t0 + n], start=(k == 0), stop=(k == 7)),
                             r=["wbig", "hTall"], w=[pgk])
                    pu_, puk_ = nxt("S")
                    for k in range(8):
                        P.op("pe", lambda h, k=k, f=f, pu_=pu_, t0=t0, n=n: h.matmul(pu_[:, 0:n], lhsT=wbig[:, k, 768 + f * 128:768 + (f + 1) * 128], rhs=hTall[:, k, t0:t0 + n], start=(k == 0), stop=(k == 7)),
                             r=["wbig", "hTall"], w=[puk_])
                    P.op("act", lambda h, pg=pg, n=n: h.activation(out=sg[:, 0:n], in_=pg[:, 0:n], func=AF.Silu), r=[pgk], w=["q_hg"])
                    P.op("dve", lambda h, f=f, pu_=pu_, n=n: h.tensor_tensor(out=hidT[:, f, 0:n], in0=pu_[:, 0:n], in1=sg[:, 0:n], op=ALU.mult), r=[puk_, "q_hg"], w=["hidT"])
                for st_ in range((n + 127) // 128):
                    T = min(128, n - st_ * 128)
                    i = t0 // 128 + st_
                    ydst = yp[i * 128:(i + 1) * 128, :] if i < NT else ys[0:NS, :]
                    src = xres if gi == 0 else facc
                    x_, xk = load_rows(src[i * 128:i * 128 + T, :], T, rk=[f"xres{i}" if gi == 0 else f"facc{i}"])
                    for cb in range(2):
                        p_, pk = nxt("O")
                        for f in range(nf):
                            P.op("pe", lambda h, f=f, cb=cb, p_=p_, T=T, st_=st_: h.matmul(p_[0:T, :], lhsT=hidT[:, f, st_ * 128:st_ * 128 + T], rhs=wbig[:, f, 1536 + cb * 512:1536 + (cb + 1) * 512],
                                                                                          start=(f == 0), stop=(f == nf - 1)), r=["hidT", "wbig"], w=[pk])
                        P.op("dve", lambda h, p_=p_, cb=cb, x_=x_, T=T: h.tensor_tensor(out=xo[0:T, cb * 512:(cb + 1) * 512], in0=p_[0:T, :], in1=x_[0:T, cb * 512:(cb + 1) * 512], op=ALU.add), r=[pk, xk], w=["xo"])
                    if gi < len(groups) - 1:
                        P.dma("sp", lambda h, i=i, T=T: h.dma_start(out=facc[i * 128:i * 128 + T, :], in_=xo[0:T, :]), r=["xo"], w=[f"facc{i}"])
                    else:
                        rstd_of(xo[0:T, :], T, "xo")
                        P.op("dve", lambda h, T=T: h.scalar_tensor_tensor(out=xo[0:T, :], in0=xo[0:T, :], scalar=ssq[0:T, 0:1], in1=mix[0:T, :], op0=ALU.mult, op1=ALU.mult), r=["xo", "ssq", "mix"], w=["xo"])
                        P.dma("sp", lambda h, T=T, ydst=ydst: h.dma_start(out=ydst, in_=xo[0:T, :]), r=["xo"], w=["out_yp"])

    if stage in (4, 5):
        x_, xk = load_rows(xres[0:128, :], 128, rk=["xres0"])
        P.dma("sp", lambda h: h.dma_start(out=yp[0:128, :], in_=x_[:]), r=[xk], w=["out_yp"])
    sems = {e: es.enter_context(nc.semaphore("s_" + e)) for e in ("pe", "act", "dve", "pool")}
    dsems = {q: [es.enter_context(nc.semaphore(f"d_{q}{i}")) for i in range(RING)] for q in ("sp", "pool")}
    block = es.enter_context(nc.Block())

    @block.tensor
    def _(h):
        P.emit("pe", h, sems, dsems)

    @block.scalar
    def _(h):
        P.emit("act", h, sems, dsems)

    @block.vector
    def _(h):
        P.emit("dve", h, sems, dsems)

    @block.gpsimd
    def _(h):
        P.emit("pool", h, sems, dsems)

    @block.sync
    def _(h):
        P.emit("sp", h, sems, dsems)

    es.close()
    return nc


def make_consts():
    c = {}
    c["c_ident"] = np.eye(128, dtype=np.float32)
    s = np.arange(128)
    c["c_caus"] = (s[:, None] <= s[None, :]).astype(np.float32)
    same = (s[:, None] // 64) == (s[None, :] // 64)
    c["c_u1"] = ((s[:, None] <= s[None, :]) & same).astype(np.float32)
    c["c_u2"] = (s[:, None] > s[None, :]).astype(np.float32)
    am = ((s[:, None] <= s[None, :]) & same).astype(np.float32)
    c["c_am"] = am
    ind = np.zeros((128, 2), np.float32); ind[:64, 0] = 1.0; ind[:, 1] = 1.0
    c["c_ind"] = ind
    inv = (10000.0 ** (-np.arange(0, 64, 2, dtype=np.float32) / 64)).astype(np.float32)
    pos = np.arange(SEQ, dtype=np.float32)
    ang = (pos[:, None] * inv[None, :]).astype(np.float32)
    c["c_cos"] = np.cos(ang.astype(np.float64)).astype(np.float32)
    c["c_sin"] = np.sin(ang.astype(np.float64)).astype(np.float32)
    angs = (np.float32(16384.0) * inv[None, :]).astype(np.float32)
    c["c_coss"] = np.cos(angs.astype(np.float64)).astype(np.float32)
    c["c_sins"] = np.sin(angs.astype(np.float64)).astype(np.float32)
    c["c_i4"] = np.eye(4, dtype=np.float32).reshape(1, 16)
    c["c_iota"] = (np.arange(128) % 64).astype(np.float32).reshape(128, 1)
    cpos = np.zeros((8, 4), np.float32); cneg = np.zeros((8, 4), np.float32)
    for hh in range(4):
        cpos[2 * hh, hh] = 1.0; cneg[2 * hh + 1, hh] = 1.0
    c["c_cpos"] = cpos; c["c_cneg"] = cneg
    bm = np.zeros((4, 512), np.float32)
    for hh in range(4):
        bm[hh, hh * 128:(hh + 1) * 128] = 1.0
    c["c_bm"] = bm
    return c


def core_inputs(c, I, consts, nphys, ckf=None, cvf=None):
    f = np.ascontiguousarray
    gc = lambda v: f(np.asarray(v).reshape(8, 128).T)
    m = dict(consts)
    m.update(
        ck=ckf if ckf is not None else I["cache_k"].reshape(nphys * 64, 1024), cv=cvf if cvf is not None else I["cache_v"].reshape(nphys * 64, 1024),
        xp=f(I["x_prompt"][c]), xs=f(I["x_sample"][4 * c:4 * c + 4, 0]), mem=f(I["mem_prompt"][c]),
        cmk=f(I["cache_mem_k"][0, 4 * c:4 * c + 4].reshape(NS * 256, D)), cmv=f(I["cache_mem_v"][0, 4 * c:4 * c + 4].reshape(NS * 256, D)),
        sh=f(I["state_hgrn"][0, 4 * c:4 * c + 4].reshape(NS * 512, 128)), pt=f(I["page_table"][4 * c:4 * c + 4]),
        w_in=I["w_in"][0], w_out=I["w_out"][0], w_mq=I["w_mq"][0], w_mk=I["w_mk"][0], w_mv=I["w_mv"][0], w_mo=I["w_mo"][0],
        w_gate=I["w_gate"][0], w_up=I["w_up"][0], w_down=I["w_down"][0],
        g_mix=gc(I["norm_mix"][0]), g_mq=gc(I["norm_mem_q"][0]), g_mkv=gc(I["norm_mem_kv"][0]), g_ffn=gc(I["norm_ffn"][0]),
        g_fin=f(I["norm_final"].reshape(1, D)), hg_lb=f(I["hg_lb"]), hg_on=f(I["hg_onorm"].reshape(1, 128)), da_on=f(I["da_onorm"].reshape(1, 128)),
        da_lam=f(I["da_lambda"].reshape(1, 256)),
    )
    return m


def kernel(**I):
    I = {k: np.asarray(v) for k, v in I.items()}
    nphys = I["cache_k"].shape[1]
    consts = make_consts()
    nc = build(nphys)
    ckf = I["cache_k"].reshape(nphys * 64, 1024); cvf = I["cache_v"].reshape(nphys * 64, 1024)
    in_maps = [core_inputs(c, I, consts, nphys, ckf, cvf) for c in range(8)]
    res = run_bass_kernel_spmd(nc, in_maps, core_ids=list(range(8))).results
    cat = lambda k: np.stack([r[k] for r in res])
    y_prompt = cat("yp")
    y_sample = cat("ys").reshape(32, 1, D)
    hsp = cat("hsp").reshape(1, 8, 4, 128, 128)
    kp = cat("kp").reshape(1, 8, SEQ, 4, 128)
    vp = cat("vp").reshape(1, 8, SEQ, 4, 128)
    mkp = cat("mkp").reshape(1, 8, 256, 4, 256)
    mvp = cat("mvp").reshape(1, 8, 256, 4, 256)
    hss = cat("hss").reshape(1, 32, 4, 128, 128)
    ks = cat("ks").reshape(1, 32, 1, 4, 128)
    vs = cat("vs").reshape(1, 32, 1, 4, 128)
    return (y_prompt, y_sample, hsp, kp, vp, mkp, mvp, hss, ks, vs)
```

```python
import os
import numpy as np
from contextlib import ExitStack
import concourse.bass as bass
import concourse.mybir as mybir
from concourse.bass_utils import run_bass_kernel_spmd

F32 = mybir.dt.float32
BF16 = mybir.dt.bfloat16
I32 = mybir.dt.int32
AF = mybir.ActivationFunctionType
ALU = mybir.AluOpType
AX = mybir.AxisListType

D = 1024
SEQ = 2048
NT = SEQ // 128
NS = 4
DFF = 2816
NPAGES = 128
EPS = 1e-6
LAM_INIT = 0.2
RING = 16
SL = int(os.environ.get("SL", "9"))
NPG = int(os.environ.get("NPG", "128"))
DL = int(os.environ.get("DL", "9"))
CL = int(os.environ.get("CL", "9"))


class Prog:
    ENGS = ("pe", "act", "dve", "pool", "sp")

    def __init__(self):
        self.ops = {e: [] for e in self.ENGS}
        self.res = {}
        self.dma_n = {"sp": 0, "pool": 0}

    def _deps(self, r, w):
        deps = set()
        for k in r:
            st = self.res.get(k)
            if st and st[0] is not None:
                deps.add(st[0])
        for k in w:
            st = self.res.get(k)
            if st:
                if st[0] is not None:
                    deps.add(st[0])
                deps.update(st[1])
        return deps

    def _update(self, r, w, ev):
        for k in r:
            self.res.setdefault(k, [None, []])[1].append(ev)
        for k in w:
            self.res[k] = [ev, []]

    def op(self, eng, fn, r=(), w=()):
        deps = self._deps(r, w)
        ev = ("c", eng, len(self.ops[eng]))
        self.ops[eng].append((fn, deps, ev))
        self._update(r, w, ev)

    def dma(self, q, fn, r=(), w=()):
        n = self.dma_n[q]
        self.dma_n[q] += 1
        ring, val = n % RING, 16 * (n // RING + 1)
        deps = self._deps(r, w)
        if n >= RING:
            deps.add(("d", q, ring, val - 16))
        ev = ("d", q, ring, val)
        self.ops[q].append((fn, deps, ev))
        self._update(r, w, ev)

    def emit(self, eng, h, sems, dsems):
        seen = {}
        for fn, deps, ev in self.ops[eng]:
            for d in sorted(deps):
                if d[0] == "c":
                    if d[1] == eng and eng == "pe":
                        continue
                    if seen.get(d[1], -1) >= d[2]:
                        continue
                    seen[d[1]] = d[2]
                    h.wait_ge(sems[d[1]], d[2] + 1)
                else:
                    key = (d[1], d[2])
                    if seen.get(key, 0) >= d[3]:
                        continue
                    seen[key] = d[3]
                    h.wait_ge(dsems[d[1]][d[2]], d[3])
            ins = fn(h)
            if ev[0] == "c":
                ins.then_inc(sems[eng], 1)
            else:
                ins.then_inc(dsems[ev[1]][ev[2]], 16)
        if eng == "sp":
            for e2 in self.ENGS:
                if e2 != "sp" and self.ops[e2]:
                    ncomp = sum(1 for o in self.ops[e2] if o[2][0] == "c")
                    if ncomp:
                        h.wait_ge(sems[e2], ncomp)
            for q in ("sp", "pool"):
                n = self.dma_n[q]
                for ring in range(min(n, RING)):
                    cnt = (n - ring + RING - 1) // RING
                    h.wait_ge(dsems[q][ring], 16 * cnt)


def build(nphys, stage=99):
    nc = bass.Bass("TRN2", target_bir_lowering=False)
    P = Prog()
    es = ExitStack()

    def din(name, shape, dt=F32):
        return nc.dram_tensor(name, list(shape), dt, kind="ExternalInput").ap()

    def dout(name, shape, dt=F32):
        return nc.dram_tensor(name, list(shape), dt, kind="ExternalOutput").ap()

    def dscr(name, shape, dt=F32):
        return nc.dram_tensor(name, list(shape), dt, kind="Internal").ap()

    cnt = [0]

    def sb(shape, dt=F32, name=None):
        cnt[0] += 1
        return es.enter_context(nc.sbuf_tensor(name or f"sb{cnt[0]}", list(shape), dt))

    def ps(shape, dt=F32, name=None):
        cnt[0] += 1
        return es.enter_context(nc.psum_tensor(name or f"ps{cnt[0]}", list(shape), dt))

    xp = din("xp", [SEQ, D]); xs = din("xs", [NS, D]); mem = din("mem", [256, D])
    cmk = din("cmk", [NS * 256, D]); cmv = din("cmv", [NS * 256, D])
    sh = din("sh", [NS * 4 * 128, 128]); pt = din("pt", [NS, NPAGES], I32)
    w_in = din("w_in", [D, 3584]); w_out = din("w_out", [D, D])
    w_mq = din("w_mq", [D, D]); w_mk = din("w_mk", [D, D]); w_mv = din("w_mv", [D, D]); w_mo = din("w_mo", [D, D])
    w_gate = din("w_gate", [D, DFF]); w_up = din("w_up", [D, DFF]); w_down = din("w_down", [DFF, D])
    g_mix = din("g_mix", [128, 8]); g_mq = din("g_mq", [128, 8]); g_mkv = din("g_mkv", [128, 8]); g_ffn = din("g_ffn", [128, 8])
    g_fin = din("g_fin", [1, D]); hg_lb = din("hg_lb", [2, 512]); hg_on = din("hg_on", [1, 128]); da_on = din("da_on", [1, 128])
    da_lam = din("da_lam", [1, 256])
    c_ident = din("c_ident", [128, 128]); c_caus = din("c_caus", [128, 128]); c_u1 = din("c_u1", [128, 128]); c_u2 = din("c_u2", [128, 128])
    c_am = din("c_am", [128, 128]); c_ind = din("c_ind", [128, 2])
    c_i4 = din("c_i4", [1, 16]); c_iota = din("c_iota", [128, 1]); c_cpos = din("c_cpos", [8, 4]); c_cneg = din("c_cneg", [8, 4]); c_bm = din("c_bm", [4, 512])
    ck = din("ck", [nphys * 64, 1024]); cv = din("cv", [nphys * 64, 1024])
    c_cos = din("c_cos", [SEQ, 32]); c_sin = din("c_sin", [SEQ, 32]); c_coss = din("c_coss", [1, 32]); c_sins = din("c_sins", [1, 32])

    yp = dout("yp", [SEQ, D]); ys = dout("ys", [NS, D]); hsp = dout("hsp", [512, 128]); kp = dout("kp", [SEQ, 512]); vp = dout("vp", [SEQ, 512])
    mkp = dout("mkp", [256, D]); mvp = dout("mvp", [256, D]); hss = dout("hss", [NS * 512, 128]); ks = dout("ks", [NS, 512]); vs = dout("vs", [NS, 512])

    xres = dscr("xres", [SEQ + 128, D])
    facc = dscr("facc", [SEQ + 128, D])

    ident_f = sb([128, 128]); ident_b = sb([128, 128], BF16); caus_b = sb([128, 128], BF16)
    u1 = sb([128, 128]); u2 = sb([128, 128]); am = sb([128, 128]); ind = sb([128, 2]); ones_b = sb([128, 128], BF16); ones_f = sb([128, 128])
    gcol = {k: sb([128, 8], name="gc_" + k) for k in ("mix", "mq", "mkv", "ffn")}
    lb_b = sb([128, 512]); oml_b = sb([128, 512]); lbt = sb([128, 1024], name="mix")
    hgon_b = sb([128, 128]); daon_b = sb([128, 128]); lam_t = sb([128, 256]); lamc = sb([128, 4]); neglam = sb([128, 1])
    cos_t = sb([128, NT, 32]); sin_t = sb([128, NT, 32]); coss_t = sb([128, 32]); sins_t = sb([128, 32])

    def ld(dst, src, keys_w, q="sp", keys_r=()):
        P.dma(q, lambda h, d=dst, s=src: h.dma_start(out=d, in_=s), r=keys_r, w=keys_w)

    def bc(ap, n):
        return ap.broadcast(0, n) if hasattr(ap, "broadcast") else ap

    ld(ident_f[:], c_ident, ["ident_f"]); ld(u1[:], c_u1, ["u1"]); ld(u2[:], c_u2, ["u2"]); ld(am[:], c_am, ["am"]); ld(ind[:], c_ind, ["ind"])
    ld(lbt[:, 0:128], c_caus, ["mix"])
    P.op("dve", lambda h: h.tensor_copy(out=ident_b[:], in_=ident_f[:]), r=["ident_f"], w=["ident_b"])
    P.op("dve", lambda h: h.tensor_copy(out=caus_b[:], in_=lbt[:, 0:128]), r=["mix"], w=["caus_b"])
    P.op("dve", lambda h: h.memset(ones_b[:], 1.0), w=["ones_b"])
    P.op("dve", lambda h: h.memset(ones_f[:], 1.0), w=["ones_f"])
    for k, g in (("mix", g_mix), ("mq", g_mq), ("mkv", g_mkv), ("ffn", g_ffn)):
        ld(gcol[k][:], g, ["gc_" + k])
    ld(hgon_b[:], hg_on.partition_broadcast(128), ["hgon_b"]); ld(daon_b[:], da_on.partition_broadcast(128), ["daon_b"])
    ld(lam_t[:], da_lam.partition_broadcast(128), ["lam_t"])
    ld(cos_t[:], c_cos.rearrange("(t p) f -> p t f", p=128), ["cos_t"]); ld(sin_t[:], c_sin.rearrange("(t p) f -> p t f", p=128), ["sin_t"])
    ld(coss_t[:], c_coss.partition_broadcast(128), ["coss_t"]); ld(sins_t[:], c_sins.partition_broadcast(128), ["sins_t"])
    ld(lbt[:, 0:512], hg_lb[0:1, :].partition_broadcast(128), ["mix"], keys_r=["caus_b"])
    ld(lbt[:, 512:1024], hg_lb[1:2, :].partition_broadcast(128), ["lbt2"], keys_r=["caus_b"])
    P.op("dve", lambda h: h.tensor_tensor(out=lb_b[:], in0=lbt[:, 0:512], in1=lbt[:, 512:1024], op=ALU.subtract), r=["mix", "lbt2"], w=["lb_b"])
    P.op("act", lambda h: h.activation(out=lb_b[:], in_=lb_b[:], func=AF.Sigmoid), r=["lb_b"], w=["lb_b"])
    P.op("dve", lambda h: h.tensor_scalar(out=oml_b[:], in0=lb_b[:], scalar1=-1.0, scalar2=1.0, op0=ALU.mult, op1=ALU.add), r=["lb_b"], w=["oml_b"])
    P.op("dve", lambda h: h.tensor_tensor(out=lam_t[:, 0:64], in0=lam_t[:, 0:64], in1=lam_t[:, 64:128], op=ALU.mult), r=["lam_t"], w=["lam_t"])
    P.op("dve", lambda h: h.tensor_tensor(out=lam_t[:, 128:192], in0=lam_t[:, 128:192], in1=lam_t[:, 192:256], op=ALU.mult), r=["lam_t"], w=["lam_t"])
    P.op("dve", lambda h: h.reduce_sum(out=lamc[:, 0:1], in_=lam_t[:, 0:64], axis=AX.X), r=["lam_t"], w=["lamc"])
    P.op("dve", lambda h: h.reduce_sum(out=lamc[:, 1:2], in_=lam_t[:, 128:192], axis=AX.X), r=["lam_t", "lamc"], w=["lamc"])
    P.op("act", lambda h: h.activation(out=lamc[:, 2:4], in_=lamc[:, 0:2], func=AF.Exp), r=["lamc"], w=["lamc"])
    P.op("dve", lambda h: h.tensor_tensor(out=neglam[:], in0=lamc[:, 3:4], in1=lamc[:, 2:3], op=ALU.subtract), r=["lamc"], w=["neglam"])
    P.op("dve", lambda h: h.tensor_scalar(out=neglam[:], in0=neglam[:], scalar1=-LAM_INIT, scalar2=None, op0=ALU.add), r=["neglam"], w=["neglam"])

    psA = [ps([128, 512], name=f"psA{i}") for i in range(2)]
    psT = ps([128, 1024], BF16, name="psT")
    psF = ps([128, 512], name="psF")
    psS = [ps([128, 512], name=f"psS{i}") for i in range(2)]
    psO = [ps([128, 512], name=f"psO{i}") for i in range(2)]
    rr = {"A": 0, "S": 0, "O": 0}

    def nxt(kind):
        rr[kind] ^= 1
        lst = {"A": psA, "S": psS, "O": psO}[kind]
        return lst[rr[kind]], f"ps{kind}{rr[kind]}"

    stg = [sb([128, 2, 512], name=f"stg{i}") for i in range(2)]
    stg_i = [0]

    def load_weight(wd, dst, ncols, gk=None, rows=D, key=None, col0=0):
        nk = rows // 128
        wv = wd.rearrange("(k p) n -> p k n", p=128)
        for c0 in range(0, ncols, 512):
            cw = min(512, ncols - c0)
            for k0 in range(0, nk, 2):
                kw = min(2, nk - k0)
                s = stg_i[0] & 1
                stg_i[0] += 1
                st = stg[s]
                P.dma("sp", lambda h, st=st, k0=k0, kw=kw, c0=c0, cw=cw: h.dma_start(out=st[:, 0:kw, 0:cw], in_=wv[:, k0:k0 + kw, col0 + c0:col0 + c0 + cw]),
                      w=[f"stg{s}"])
                for kq in range(kw):
                    if gk is None:
                        P.op("act", lambda h, st=st, k0=k0, kq=kq, c0=c0, cw=cw: h.activation(out=dst[:, k0 + kq, c0:c0 + cw], in_=st[:, kq, 0:cw], func=AF.Copy),
                             r=[f"stg{s}"], w=[key])
                    else:
                        g = gcol[gk]
                        P.op("act", lambda h, st=st, k0=k0, kq=kq, c0=c0, cw=cw, g=g: h.activation(out=dst[:, k0 + kq, c0:c0 + cw], in_=st[:, kq, 0:cw], func=AF.Copy,
                                                                                              scale=g[:, k0 + kq:k0 + kq + 1]),
                             r=[f"stg{s}", "gc_" + gk], w=[key])

    xt = [sb([128, D], name=f"xt{i}") for i in range(2)]
    xo = sb([128, D], name="xo"); kvtmp = xo
    hb = sb([128, D], BF16, name="hb"); junk = sb([128, D], BF16, name="junk")
    hT = sb([128, 8, 128], BF16, name="hT")
    ssq = sb([128, 4], name="ssq")
    xt_i = [0]

    def load_rows(src_ap, T, rk=()):
        s = xt_i[0] & 1
        xt_i[0] += 1
        P.dma("sp", lambda h: h.dma_start(out=xt[s][0:T, :], in_=src_ap), r=list(rk), w=[f"xt{s}"])
        return xt[s], f"xt{s}"

    def rstd_of(x_ap, T, xkey, width=D, col=0):
        P.op("act", lambda h: h.activation(out=junk[0:T, 0:width], in_=x_ap, func=AF.Square, accum_out=ssq[0:T, col:col + 1]), r=[xkey], w=["junk", "ssq"])
        P.op("act", lambda h: h.activation(out=ssq[0:T, col:col + 1], in_=ssq[0:T, col:col + 1], func=AF.Sqrt, scale=1.0 / width, bias=EPS), r=["ssq"], w=["ssq"])
        P.op("dve", lambda h: h.reciprocal(out=ssq[0:T, col:col + 1], in_=ssq[0:T, col:col + 1]), r=["ssq"], w=["ssq"])

    def transpose_to(dst, dkey, src_bf, skey, T, nchunks=8):
        for c in range(nchunks):
            P.op("pe", lambda h, c=c: h.transpose(out=psT[:, c * 128:c * 128 + T], in_=src_bf[0:T, c * 128:(c + 1) * 128], identity=ident_b[0:T, 0:T]),
                 r=[skey, "ident_b"], w=["psT"])
        P.op("act", lambda h: h.activation(out=dst[:, 0:nchunks, 0:T], in_=psT[:, 0:nchunks * 128].rearrange("p (c t) -> p c t", t=128)[:, :, 0:T], func=AF.Copy),
             r=["psT"], w=[dkey])

    def norm_T(x_ap, xkey, T):
        rstd_of(x_ap, T, xkey)
        P.op("dve", lambda h: h.tensor_scalar(out=hb[0:T, :], in0=x_ap, scalar1=ssq[0:T, 0:1], scalar2=None, op0=ALU.mult), r=[xkey, "ssq"], w=["hb"])
        transpose_to(hT, "hT", hb, "hb", T)

    def linear(lT, lkey, T, Wb, wkey, c0, cw, nk=8):
        pt_, pkey = nxt("A")
        for k in range(nk):
            P.op("pe", lambda h, k=k: h.matmul(pt_[0:T, 0:cw], lhsT=lT[:, k, 0:T], rhs=Wb[:, k, c0:c0 + cw], start=(k == 0), stop=(k == nk - 1)),
                 r=[lkey, wkey], w=[pkey])
        return pt_, pkey

    wbig = sb([128, 8, 3584], BF16, name="wbig")
    memKT = sb([128, 8, 256], BF16, name="memKT")
    memV = sb([128, 2, D], BF16, name="memV")
    kvb = sb([128, D], BF16, name="kvb")

    def prep_mem_k(k_ap, kkey, nt_, KT, ktkey):
        P.op("dve", lambda h: h.tensor_copy(out=kvb[:], in_=k_ap), r=[kkey], w=["kvb"])
        for c in range(8):
            P.op("pe", lambda h, c=c: h.transpose(out=psT[:, c * 128:(c + 1) * 128], in_=kvb[:, c * 128:(c + 1) * 128], identity=ident_b[:]),
                 r=["kvb", "ident_b"], w=["psT"])
        P.op("act", lambda h: h.activation(out=KT[:, :, nt_ * 128:(nt_ + 1) * 128], in_=psT[:].rearrange("p (c t) -> p c t", t=128), func=AF.Copy),
             r=["psT"], w=[ktkey])

    load_weight(w_mk, wbig, D, gk="mkv", key="wbig")
    load_weight(w_mv, wbig[:, :, 1024:2048], D, gk="mkv", key="wbig")
    for nt_ in range(2):
        x_, xk = load_rows(mem[nt_ * 128:(nt_ + 1) * 128, :], 128)
        norm_T(x_[:], xk, 128)
        for which in range(2):
            for cb in range(2):
                p_, pk = linear(hT, "hT", 128, wbig, "wbig", which * 1024 + cb * 512, 512)
                P.op("dve", lambda h, p_=p_, cb=cb: h.tensor_copy(out=kvtmp[:, cb * 512:(cb + 1) * 512], in_=p_[:]), r=[pk], w=["xo"])
            outd = (mkp, mvp)[which]
            P.dma("sp", lambda h, outd=outd, nt_=nt_: h.dma_start(out=outd[nt_ * 128:(nt_ + 1) * 128, :], in_=kvtmp[:]), r=["xo"], w=["out_mkv"])
            if which == 0:
                prep_mem_k(kvtmp[:], "xo", nt_, memKT, "memKT")
            else:
                P.op("dve", lambda h, nt_=nt_: h.tensor_copy(out=memV[:, nt_, :], in_=kvtmp[:]), r=["xo"], w=["memV"])


    wout_b = sb([128, 8, D], BF16, name="wout_b")
    load_weight(w_in, wbig, 3584, gk="mix", key="wbig")
    load_weight(w_out, wout_b, D, key="wout_b")
    big2 = sb([128, 16640], BF16, name="big2")
    KT = big2[:, 0:8192].rearrange("p (c t) -> p c t", t=SEQ)
    Vaug = big2[:, 8192:8192 + 8448].rearrange("p (t g f) -> p t g f", g=4, f=132)
    P.op("pool", lambda h: h.memset(big2[:, 8192:16640], 1.0), w=["Vaug"])
    S = sb([128, 4, 128], name="S"); S_b = sb([128, 4, 128], BF16, name="S_b")
    P.op("dve", lambda h: h.memset(S[:], 0.0), w=["S"])
    P.op("dve", lambda h: h.memset(S_b[:], 0.0), w=["S_b"])
    q_hg = sb([128, 512], name="q_hg"); logf = sb([128, 512], name="logf"); kk = sb([128, 512], name="kk"); tmpf = sb([128, 512], name="tmpf")
    v_hg = sb([128, 512], name="v_hg"); v_hgb = sb([128, 512], BF16, name="v_hgb"); gs = sb([128, 512], name="gs")
    qr = sb([128, 512], name="qr"); kr = sb([128, 512], name="kr"); vr = sb([128, 512], name="vr")
    qrb = sb([128, 512], BF16, name="qrb"); krb = sb([128, 512], BF16, name="krb")
    r1 = sb([128, 256], name="r1"); r2 = sb([128, 256], name="r2")
    ebuf = sb([128, 512], name="ebuf"); qe = sb([128, 512], BF16, name="qe"); kinv = sb([128, 512], BF16, name="kinv"); kdec = sb([128, 512], BF16, name="kdec")
    gam = sb([128, 8], name="gam")
    qeT = sb([128, 4, 128], BF16, name="qeT"); qeF = sb([128, 4, 128], BF16, name="qeF"); kinvT = sb([128, 4, 128], BF16, name="kinvT"); kdT0 = sb([128, 4, 64], BF16, name="kdT0")
    AT = sb([128, 128], BF16, name="AT")
    qT1 = sb([128, 4, 128], BF16, name="qT1"); qT2 = sb([128, 4, 128], BF16, name="qT2"); qTm = [qT1, qT2]
    P.op("pool", lambda h: h.memset(qT1[:], 0.0), w=["qT"])
    P.op("pool", lambda h: h.memset(qT2[:], 0.0), w=["qT"])
    Eb = [sb([128, 256], BF16, name=f"Eb{i}") for i in range(2)]
    rl = sb([128, 4], name="rl"); od = sb([128, 512], name="od"); tda = sb([128, 128], name="tda")
    mix = lbt; mixb = sb([128, D], BF16, name="mixb"); mixT = sb([128, 8, 128], BF16, name="mixT")
    daon08 = sb([128, 128], name="daon08")
    P.op("dve", lambda h: h.tensor_scalar(out=daon08[:], in0=daon_b[:], scalar1=1.0 - LAM_INIT, scalar2=None, op0=ALU.mult), r=["daon_b"], w=["daon08"])

    def rope(p_, pk, T, dst, dkey, cosap, sinap):
        pv = p_[0:T, :].rearrange("p (g two f) -> p g two f", two=2, f=32)
        dv_ = dst[0:T, :].rearrange("p (g two f) -> p g two f", two=2, f=32)
        cb_ = cosap.unsqueeze(1).to_broadcast([T, 8, 32]); sb_ = sinap.unsqueeze(1).to_broadcast([T, 8, 32])
        r1v = r1[0:T, :].rearrange("p (g f) -> p g f", f=32); r2v = r2[0:T, :].rearrange("p (g f) -> p g f", f=32)
        P.op("dve", lambda h: h.tensor_tensor(out=r1v, in0=pv[:, :, 0, :], in1=cb_, op=ALU.mult), r=[pk, "cos_t", "coss_t"], w=["r1"])
        P.op("dve", lambda h: h.tensor_tensor(out=r2v, in0=pv[:, :, 1, :], in1=sb_, op=ALU.mult), r=[pk, "sin_t", "sins_t"], w=["r2"])
        P.op("dve", lambda h: h.tensor_tensor(out=dv_[:, :, 0, :], in0=r1v, in1=r2v, op=ALU.subtract), r=["r1", "r2"], w=[dkey])
        P.op("dve", lambda h: h.tensor_tensor(out=r1v, in0=pv[:, :, 0, :], in1=sb_, op=ALU.mult), r=[pk, dkey], w=["r1"])
        P.op("dve", lambda h: h.tensor_tensor(out=r2v, in0=pv[:, :, 1, :], in1=cb_, op=ALU.mult), r=[pk, dkey], w=["r2"])
        P.op("dve", lambda h: h.tensor_tensor(out=dv_[:, :, 1, :], in0=r1v, in1=r2v, op=ALU.add), r=["r1", "r2", dkey], w=[dkey])

    def mixer_inputs(T, cosap, sinap):
        for cb in range(7):
            p_, pk = linear(hT, "hT", T, wbig, "wbig", cb * 512, 512)
            pp = p_[0:T, :]
            if cb == 0:
                P.op("act", lambda h, pp=pp: h.activation(out=q_hg[0:T, :], in_=pp, func=AF.Copy), r=[pk], w=["q_hg"])
            elif cb == 1:
                P.op("act", lambda h, pp=pp: h.activation(out=tmpf[0:T, :], in_=pp, func=AF.Sigmoid), r=[pk], w=["tmpf"])
                P.op("dve", lambda h: h.tensor_tensor(out=tmpf[0:T, :], in0=tmpf[0:T, :], in1=oml_b[0:T, :], op=ALU.mult), r=["tmpf", "oml_b"], w=["tmpf"])
                P.op("dve", lambda h: h.tensor_tensor(out=tmpf[0:T, :], in0=tmpf[0:T, :], in1=lb_b[0:T, :], op=ALU.add), r=["tmpf", "lb_b"], w=["tmpf"])
                P.op("act", lambda h: h.activation(out=logf[0:T, :], in_=tmpf[0:T, :], func=AF.Ln), r=["tmpf"], w=["logf"])
                P.op("dve", lambda h: h.tensor_scalar(out=kk[0:T, :], in0=tmpf[0:T, :], scalar1=-1.0, scalar2=1.0, op0=ALU.mult, op1=ALU.add), r=["tmpf"], w=["kk"])
            elif cb == 2:
                P.op("act", lambda h, pp=pp: h.activation(out=v_hg[0:T, :], in_=pp, func=AF.Copy), r=[pk], w=["v_hg"])
                P.op("dve", lambda h: h.tensor_copy(out=v_hgb[0:T, :], in_=v_hg[0:T, :]), r=["v_hg"], w=["v_hgb"])
            elif cb == 3:
                P.op("act", lambda h, pp=pp: h.activation(out=gs[0:T, :], in_=pp, func=AF.Silu), r=[pk], w=["gs"])
                P.op("dve", lambda h: h.tensor_tensor(out=gs[0:T, :].rearrange("p (g f) -> p g f", f=128), in0=gs[0:T, :].rearrange("p (g f) -> p g f", f=128),
                                                      in1=hgon_b[0:T, :].unsqueeze(1).to_broadcast([T, 4, 128]), op=ALU.mult), r=["gs", "hgon_b"], w=["gs"])
            elif cb == 4:
                rope(p_, pk, T, qr, "qr", cosap, sinap)
                P.op("dve", lambda h: h.tensor_scalar(out=qrb[0:T, :], in0=qr[0:T, :], scalar1=0.125, scalar2=None, op0=ALU.mult), r=["qr"], w=["qrb"])
            elif cb == 5:
                rope(p_, pk, T, kr, "kr", cosap, sinap)
                P.op("pool", lambda h: h.tensor_copy(out=krb[0:T, :], in_=kr[0:T, :]), r=["kr"], w=["krb"])
            else:
                P.op("act", lambda h, pp=pp: h.activation(out=vr[0:T, :], in_=pp, func=AF.Copy), r=[pk], w=["vr"])

    def head_norm(src_ap_fn, skeys, T, dst_col0, scale_ap_fn, scale_key, col):
        for hh in range(4):
            P.op("act", lambda h, hh=hh: h.activation(out=junk[0:T, 0:128], in_=src_ap_fn(hh), func=AF.Square, accum_out=ssq[0:T, col + hh:col + hh + 1]), r=skeys, w=["junk", "ssq"])
        P.op("act", lambda h: h.activation(out=ssq[0:T, col:col + 4], in_=ssq[0:T, col:col + 4], func=AF.Sqrt, scale=1.0 / 128, bias=EPS), r=["ssq"], w=["ssq"])
        P.op("dve", lambda h: h.reciprocal(out=ssq[0:T, col:col + 4], in_=ssq[0:T, col:col + 4]), r=["ssq"], w=["ssq"])
        for hh in range(4):
            P.op("dve", lambda h, hh=hh: h.scalar_tensor_tensor(out=mix[0:T, dst_col0 + hh * 128:dst_col0 + (hh + 1) * 128], in0=src_ap_fn(hh), scalar=ssq[0:T, col + hh:col + hh + 1],
                                                                in1=scale_ap_fn(hh), op0=ALU.mult, op1=ALU.mult), r=skeys + ["ssq", scale_key], w=["mix"])

    def out_proj(T, x_ap, xkey, row0):
        P.op("act", lambda h: h.activation(out=mixb[0:T, :], in_=mix[0:T, :], func=AF.Copy), r=["mix"], w=["mixb"])
        transpose_to(mixT, "mixT", mixb, "mixb", T)
        for cb in range(2):
            p_, pk = linear(mixT, "mixT", T, wout_b, "wout_b", cb * 512, 512)
            P.op("dve", lambda h, p_=p_, cb=cb: h.tensor_tensor(out=xo[0:T, cb * 512:(cb + 1) * 512], in0=p_[0:T, :], in1=x_ap[:, cb * 512:(cb + 1) * 512], op=ALU.add), r=[pk, xkey], w=["xo"])
        P.dma("sp", lambda h: h.dma_start(out=xres[row0:row0 + T, :], in_=xo[0:T, :]), r=["xo"], w=[f"xres{row0 // 128}"])

    ssq8 = 0
    for i in range(NT if stage >= 1 else 0):
        x_, xk = load_rows(xp[i * 128:(i + 1) * 128, :], 128)
        norm_T(x_[:], xk, 128)
        mixer_inputs(128, cos_t[:, i, :], sin_t[:, i, :])
        P.dma("sp", lambda h, i=i: h.dma_start(out=kp[i * 128:(i + 1) * 128, :], in_=kr[:]), r=["kr"], w=["out_kp"])
        P.dma("sp", lambda h, i=i: h.dma_start(out=vp[i * 128:(i + 1) * 128, :], in_=vr[:]), r=["vr"], w=["out_vp"])
        P.op("pool", lambda h, i=i: h.tensor_copy(out=Vaug[:, i, :, 0:128], in_=vr[:].rearrange("p (g f) -> p g f", f=128)), r=["vr"], w=["Vaug"])
        if stage < 2:
            continue
        pb, pbk = nxt("A")
        P.op("pe", lambda h, pb=pb: h.matmul(pb[:], lhsT=u1[:], rhs=logf[:], start=True, stop=True), r=["u1", "logf"], w=[pbk])
        P.op("act", lambda h, pb=pb: h.activation(out=ebuf[:], in_=pb[:], func=AF.Exp), r=[pbk], w=["ebuf"])
        P.op("dve", lambda h: h.tensor_tensor(out=qe[:], in0=q_hg[:], in1=ebuf[:], op=ALU.mult), r=["q_hg", "ebuf"], w=["qe"])
        P.op("act", lambda h, pb=pb: h.activation(out=ebuf[:], in_=pb[:], func=AF.Exp, scale=-1.0), r=[pbk, "qe"], w=["ebuf"])
        P.op("dve", lambda h: h.tensor_tensor(out=kinv[:], in0=kk[:], in1=ebuf[:], op=ALU.mult), r=["kk", "ebuf"], w=["kinv"])
        pd, pdk = nxt("A")
        P.op("pe", lambda h, pd=pd: h.matmul(pd[:], lhsT=u2[:], rhs=logf[:], start=True, stop=True), r=["u2", "logf"], w=[pdk])
        P.op("act", lambda h, pd=pd: h.activation(out=ebuf[:], in_=pd[:], func=AF.Exp), r=[pdk, "kinv"], w=["ebuf"])
        P.op("dve", lambda h: h.tensor_tensor(out=kdec[:], in0=kk[:], in1=ebuf[:], op=ALU.mult), r=["kk", "ebuf"], w=["kdec"])
        for hh in range(4):
            P.op("pe", lambda h, hh=hh: h.matmul(psF[:, hh * 2:hh * 2 + 2], lhsT=logf[:, hh * 128:(hh + 1) * 128], rhs=ind[:], start=True, stop=True), r=["logf", "ind"], w=["psF"])
        P.op("act", lambda h: h.activation(out=gam[:], in_=psF[:, 0:8], func=AF.Exp), r=["psF"], w=["gam"])
        transpose_to(qeT, "qeT", qe, "qe", 128, nchunks=4)
        transpose_to(kinvT, "kinvT", kinv, "kinv", 128, nchunks=4)
        for hh in range(4):
            P.op("dve", lambda h, hh=hh: h.tensor_copy(out=qeF[:, hh, 0:64], in_=qeT[:, hh, 0:64]), r=["qeT"], w=["qeF"])
            P.op("dve", lambda h, hh=hh: h.tensor_scalar(out=qeF[:, hh, 64:128], in0=qeT[:, hh, 64:128], scalar1=gam[:, 2 * hh:2 * hh + 1], scalar2=None, op0=ALU.mult), r=["qeT", "gam"], w=["qeF"])
            P.op("dve", lambda h, hh=hh: h.tensor_scalar(out=kdT0[:, hh, :], in0=kinvT[:, hh, 0:64], scalar1=gam[:, 2 * hh:2 * hh + 1], scalar2=None, op0=ALU.mult), r=["kinvT", "gam"], w=["kdT0"])
        po, pok = nxt("O")
        for hh in range(4):
            pa, pak = nxt("S")
            P.op("pe", lambda h, hh=hh, pa=pa: h.matmul(pa[:, 0:128], lhsT=kinvT[:, hh, :], rhs=qeT[:, hh, :], start=True, stop=True), r=["kinvT", "qeT"], w=[pak])
            P.op("pe", lambda h, hh=hh, pa=pa: h.matmul(pa[0:64, 128:192], lhsT=kdT0[:, hh, :], rhs=qeT[:, hh, 64:128], start=True, stop=True), r=["kdT0", "qeT"], w=[pak])
            P.op("dve", lambda h, pa=pa: h.tensor_tensor(out=AT[:], in0=pa[:, 0:128], in1=am[:], op=ALU.mult), r=[pak, "am"], w=["AT"])
            P.op("dve", lambda h, pa=pa: h.tensor_copy(out=AT[0:64, 64:128], in_=pa[0:64, 128:192]), r=[pak, "AT"], w=["AT"])
            P.op("pe", lambda h, hh=hh, po=po: h.matmul(po[:, hh * 128:(hh + 1) * 128], lhsT=AT[:], rhs=v_hgb[:, hh * 128:(hh + 1) * 128], start=True, stop=False), r=["AT", "v_hgb"], w=[pok])
            P.op("pe", lambda h, hh=hh, po=po: h.matmul(po[:, hh * 128:(hh + 1) * 128], lhsT=qeF[:, hh, :], rhs=S_b[:, hh, :], start=False, stop=True), r=["qeF", "S_b"], w=[pok])
            pu, puk = nxt("A")
            P.op("pe", lambda h, hh=hh, pu=pu: h.matmul(pu[:, 0:128], lhsT=kdec[:, hh * 128:(hh + 1) * 128], rhs=v_hgb[:, hh * 128:(hh + 1) * 128], start=True, stop=True), r=["kdec", "v_hgb"], w=[puk])
            P.op("dve", lambda h, hh=hh, pu=pu: h.scalar_tensor_tensor(out=S[:, hh, :], in0=S[:, hh, :], scalar=gam[:, 2 * hh + 1:2 * hh + 2], in1=pu[:, 0:128], op0=ALU.mult, op1=ALU.add),
                 r=["S", "gam", puk], w=["S"])
        P.op("act", lambda h: h.activation(out=S_b[:], in_=S[:], func=AF.Copy), r=["S"], w=["S_b"])
        head_norm(lambda hh, po=po: po[:, hh * 128:(hh + 1) * 128], [pok], 128, 0, lambda hh: gs[:, hh * 128:(hh + 1) * 128], "gs", 0)
        if i == NT - 1:
            for hh in range(4):
                P.dma("sp", lambda h, hh=hh: h.dma_start(out=hsp[hh * 128:(hh + 1) * 128, :], in_=S[:, hh, :]), r=["S"], w=["out_hsp"])
        if stage < 3:
            continue
        for c_ in range(4):
            P.op("pe", lambda h, c_=c_: h.transpose(out=psT[:, c_ * 128:(c_ + 1) * 128], in_=qrb[:, c_ * 128:(c_ + 1) * 128], identity=ident_b[:]), r=["qrb", "ident_b"], w=["psT"])
        P.op("act", lambda h: h.activation(out=qT1[0:64, :, :], in_=psT[0:64, 0:512].rearrange("p (c t) -> p c t", t=128), func=AF.Copy), r=["psT"], w=["qT"])
        P.op("act", lambda h: h.activation(out=qT2[64:128, :, :], in_=psT[64:128, 0:512].rearrange("p (c t) -> p c t", t=128), func=AF.Copy), r=["psT", "qT"], w=["qT"])
        for c_ in range(4):
            P.op("pe", lambda h, c_=c_: h.transpose(out=psT[:, c_ * 128:(c_ + 1) * 128], in_=krb[:, c_ * 128:(c_ + 1) * 128], identity=ident_b[:]), r=["krb", "ident_b"], w=["psT"])
        P.op("act", lambda h, i=i: h.activation(out=KT[:, :, i * 128:(i + 1) * 128], in_=psT[:, 0:512].rearrange("p (c t) -> p c t", t=128), func=AF.Copy), r=["psT"], w=["KT"])
        for hh in range(4):
            for j in range(i + 1):
                pS, pSk = nxt("S")
                for m_ in range(2):
                    P.op("pe", lambda h, hh=hh, j=j, m_=m_, pS=pS: h.matmul(pS[:, m_ * 128:(m_ + 1) * 128], lhsT=KT[:, hh, j * 128:(j + 1) * 128],
                                                                          rhs=qTm[m_][:, hh, :], start=True, stop=True), r=["KT", "qT"], w=[pSk])
                E_ = Eb[j & 1]; ek = f"Eb{j & 1}"
                P.op("act", lambda h, pS=pS, E_=E_: h.activation(out=E_[:], in_=pS[:, 0:256], func=AF.Exp), r=[pSk], w=[ek])
                if j == i:
                    P.op("dve", lambda h, E_=E_: h.tensor_tensor(out=E_[:].rearrange("p (m q) -> p m q", m=2), in0=E_[:].rearrange("p (m q) -> p m q", m=2),
                                                               in1=caus_b[:].unsqueeze(1).to_broadcast([128, 2, 128]), op=ALU.mult), r=[ek, "caus_b"], w=[ek])
                for m_ in range(2):
                    P.op("pe", lambda h, hh=hh, j=j, m_=m_, E_=E_: h.matmul(psO[m_][:, 0:130], lhsT=E_[:, m_ * 128:(m_ + 1) * 128], rhs=Vaug[:, j, hh, 0:130],
                                                                          start=(j == 0), stop=(j == i)), r=[ek, "Vaug"], w=[f"psO{m_}"])
            P.op("dve", lambda h: h.reciprocal(out=rl[:, 0:1], in_=psO[0][:, 128:129]), r=["psO0"], w=["rl"])
            P.op("dve", lambda h: h.reciprocal(out=rl[:, 1:2], in_=psO[1][:, 128:129]), r=["psO1", "rl"], w=["rl"])
            P.op("dve", lambda h: h.tensor_tensor(out=rl[:, 2:3], in0=rl[:, 1:2], in1=neglam[:], op=ALU.mult), r=["rl", "neglam"], w=["rl"])
            P.op("dve", lambda h: h.tensor_scalar(out=tda[:], in0=psO[0][:, 0:128], scalar1=rl[:, 0:1], scalar2=None, op0=ALU.mult), r=["psO0", "rl"], w=["tda"])
            P.op("dve", lambda h, hh=hh: h.scalar_tensor_tensor(out=od[:, hh * 128:(hh + 1) * 128], in0=psO[1][:, 0:128], scalar=rl[:, 2:3], in1=tda[:], op0=ALU.mult, op1=ALU.add),
                 r=["psO1", "rl", "tda"], w=["od"])
        head_norm(lambda hh: od[:, hh * 128:(hh + 1) * 128], ["od"], 128, 512, lambda hh: daon08[:], "daon08", 0)
        if stage < 4:
            continue
        out_proj(128, x_[:], xk, i * 128)


    if stage >= 4:
        x_, xk = load_rows(xs[0:NS, :], NS)
        norm_T(x_[0:NS, :], xk, NS)
        mixer_inputs(NS, coss_t[0:NS, :], sins_t[0:NS, :])
        P.dma("sp", lambda h: h.dma_start(out=ks[0:NS, :], in_=kr[0:NS, :]), r=["kr"], w=["out_ks"])
        P.dma("sp", lambda h: h.dma_start(out=vs[0:NS, :], in_=vr[0:NS, :]), r=["vr"], w=["out_vs"])
        smalls = sb([128, 64], name="smalls"); i4b = sb([128, 16], name="i4b"); qmask = sb([128, 64], name="qmask"); colsT = sb([128, 48], name="colsT")
        iota_c = sb([128, 1], name="iota_c"); cpos = sb([8, 4], name="cpos"); cneg = sb([8, 4], name="cneg"); Cm = sb([8, 4], name="Cm")
        ld(i4b[:], c_i4.partition_broadcast(128), ["i4b"]); ld(iota_c[:], c_iota, ["iota_c"]); ld(cpos[:], c_cpos, ["cpos"]); ld(cneg[:], c_cneg, ["cneg"])
        P.op("dve", lambda h: h.scalar_tensor_tensor(out=Cm[:], in0=cneg[:], scalar=neglam[0:8, 0:1], in1=cpos[:], op0=ALU.mult, op1=ALU.add), r=["cneg", "cpos", "neglam"], w=["Cm"])
        Ss = big2[:, 0:4096].bitcast(F32).rearrange("p (g v) -> p g v", v=128)
        s_all = big2[:, 4096:6144].bitcast(F32)
        Kt = [big2[:, 6144 + r_ * 2048:6144 + (r_ + 1) * 2048].bitcast(F32) for r_ in range(2)]
        Vt = [big2[:, 10240 + r_ * 2048:10240 + (r_ + 1) * 2048].bitcast(F32) for r_ in range(2)]
        prodt = [big2[:, 0:2048].bitcast(F32)]
        selT = kvb[:].bitcast(F32)[0:4, :].rearrange("p (b m) -> p b m", m=128)
        P.op("dve", lambda h: h.tensor_copy(out=selT, in_=ident_f[0:4, 0:4].unsqueeze(2).to_broadcast([4, 4, 128])), r=["ident_f", "KT", "Vaug"], w=["kvb"])
        P.dma("sp", lambda h: h.dma_start(out=Ss, in_=sh.rearrange("(g p) v -> p g v", p=128)), r=["KT", "Vaug"], w=["Ss"])
        if SL >= 1:
            P.op("act", lambda h: h.activation(out=tmpf[0:4, :], in_=logf[0:4, :], func=AF.Exp), r=["logf"], w=["tmpf"])
            for wi, (src, skey) in enumerate(((tmpf, "tmpf"), (kk, "kk"), (q_hg, "q_hg"))):
                for hh in range(4):
                    P.op("pe", lambda h, wi=wi, hh=hh, src=src: h.transpose(out=psF[:, (wi * 4 + hh) * 4:(wi * 4 + hh) * 4 + 4], in_=src[0:4, hh * 128:(hh + 1) * 128], identity=ident_f[0:4, 0:4]),
                         r=[skey, "ident_f"], w=["psF"])
            P.op("act", lambda h: h.activation(out=colsT[:], in_=psF[:, 0:48], func=AF.Copy), r=["psF"], w=["colsT"])
            for b in range(NS):
                pvb, pvbk = nxt("A")
                P.op("pe", lambda h, b=b, pvb=pvb: h.matmul(pvb[:, 0:512], lhsT=selT[:, b, :], rhs=v_hg[0:4, :], start=True, stop=True), r=["kvb", "v_hg"], w=[pvbk])
                for hh in range(4):
                    g_ = b * 4 + hh
                    P.op("dve", lambda h, hh=hh, b=b, pvb=pvb: h.tensor_scalar(out=tda[:], in0=pvb[:, hh * 128:(hh + 1) * 128], scalar1=colsT[:, 16 + hh * 4 + b:16 + hh * 4 + b + 1], scalar2=None, op0=ALU.mult),
                         r=[pvbk, "colsT"], w=["tda"])
                    P.op("dve", lambda h, hh=hh, b=b, g_=g_: h.scalar_tensor_tensor(out=Ss[:, g_, :], in0=Ss[:, g_, :], scalar=colsT[:, hh * 4 + b:hh * 4 + b + 1], in1=tda[:], op0=ALU.mult, op1=ALU.add),
                         r=["Ss", "colsT", "tda"], w=["Ss"])
            P.dma("sp", lambda h: h.dma_start(out=hss.rearrange("(g p) v -> p g v", p=128), in_=Ss), r=["Ss"], w=["out_hss"])
            qm4 = qmask[:].rearrange("p (a b c) -> p a b c", b=4, c=4)
            for hh in range(4):
                P.op("dve", lambda h, hh=hh: h.tensor_tensor(out=qm4[:, hh, :, :], in0=colsT[:, 32 + hh * 4:32 + hh * 4 + 4].unsqueeze(1).to_broadcast([128, 4, 4]),
                                                             in1=i4b[:].rearrange("p (a b) -> p a b", b=4), op=ALU.mult), r=["colsT", "i4b"], w=["qmask"])
            po, pok = nxt("O")
            for hh in range(4):
                for b in range(NS):
                    P.op("pe", lambda h, hh=hh, b=b, po=po: h.matmul(po[0:4, hh * 128:(hh + 1) * 128], lhsT=qm4[:, hh, b, :], rhs=Ss[:, b * 4 + hh, :], start=(b == 0), stop=(b == NS - 1)),
                         r=["qmask", "Ss"], w=[pok])
            head_norm(lambda hh, po=po: po[0:4, hh * 128:(hh + 1) * 128], [pok], NS, 0, lambda hh: gs[0:4, hh * 128:(hh + 1) * 128], "gs", 0)

        if SL >= 2:
            pti_t = sb([128, 512], I32, name="pti_t"); pti = pti_t[:]
            ptv = pt.rearrange("(o b) (g two) -> o two b g", o=1, two=2)
            for h2 in range(2):
                for b_ in range(NS):
                    P.dma("sp", lambda h, h2=h2, b_=b_: h.dma_start(out=pti[h2 * 64:(h2 + 1) * 64, b_ * 64:(b_ + 1) * 64], in_=ptv[:, h2, b_, :].partition_broadcast(64), allow_slow_non_contiguous=True), w=["ebuf"])
            P.op("dve", lambda h: h.tensor_copy(out=tmpf[:, 0:256], in_=pti[:, 0:256]), r=["ebuf"], w=["tmpf"])
            P.op("dve", lambda h: h.tensor_scalar(out=tmpf[:, 0:256], in0=tmpf[:, 0:256], scalar1=64.0, scalar2=iota_c[:, 0:1], op0=ALU.mult, op1=ALU.add), r=["tmpf", "iota_c"], w=["tmpf"])
            P.op("dve", lambda h: h.tensor_copy(out=pti[:, 0:256], in_=tmpf[:, 0:256]), r=["tmpf"], w=["ebuf"])
            qs = logf
            P.op("dve", lambda h: h.tensor_scalar(out=qs[0:4, :], in0=qr[0:4, :], scalar1=0.125, scalar2=None, op0=ALU.mult), r=["qr", "logf"], w=["logf"])
            P.op("dve", lambda h: h.tensor_tensor(out=kk[0:4, :], in0=qs[0:4, :], in1=kr[0:4, :], op=ALU.mult), r=["logf", "kr", "kk"], w=["kk"])
            P.op("dve", lambda h: h.tensor_reduce(out=smalls[0:4, 0:8], in_=kk[0:4, :].rearrange("p (m d) -> p m d", d=64), axis=AX.X, op=ALU.add), r=["kk"], w=["smalls"])
            P.op("act", lambda h: h.activation(out=smalls[0:4, 8:16], in_=smalls[0:4, 0:8], func=AF.Exp), r=["smalls"], w=["smalls"])
            bm_t = v_hg
            P.dma("sp", lambda h: h.dma_start(out=bm_t[0:4, :], in_=c_bm), r=["v_hg"], w=["v_hg"])
        if SL == 2:
            P.dma("sp", lambda h: h.dma_start(out=kp[0:128, :], in_=tmpf[:]), r=["tmpf"], w=["out_kp"])
            P.dma("sp", lambda h: h.dma_start(out=vp[0:128, :], in_=ebuf[:]), r=["ebuf"], w=["out_vp"])
        if SL >= 3:
            s_alls = [s_all, big2[:, 2048:4096].bitcast(F32)]
            sm = [smalls, sb([128, 64], name="smalls2")]
            pov = povk = None
            for b in range(NS + 1):
                kb, vb = b, b - 1
                if kb < NS:
                    sa = s_alls[kb % 2]; sak = f"s_all{kb % 2}"; smk = sm[kb % 2]; smkk = f"sm{kb % 2}"
                    pq, pqk = nxt("A")
                    P.op("pe", lambda h, kb=kb, pq=pq: h.matmul(pq[:, 0:512], lhsT=selT[:, kb, :], rhs=qs[0:4, :], start=True, stop=True), r=["kvb", "logf"], w=[pqk])
                    P.op("act", lambda h, pq=pq: h.activation(out=q_hg[:], in_=pq[:, 0:512], func=AF.Copy), r=[pqk], w=["q_hg"])
                if vb >= 0:
                    sv = s_alls[vb % 2]; svk = f"s_all{vb % 2}"; smv = sm[vb % 2]; smvk = f"sm{vb % 2}"
                    pov, povk = nxt("O")
                for g in range(NPAGES // 2):
                    r_ = g % 2
                    if kb < NS:
                        P.dma("pool", lambda h, kb=kb, g=g, r_=r_: h.indirect_dma_start(out=Kt[r_], out_offset=None, in_=ck, in_offset=bass.IndirectOffsetOnAxis(ap=pti[:, kb * 64 + g:kb * 64 + g + 1], axis=0)),
                              r=["ebuf"], w=[f"Kt{r_}"])
                        P.op("dve", lambda h, r_=r_: h.tensor_tensor(out=prodt[0].rearrange("p (t f) -> p t f", t=2), in0=Kt[r_].rearrange("p (t f) -> p t f", t=2),
                                                                 in1=q_hg[:].unsqueeze(1).to_broadcast([128, 2, 512]), op=ALU.mult), r=[f"Kt{r_}", "q_hg"], w=["prodt0", "Ss"])
                        P.op("dve", lambda h, g=g, sa=sa: h.tensor_reduce(out=sa[:, g * 16:(g + 1) * 16], in_=prodt[0].rearrange("p (m d) -> p m d", d=64), axis=AX.X, op=ALU.add), r=["prodt0"], w=[sak])
                    if vb >= 0:
                        P.dma("pool", lambda h, vb=vb, g=g, r_=r_: h.indirect_dma_start(out=Vt[r_], out_offset=None, in_=cv, in_offset=bass.IndirectOffsetOnAxis(ap=pti[:, vb * 64 + g:vb * 64 + g + 1], axis=0)),
                              r=["ebuf"], w=[f"Vt{r_}"])
                        for t2 in range(2):
                            P.op("pe", lambda h, g=g, t2=t2, r_=r_, pov=pov, sv=sv: h.matmul(pov[0:8, 0:512], lhsT=sv[:, g * 16 + t2 * 8:g * 16 + t2 * 8 + 8], rhs=Vt[r_][:, t2 * 512:(t2 + 1) * 512],
                                                                                        start=(g == 0 and t2 == 0), stop=False), r=[svk, f"Vt{r_}"], w=[povk])
                if vb >= 0:
                    P.op("pe", lambda h, pov=pov, smv=smv: h.matmul(pov[0:8, 0:512], lhsT=smv[0:4, 16:24], rhs=vr[0:4, :], start=False, stop=True), r=[smvk, "vr"], w=[povk])
                    pol, polk = nxt("S")
                    P.op("pe", lambda h, pol=pol, smv=smv: h.matmul(pol[0:8, 0:1], lhsT=smv[:, 24:32], rhs=ones_f[:, 0:1], start=True, stop=False), r=[smvk, "ones_f"], w=[polk])
                    P.op("pe", lambda h, pol=pol, smv=smv: h.matmul(pol[0:8, 0:1], lhsT=smv[0:4, 16:24], rhs=ones_f[0:4, 0:1], start=False, stop=True), r=[smvk, "ones_f"], w=[polk])
                    P.op("dve", lambda h, pol=pol, smv=smv: h.reciprocal(out=smv[0:8, 32:33], in_=pol[0:8, 0:1]), r=[polk], w=[smvk])
                    P.op("dve", lambda h, pov=pov, smv=smv: h.tensor_scalar(out=kk[0:8, :], in0=pov[0:8, 0:512], scalar1=smv[0:8, 32:33], scalar2=None, op0=ALU.mult), r=[povk, smvk], w=["kk"])
                    pfin, pfink = nxt("A")
                    P.op("pe", lambda h, pfin=pfin: h.matmul(pfin[0:4, 0:512], lhsT=Cm[0:8, 0:4], rhs=kk[0:8, :], start=True, stop=True), r=["Cm", "kk"], w=[pfink])
                    P.op("dve", lambda h, pfin=pfin: h.tensor_tensor(out=tmpf[0:4, :], in0=pfin[0:4, 0:512], in1=bm_t[0:4, :], op=ALU.mult), r=[pfink, "v_hg"], w=["tmpf"])
                    P.op("pe", lambda h, vb=vb: h.matmul(psF[0:4, 0:512], lhsT=i4b[0:4, vb * 4:(vb + 1) * 4], rhs=tmpf[0:4, :], start=(vb == 0), stop=(vb == NS - 1)), r=["i4b", "tmpf"], w=["psF"])
                if kb < NS:
                    P.op("act", lambda h, sa=sa: h.activation(out=sa, in_=sa, func=AF.Exp), r=[sak], w=[sak])
                    P.op("dve", lambda h, sa=sa, smk=smk: h.tensor_reduce(out=smk[:, 24:32], in_=sa.rearrange("p (j m) -> p m j", m=8), axis=AX.X, op=ALU.add), r=[sak], w=[smkk])
                    P.op("dve", lambda h, kb=kb, smk=smk: h.tensor_scalar(out=smk[0:4, 16:24], in0=smalls[0:4, 8:16], scalar1=ident_f[0:4, kb:kb + 1], scalar2=None, op0=ALU.mult), r=["smalls", "ident_f", smkk], w=[smkk])
        if SL >= 4:
            head_norm(lambda hh: psF[0:4, hh * 128:(hh + 1) * 128], ["psF"], NS, 512, lambda hh: daon08[0:4, :], "daon08", 0)
        x_, xk = load_rows(xs[0:NS, :], NS)
        out_proj(NS, x_[0:NS, :], xk, SEQ)

    if stage >= 5:
        load_weight(w_mq, wbig, D, gk="mq", key="wbig")
        load_weight(w_mo, wbig[:, :, 1024:2048], D, key="wbig")
        qTc = big2[:, 0:4096].rearrange("p (c t) -> p c t", t=512); oTc = big2[:, 4096:8192].rearrange("p (c t) -> p c t", t=512)
        ETc = big2[:, 8192:9216].rearrange("p (c t) -> p c t", t=512); hT4 = big2[:, 9216:13312].rearrange("p (c t) -> p c t", t=512); rlc = q_hg
        def cross_core(c0, n):
            for hh in range(4):
                for nt_ in range(2):
                    pS, pSk = nxt("S")
                    for c_ in range(2):
                        P.op("pe", lambda h, hh=hh, nt_=nt_, c_=c_, pS=pS: h.matmul(pS[:, 0:n], lhsT=memKT[:, hh * 2 + c_, nt_ * 128:(nt_ + 1) * 128], rhs=qTc[:, hh * 2 + c_, c0:c0 + n],
                                                                                  start=(c_ == 0), stop=(c_ == 1)), r=["memKT", "qTc"], w=[pSk])
                    P.op("act", lambda h, nt_=nt_, pS=pS: h.activation(out=ETc[:, nt_, 0:n], in_=pS[:, 0:n], func=AF.Exp), r=[pSk], w=["ETc"])
                for nt_ in range(2):
                    P.op("pe", lambda h, nt_=nt_: h.matmul(psF[:, 0:n], lhsT=ones_b[:], rhs=ETc[:, nt_, 0:n], start=(nt_ == 0), stop=(nt_ == 1)), r=["ones_b", "ETc"], w=["psF"])
                P.op("dve", lambda h: h.reciprocal(out=rlc[:, 0:n], in_=psF[:, 0:n]), r=["psF"], w=["q_hg"])
                for c_ in range(2):
                    po_, pok_ = nxt("O")
                    for nt_ in range(2):
                        P.op("pe", lambda h, hh=hh, nt_=nt_, c_=c_, po_=po_: h.matmul(po_[:, 0:n], lhsT=memV[:, nt_, hh * 256 + c_ * 128:hh * 256 + (c_ + 1) * 128], rhs=ETc[:, nt_, 0:n],
                                                                                    start=(nt_ == 0), stop=(nt_ == 1)), r=["memV", "ETc"], w=[pok_])
                    P.op("dve", lambda h, hh=hh, c_=c_, po_=po_: h.tensor_tensor(out=oTc[:, hh * 2 + c_, c0:c0 + n], in0=po_[:, 0:n], in1=rlc[:, 0:n], op=ALU.mult), r=[pok_, "q_hg"], w=["oTc"])

        def cross_q(T, src=None, skey="hT"):
            src = hT if src is None else src
            for oc in range(8):
                pq, pqk = nxt("A")
                for k in range(8):
                    P.op("pe", lambda h, k=k, oc=oc, pq=pq: h.matmul(pq[:, 0:T], lhsT=wbig[:, k, oc * 128:(oc + 1) * 128], rhs=src[:, k, 0:T], start=(k == 0), stop=(k == 7)), r=["wbig", skey], w=[pqk])
                P.op("dve", lambda h, oc=oc, pq=pq: h.tensor_scalar(out=qTc[:, oc, 0:T], in0=pq[:, 0:T], scalar1=1.0 / 16, scalar2=None, op0=ALU.mult), r=[pqk], w=["qTc"])

        def cross_out(T, x_, xk, row0, c0=0):
            for cb in range(2):
                p_, pk = linear(oTc[:, :, c0:c0 + T], "oTc", T, wbig, "wbig", 1024 + cb * 512, 512)
                P.op("dve", lambda h, p_=p_, cb=cb: h.tensor_tensor(out=xo[0:T, cb * 512:(cb + 1) * 512], in0=p_[0:T, :], in1=x_[0:T, cb * 512:(cb + 1) * 512], op=ALU.add), r=[pk, xk], w=["xo"])
            P.dma("sp", lambda h: h.dma_start(out=xres[row0:row0 + T, :], in_=xo[0:T, :]), r=["xo"], w=[f"xres{row0 // 128}"])

        for blk in range(NT // 4):
            for st_ in range(4):
                i = blk * 4 + st_
                x_, xk = load_rows(xres[i * 128:(i + 1) * 128, :], 128, rk=[f"xres{i}"])
                rstd_of(x_[0:128, :], 128, xk)
                P.op("dve", lambda h, x_=x_: h.tensor_scalar(out=hb[:, :], in0=x_[:, :], scalar1=ssq[:, 0:1], scalar2=None, op0=ALU.mult), r=[xk, "ssq"], w=["hb"])
                for c in range(8):
                    P.op("pe", lambda h, c=c: h.transpose(out=psT[:, c * 128:(c + 1) * 128], in_=hb[:, c * 128:(c + 1) * 128], identity=ident_b[:]), r=["hb", "ident_b"], w=["psT"])
                P.op("act", lambda h, st_=st_: h.activation(out=hT4[:, :, st_ * 128:(st_ + 1) * 128], in_=psT[:, 0:1024].rearrange("p (c t) -> p c t", t=128), func=AF.Copy), r=["psT"], w=["hT4"])
            cross_q(512, hT4, "hT4")
            cross_core(0, 512)
            for st_ in range(4):
                i = blk * 4 + st_
                x_, xk = load_rows(xres[i * 128:(i + 1) * 128, :], 128, rk=[f"xres{i}"])
                cross_out(128, x_, xk, i * 128, c0=st_ * 128)

        if CL >= 1:
            x_s, xk_s = load_rows(xres[SEQ:SEQ + NS, :], NS, rk=[f"xres{SEQ // 128}"])
            norm_T(x_s[0:NS, :], xk_s, NS)
            cross_q(NS)
            for b in range(NS):
                for nt_ in range(2):
                    kt_, ktk_ = load_rows(cmk[b * 256 + nt_ * 128:b * 256 + (nt_ + 1) * 128, :], 128)
                    prep_mem_k(kt_[:], ktk_, nt_, memKT, "memKT")
                    vt_, vtk_ = load_rows(cmv[b * 256 + nt_ * 128:b * 256 + (nt_ + 1) * 128, :], 128)
                    P.op("dve", lambda h, nt_=nt_, vt_=vt_: h.tensor_copy(out=memV[:, nt_, :], in_=vt_[:]), r=[vtk_], w=["memV"])
                if CL >= 3:
                    cross_core(b, 1)
            x_s, xk_s = load_rows(xres[SEQ:SEQ + NS, :], NS, rk=[f"xres{SEQ // 128}"])
            if CL >= 4:
                cross_out(NS, x_s, xk_s, SEQ)

    if stage >= 6:
        NTOK = SEQ + NS
        hTall = big2[:, 0:8 * NTOK].rearrange("p (k t) -> p k t", t=NTOK)
        hidT = wbig[:, 0:6, 2560:3072]
        sg = q_hg
        groups = [(0, 6), (6, 6), (12, 5), (17, 5)]
        ld(mix[:], g_fin.partition_broadcast(128), ["mix"])
        for i in range(NT + 1):
            T = 128 if i < NT else NS
            x_, xk = load_rows(xres[i * 128:i * 128 + T, :], T, rk=[f"xres{i}"])
            rstd_of(x_[0:T, :], T, xk)
            P.op("dve", lambda h, T=T, x_=x_: h.tensor_scalar(out=hb[0:T, :], in0=x_[0:T, :], scalar1=ssq[0:T, 0:1], scalar2=None, op0=ALU.mult), r=[xk, "ssq"], w=["hb"])
            for c in range(8):
                P.op("pe", lambda h, c=c, T=T: h.transpose(out=psT[:, c * 128:c * 128 + T], in_=hb[0:T, c * 128:(c + 1) * 128], identity=ident_b[0:T, 0:T]), r=["hb", "ident_b"], w=["psT"])
            P.op("act", lambda h, T=T, i=i: h.activation(out=hTall[:, :, i * 128:i * 128 + T], in_=psT[:, 0:1024].rearrange("p (c t) -> p c t", t=128)[:, :, 0:T], func=AF.Copy),
                 r=["psT"], w=["hTall"])
        blocks = [(0, 512), (512, 512), (1024, 512), (1536, 512), (SEQ, NS)]
        for gi, (f0, nf) in enumerate(groups):
            load_weight(w_gate, wbig[:, :, 0:768], nf * 128, gk="ffn", key="wbig", col0=f0 * 128)
            load_weight(w_up, wbig[:, :, 768:1536], nf * 128, gk="ffn", key="wbig", col0=f0 * 128)
            load_weight(w_down[f0 * 128:(f0 + nf) * 128, :], wbig[:, 0:nf, 1536:2560], D, key="wbig", rows=nf * 128)
            for (t0, n) in blocks:
                for f in range(nf):
                    pg, pgk = nxt("A")
                    for k in range(8):
                        P.op("pe", lambda h, k=k, f=f, pg=pg, t0=t0, n=n: h.matmul(pg[:, 0:n], lhsT=wbig[:, k, f * 128:(f + 1) * 128], rhs=hTall[:, k, t0:t0 + n], start=(k == 0), stop=(k == 7)),
                             r=["wbig", "hTall"], w=[pgk])
                    pu_, puk_ = nxt("S")
                    for k in range(8):
                        P.op("pe", lambda h, k=k, f=f, pu_=pu_, t0=t0, n=n: h.matmul(pu_[:, 0:n], lhsT=wbig[:, k, 768 + f * 128:768 + (f + 1) * 128], rhs=hTall[:, k, t0:t0 + n], start=(k == 0), stop=(k == 7)),
                             r=["wbig", "hTall"], w=[puk_])
                    P.op("act", lambda h, pg=pg, n=n: h.activation(out=sg[:, 0:n], in_=pg[:, 0:n], func=AF.Silu), r=[pgk], w=["q_hg"])
                    P.op("dve", lambda h, f=f, pu_=pu_, n=n: h.tensor_tensor(out=hidT[:, f, 0:n], in0=pu_[:, 0:n], in1=sg[:, 0:n], op=ALU.mult), r=[puk_, "q_hg"], w=["hidT"])
                for st_ in range((n + 127) // 128):
                    T = min(128, n - st_ * 128)
                    i = t0 // 128 + st_
                    ydst = yp[i * 128:(i + 1) * 128, :] if i < NT else ys[0:NS, :]
                    src = xres if gi == 0 else facc
                    x_, xk = load_rows(src[i * 128:i * 128 + T, :], T, rk=[f"xres{i}" if gi == 0 else f"facc{i}"])
                    for cb in range(2):
                        p_, pk = nxt("O")
                        for f in range(nf):
                            P.op("pe", lambda h, f=f, cb=cb, p_=p_, T=T, st_=st_: h.matmul(p_[0:T, :], lhsT=hidT[:, f, st_ * 128:st_ * 128 + T], rhs=wbig[:, f, 1536 + cb * 512:1536 + (cb + 1) * 512],
                                                                                          start=(f == 0), stop=(f == nf - 1)), r=["hidT", "wbig"], w=[pk])
                        P.op("dve", lambda h, p_=p_, cb=cb, x_=x_, T=T: h.tensor_tensor(out=xo[0:T, cb * 512:(cb + 1) * 512], in0=p_[0:T, :], in1=x_[0:T, cb * 512:(cb + 1) * 512], op=ALU.add), r=[pk, xk], w=["xo"])
                    if gi < len(groups) - 1:
                        P.dma("sp", lambda h, i=i, T=T: h.dma_start(out=facc[i * 128:i * 128 + T, :], in_=xo[0:T, :]), r=["xo"], w=[f"facc{i}"])
                    else:
                        rstd_of(xo[0:T, :], T, "xo")
                        P.op("dve", lambda h, T=T: h.scalar_tensor_tensor(out=xo[0:T, :], in0=xo[0:T, :], scalar=ssq[0:T, 0:1], in1=mix[0:T, :], op0=ALU.mult, op1=ALU.mult), r=["xo", "ssq", "mix"], w=["xo"])
                        P.dma("sp", lambda h, T=T, ydst=ydst: h.dma_start(out=ydst, in_=xo[0:T, :]), r=["xo"], w=["out_yp"])

    if stage in (4, 5):
        x_, xk = load_rows(xres[0:128, :], 128, rk=["xres0"])
        P.dma("sp", lambda h: h.dma_start(out=yp[0:128, :], in_=x_[:]), r=[xk], w=["out_yp"])
    sems = {e: es.enter_context(nc.semaphore("s_" + e)) for e in ("pe", "act", "dve", "pool")}
    dsems = {q: [es.enter_context(nc.semaphore(f"d_{q}{i}")) for i in range(RING)] for q in ("sp", "pool")}
    block = es.enter_context(nc.Block())

    @block.tensor
    def _(h):
        P.emit("pe", h, sems, dsems)

    @block.scalar
    def _(h):
        P.emit("act", h, sems, dsems)

    @block.vector
    def _(h):
        P.emit("dve", h, sems, dsems)

    @block.gpsimd
    def _(h):
        P.emit("pool", h, sems, dsems)

    @block.sync
    def _(h):
        P.emit("sp", h, sems, dsems)

    es.close()
    return nc


def make_consts():
    c = {}
    c["c_ident"] = np.eye(128, dtype=np.float32)
    s = np.arange(128)
    c["c_caus"] = (s[:, None] <= s[None, :]).astype(np.float32)
    same = (s[:, None] // 64) == (s[None, :] // 64)
    c["c_u1"] = ((s[:, None] <= s[None, :]) & same).astype(np.float32)
    c["c_u2"] = (s[:, None] > s[None, :]).astype(np.float32)
    am = ((s[:, None] <= s[None, :]) & same).astype(np.float32)
    c["c_am"] = am
    ind = np.zeros((128, 2), np.float32); ind[:64, 0] = 1.0; ind[:, 1] = 1.0
    c["c_ind"] = ind
    inv = (10000.0 ** (-np.arange(0, 64, 2, dtype=np.float32) / 64)).astype(np.float32)
    pos = np.arange(SEQ, dtype=np.float32)
    ang = (pos[:, None] * inv[None, :]).astype(np.float32)
    c["c_cos"] = np.cos(ang.astype(np.float64)).astype(np.float32)
    c["c_sin"] = np.sin(ang.astype(np.float64)).astype(np.float32)
    angs = (np.float32(16384.0) * inv[None, :]).astype(np.float32)
    c["c_coss"] = np.cos(angs.astype(np.float64)).astype(np.float32)
    c["c_sins"] = np.sin(angs.astype(np.float64)).astype(np.float32)
    c["c_i4"] = np.eye(4, dtype=np.float32).reshape(1, 16)
    c["c_iota"] = (np.arange(128) % 64).astype(np.float32).reshape(128, 1)
    cpos = np.zeros((8, 4), np.float32); cneg = np.zeros((8, 4), np.float32)
    for hh in range(4):
        cpos[2 * hh, hh] = 1.0; cneg[2 * hh + 1, hh] = 1.0
    c["c_cpos"] = cpos; c["c_cneg"] = cneg
    bm = np.zeros((4, 512), np.float32)
    for hh in range(4):
        bm[hh, hh * 128:(hh + 1) * 128] = 1.0
    c["c_bm"] = bm
    return c


def core_inputs(c, I, consts, nphys, ckf=None, cvf=None):
    f = np.ascontiguousarray
    gc = lambda v: f(np.asarray(v).reshape(8, 128).T)
    m = dict(consts)
    m.update(
        ck=ckf if ckf is not None else I["cache_k"].reshape(nphys * 64, 1024), cv=cvf if cvf is not None else I["cache_v"].reshape(nphys * 64, 1024),
        xp=f(I["x_prompt"][c]), xs=f(I["x_sample"][4 * c:4 * c + 4, 0]), mem=f(I["mem_prompt"][c]),
        cmk=f(I["cache_mem_k"][0, 4 * c:4 * c + 4].reshape(NS * 256, D)), cmv=f(I["cache_mem_v"][0, 4 * c:4 * c + 4].reshape(NS * 256, D)),
        sh=f(I["state_hgrn"][0, 4 * c:4 * c + 4].reshape(NS * 512, 128)), pt=f(I["page_table"][4 * c:4 * c + 4]),
        w_in=I["w_in"][0], w_out=I["w_out"][0], w_mq=I["w_mq"][0], w_mk=I["w_mk"][0], w_mv=I["w_mv"][0], w_mo=I["w_mo"][0],
        w_gate=I["w_gate"][0], w_up=I["w_up"][0], w_down=I["w_down"][0],
        g_mix=gc(I["norm_mix"][0]), g_mq=gc(I["norm_mem_q"][0]), g_mkv=gc(I["norm_mem_kv"][0]), g_ffn=gc(I["norm_ffn"][0]),
        g_fin=f(I["norm_final"].reshape(1, D)), hg_lb=f(I["hg_lb"]), hg_on=f(I["hg_onorm"].reshape(1, 128)), da_on=f(I["da_onorm"].reshape(1, 128)),
        da_lam=f(I["da_lambda"].reshape(1, 256)),
    )
    return m


def kernel(**I):
    I = {k: np.asarray(v) for k, v in I.items()}
    nphys = I["cache_k"].shape[1]
    consts = make_consts()
    nc = build(nphys)
    ckf = I["cache_k"].reshape(nphys * 64, 1024); cvf = I["cache_v"].reshape(nphys * 64, 1024)
    in_maps = [core_inputs(c, I, consts, nphys, ckf, cvf) for c in range(8)]
    res = run_bass_kernel_spmd(nc, in_maps, core_ids=list(range(8))).results
    cat = lambda k: np.stack([r[k] for r in res])
    y_prompt = cat("yp")
    y_sample = cat("ys").reshape(32, 1, D)
    hsp = cat("hsp").reshape(1, 8, 4, 128, 128)
    kp = cat("kp").reshape(1, 8, SEQ, 4, 128)
    vp = cat("vp").reshape(1, 8, SEQ, 4, 128)
    mkp = cat("mkp").reshape(1, 8, 256, 4, 256)
    mvp = cat("mvp").reshape(1, 8, 256, 4, 256)
    hss = cat("hss").reshape(1, 32, 4, 128, 128)
    ks = cat("ks").reshape(1, 32, 1, 4, 128)
    vs = cat("vs").reshape(1, 32, 1, 4, 128)
    return (y_prompt, y_sample, hsp, kp, vp, mkp, mvp, hss, ks, vs)
```

```python
import os
import numpy as np
from contextlib import ExitStack
import concourse.bass as bass
import concourse.mybir as mybir
from concourse.bass_utils import run_bass_kernel_spmd

F32 = mybir.dt.float32
BF16 = mybir.dt.bfloat16
I32 = mybir.dt.int32
AF = mybir.ActivationFunctionType
ALU = mybir.AluOpType
AX = mybir.AxisListType

D = 1024
SEQ = 2048
NT = SEQ // 128
NS = 4
DFF = 2816
NPAGES = 128
EPS = 1e-6
LAM_INIT = 0.2
RING = 16
SL = int(os.environ.get("SL", "9"))
NPG = int(os.environ.get("NPG", "128"))
DL = int(os.environ.get("DL", "9"))
CL = int(os.environ.get("CL", "9"))


class Prog:
    ENGS = ("pe", "act", "dve", "pool", "sp")

    def __init__(self):
        self.ops = {e: [] for e in self.ENGS}
        self.res = {}
        self.dma_n = {"sp": 0, "pool": 0}

    def _deps(self, r, w):
        deps = set()
        for k in r:
            st = self.res.get(k)
            if st and st[0] is not None:
                deps.add(st[0])
        for k in w:
            st = self.res.get(k)
            if st:
                if st[0] is not None:
                    deps.add(st[0])
                deps.update(st[1])
        return deps

    def _update(self, r, w, ev):
        for k in r:
            self.res.setdefault(k, [None, []])[1].append(ev)
        for k in w:
            self.res[k] = [ev, []]

    def op(self, eng, fn, r=(), w=()):
        deps = self._deps(r, w)
        ev = ("c", eng, len(self.ops[eng]))
        self.ops[eng].append((fn, deps, ev))
        self._update(r, w, ev)

    def dma(self, q, fn, r=(), w=()):
        n = self.dma_n[q]
        self.dma_n[q] += 1
        ring, val = n % RING, 16 * (n // RING + 1)
        deps = self._deps(r, w)
        if n >= RING:
            deps.add(("d", q, ring, val - 16))
        ev = ("d", q, ring, val)
        self.ops[q].append((fn, deps, ev))
        self._update(r, w, ev)

    def emit(self, eng, h, sems, dsems):
        seen = {}
        for fn, deps, ev in self.ops[eng]:
            for d in sorted(deps):
                if d[0] == "c":
                    if d[1] == eng and eng == "pe":
                        continue
                    if seen.get(d[1], -1) >= d[2]:
                        continue
                    seen[d[1]] = d[2]
                    h.wait_ge(sems[d[1]], d[2] + 1)
                else:
                    key = (d[1], d[2])
                    if seen.get(key, 0) >= d[3]:
                        continue
                    seen[key] = d[3]
                    h.wait_ge(dsems[d[1]][d[2]], d[3])
            ins = fn(h)
            if ev[0] == "c":
                ins.then_inc(sems[eng], 1)
            else:
                ins.then_inc(dsems[ev[1]][ev[2]], 16)
        if eng == "sp":
            for e2 in self.ENGS:
                if e2 != "sp" and self.ops[e2]:
                    ncomp = sum(1 for o in self.ops[e2] if o[2][0] == "c")
                    if ncomp:
                        h.wait_ge(sems[e2], ncomp)
            for q in ("sp", "pool"):
                n = self.dma_n[q]
                for ring in range(min(n, RING)):
                    cnt = (n - ring + RING - 1) // RING
                    h.wait_ge(dsems[q][ring], 16 * cnt)


def build(nphys, stage=99):
    nc = bass.Bass("TRN2", target_bir_lowering=False)
    P = Prog()
    es = ExitStack()

    def din(name, shape, dt=F32):
        return nc.dram_tensor(name, list(shape), dt, kind="ExternalInput").ap()

    def dout(name, shape, dt=F32):
        return nc.dram_tensor(name, list(shape), dt, kind="ExternalOutput").ap()

    def dscr(name, shape, dt=F32):
        return nc.dram_tensor(name, list(shape), dt, kind="Internal").ap()

    cnt = [0]

    def sb(shape, dt=F32, name=None):
        cnt[0] += 1
        return es.enter_context(nc.sbuf_tensor(name or f"sb{cnt[0]}", list(shape), dt))

    def ps(shape, dt=F32, name=None):
        cnt[0] += 1
        return es.enter_context(nc.psum_tensor(name or f"ps{cnt[0]}", list(shape), dt))

    xp = din("xp", [SEQ, D]); xs = din("xs", [NS, D]); mem = din("mem", [256, D])
    cmk = din("cmk", [NS * 256, D]); cmv = din("cmv", [NS * 256, D])
    sh = din("sh", [NS * 4 * 128, 128]); pt = din("pt", [NS, NPAGES], I32)
    w_in = din("w_in", [D, 3584]); w_out = din("w_out", [D, D])
    w_mq = din("w_mq", [D, D]); w_mk = din("w_mk", [D, D]); w_mv = din("w_mv", [D, D]); w_mo = din("w_mo", [D, D])
    w_gate = din("w_gate", [D, DFF]); w_up = din("w_up", [D, DFF]); w_down = din("w_down", [DFF, D])
    g_mix = din("g_mix", [128, 8]); g_mq = din("g_mq", [128, 8]); g_mkv = din("g_mkv", [128, 8]); g_ffn = din("g_ffn", [128, 8])
    g_fin = din("g_fin", [1, D]); hg_lb = din("hg_lb", [2, 512]); hg_on = din("hg_on", [1, 128]); da_on = din("da_on", [1, 128])
    da_lam = din("da_lam", [1, 256])
    c_ident = din("c_ident", [128, 128]); c_caus = din("c_caus", [128, 128]); c_u1 = din("c_u1", [128, 128]); c_u2 = din("c_u2", [128, 128])
    c_am = din("c_am", [128, 128]); c_ind = din("c_ind", [128, 2])
    c_i4 = din("c_i4", [1, 16]); c_iota = din("c_iota", [128, 1]); c_cpos = din("c_cpos", [8, 4]); c_cneg = din("c_cneg", [8, 4]); c_bm = din("c_bm", [4, 512])
    ck = din("ck", [nphys * 64, 1024]); cv = din("cv", [nphys * 64, 1024])
    c_cos = din("c_cos", [SEQ, 32]); c_sin = din("c_sin", [SEQ, 32]); c_coss = din("c_coss", [1, 32]); c_sins = din("c_sins", [1, 32])

    yp = dout("yp", [SEQ, D]); ys = dout("ys", [NS, D]); hsp = dout("hsp", [512, 128]); kp = dout("kp", [SEQ, 512]); vp = dout("vp", [SEQ, 512])
    mkp = dout("mkp", [256, D]); mvp = dout("mvp", [256, D]); hss = dout("hss", [NS * 512, 128]); ks = dout("ks", [NS, 512]); vs = dout("vs", [NS, 512])

    xres = dscr("xres", [SEQ + 128, D])
    facc = dscr("facc", [SEQ + 128, D])

    ident_f = sb([128, 128]); ident_b = sb([128, 128], BF16); caus_b = sb([128, 128], BF16)
    u1 = sb([128, 128]); u2 = sb([128, 128]); am = sb([128, 128]); ind = sb([128, 2]); ones_b = sb([128, 128], BF16); ones_f = sb([128, 128])
    gcol = {k: sb([128, 8], name="gc_" + k) for k in ("mix", "mq", "mkv", "ffn")}
    lb_b = sb([128, 512]); oml_b = sb([128, 512]); lbt = sb([128, 1024], name="mix")
    hgon_b = sb([128, 128]); daon_b = sb([128, 128]); lam_t = sb([128, 256]); lamc = sb([128, 4]); neglam = sb([128, 1])
    cos_t = sb([128, NT, 32]); sin_t = sb([128, NT, 32]); coss_t = sb([128, 32]); sins_t = sb([128, 32])

    def ld(dst, src, keys_w, q="sp", keys_r=()):
        P.dma(q, lambda h, d=dst, s=src: h.dma_start(out=d, in_=s), r=keys_r, w=keys_w)

    def bc(ap, n):
        return ap.broadcast(0, n) if hasattr(ap, "broadcast") else ap

    ld(ident_f[:], c_ident, ["ident_f"]); ld(u1[:], c_u1, ["u1"]); ld(u2[:], c_u2, ["u2"]); ld(am[:], c_am, ["am"]); ld(ind[:], c_ind, ["ind"])
    ld(lbt[:, 0:128], c_caus, ["mix"])
    P.op("dve", lambda h: h.tensor_copy(out=ident_b[:], in_=ident_f[:]), r=["ident_f"], w=["ident_b"])
    P.op("dve", lambda h: h.tensor_copy(out=caus_b[:], in_=lbt[:, 0:128]), r=["mix"], w=["caus_b"])
    P.op("dve", lambda h: h.memset(ones_b[:], 1.0), w=["ones_b"])
    P.op("dve", lambda h: h.memset(ones_f[:], 1.0), w=["ones_f"])
    for k, g in (("mix", g_mix), ("mq", g_mq), ("mkv", g_mkv), ("ffn", g_ffn)):
        ld(gcol[k][:], g, ["gc_" + k])
    ld(hgon_b[:], hg_on.partition_broadcast(128), ["hgon_b"]); ld(daon_b[:], da_on.partition_broadcast(128), ["daon_b"])
    ld(lam_t[:], da_lam.partition_broadcast(128), ["lam_t"])
    ld(cos_t[:], c_cos.rearrange("(t p) f -> p t f", p=128), ["cos_t"]); ld(sin_t[:], c_sin.rearrange("(t p) f -> p t f", p=128), ["sin_t"])
    ld(coss_t[:], c_coss.partition_broadcast(128), ["coss_t"]); ld(sins_t[:], c_sins.partition_broadcast(128), ["sins_t"])
    ld(lbt[:, 0:512], hg_lb[0:1, :].partition_broadcast(128), ["mix"], keys_r=["caus_b"])
    ld(lbt[:, 512:1024], hg_lb[1:2, :].partition_broadcast(128), ["lbt2"], keys_r=["caus_b"])
    P.op("dve", lambda h: h.tensor_tensor(out=lb_b[:], in0=lbt[:, 0:512], in1=lbt[:, 512:1024], op=ALU.subtract), r=["mix", "lbt2"], w=["lb_b"])
    P.op("act", lambda h: h.activation(out=lb_b[:], in_=lb_b[:], func=AF.Sigmoid), r=["lb_b"], w=["lb_b"])
    P.op("dve", lambda h: h.tensor_scalar(out=oml_b[:], in0=lb_b[:], scalar1=-1.0, scalar2=1.0, op0=ALU.mult, op1=ALU.add), r=["lb_b"], w=["oml_b"])
    P.op("dve", lambda h: h.tensor_tensor(out=lam_t[:, 0:64], in0=lam_t[:, 0:64], in1=lam_t[:, 64:128], op=ALU.mult), r=["lam_t"], w=["lam_t"])
    P.op("dve", lambda h: h.tensor_tensor(out=lam_t[:, 128:192], in0=lam_t[:, 128:192], in1=lam_t[:, 192:256], op=ALU.mult), r=["lam_t"], w=["lam_t"])
    P.op("dve", lambda h: h.reduce_sum(out=lamc[:, 0:1], in_=lam_t[:, 0:64], axis=AX.X), r=["lam_t"], w=["lamc"])
    P.op("dve", lambda h: h.reduce_sum(out=lamc[:, 1:2], in_=lam_t[:, 128:192], axis=AX.X), r=["lam_t", "lamc"], w=["lamc"])
    P.op("act", lambda h: h.activation(out=lamc[:, 2:4], in_=lamc[:, 0:2], func=AF.Exp), r=["lamc"], w=["lamc"])
    P.op("dve", lambda h: h.tensor_tensor(out=neglam[:], in0=lamc[:, 3:4], in1=lamc[:, 2:3], op=ALU.subtract), r=["lamc"], w=["neglam"])
    P.op("dve", lambda h: h.tensor_scalar(out=neglam[:], in0=neglam[:], scalar1=-LAM_INIT, scalar2=None, op0=ALU.add), r=["neglam"], w=["neglam"])

    psA = [ps([128, 512], name=f"psA{i}") for i in range(2)]
    psT = ps([128, 1024], BF16, name="psT")
    psF = ps([128, 512], name="psF")
    psS = [ps([128, 512], name=f"psS{i}") for i in range(2)]
    psO = [ps([128, 512], name=f"psO{i}") for i in range(2)]
    rr = {"A": 0, "S": 0, "O": 0}

    def nxt(kind):
        rr[kind] ^= 1
        lst = {"A": psA, "S": psS, "O": psO}[kind]
        return lst[rr[kind]], f"ps{kind}{rr[kind]}"

    stg = [sb([128, 2, 512], name=f"stg{i}") for i in range(2)]
    stg_i = [0]

    def load_weight(wd, dst, ncols, gk=None, rows=D, key=None, col0=0):
        nk = rows // 128
        wv = wd.rearrange("(k p) n -> p k n", p=128)
        for c0 in range(0, ncols, 512):
            cw = min(512, ncols - c0)
            for k0 in range(0, nk, 2):
                kw = min(2, nk - k0)
                s = stg_i[0] & 1
                stg_i[0] += 1
                st = stg[s]
                P.dma("sp", lambda h, st=st, k0=k0, kw=kw, c0=c0, cw=cw: h.dma_start(out=st[:, 0:kw, 0:cw], in_=wv[:, k0:k0 + kw, col0 + c0:col0 + c0 + cw]),
                      w=[f"stg{s}"])
                for kq in range(kw):
                    if gk is None:
                        P.op("act", lambda h, st=st, k0=k0, kq=kq, c0=c0, cw=cw: h.activation(out=dst[:, k0 + kq, c0:c0 + cw], in_=st[:, kq, 0:cw], func=AF.Copy),
                             r=[f"stg{s}"], w=[key])
                    else:
                        g = gcol[gk]
                        P.op("act", lambda h, st=st, k0=k0, kq=kq, c0=c0, cw=cw, g=g: h.activation(out=dst[:, k0 + kq, c0:c0 + cw], in_=st[:, kq, 0:cw], func=AF.Copy,
                                                                                              scale=g[:, k0 + kq:k0 + kq + 1]),
                             r=[f"stg{s}", "gc_" + gk], w=[key])

    xt = [sb([128, D], name=f"xt{i}") for i in range(2)]
    xo = sb([128, D], name="xo"); kvtmp = xo
    hb = sb([128, D], BF16, name="hb"); junk = sb([128, D], BF16, name="junk")
    hT = sb([128, 8, 128], BF16, name="hT")
    ssq = sb([128, 4], name="ssq")
    xt_i = [0]

    def load_rows(src_ap, T, rk=()):
        s = xt_i[0] & 1
        xt_i[0] += 1
        P.dma("sp", lambda h: h.dma_start(out=xt[s][0:T, :], in_=src_ap), r=list(rk), w=[f"xt{s}"])
        return xt[s], f"xt{s}"

    def rstd_of(x_ap, T, xkey, width=D, col=0):
        P.op("act", lambda h: h.activation(out=junk[0:T, 0:width], in_=x_ap, func=AF.Square, accum_out=ssq[0:T, col:col + 1]), r=[xkey], w=["junk", "ssq"])
        P.op("act", lambda h: h.activation(out=ssq[0:T, col:col + 1], in_=ssq[0:T, col:col + 1], func=AF.Sqrt, scale=1.0 / width, bias=EPS), r=["ssq"], w=["ssq"])
        P.op("dve", lambda h: h.reciprocal(out=ssq[0:T, col:col + 1], in_=ssq[0:T, col:col + 1]), r=["ssq"], w=["ssq"])

    def transpose_to(dst, dkey, src_bf, skey, T, nchunks=8):
        for c in range(nchunks):
            P.op("pe", lambda h, c=c: h.transpose(out=psT[:, c * 128:c * 128 + T], in_=src_bf[0:T, c * 128:(c + 1) * 128], identity=ident_b[0:T, 0:T]),
                 r=[skey, "ident_b"], w=["psT"])
        P.op("act", lambda h: h.activation(out=dst[:, 0:nchunks, 0:T], in_=psT[:, 0:nchunks * 128].rearrange("p (c t) -> p c t", t=128)[:, :, 0:T], func=AF.Copy),
             r=["psT"], w=[dkey])

    def norm_T(x_ap, xkey, T):
        rstd_of(x_ap, T, xkey)
        P.op("dve", lambda h: h.tensor_scalar(out=hb[0:T, :], in0=x_ap, scalar1=ssq[0:T, 0:1], scalar2=None, op0=ALU.mult), r=[xkey, "ssq"], w=["hb"])
        transpose_to(hT, "hT", hb, "hb", T)

    def linear(lT, lkey, T, Wb, wkey, c0, cw, nk=8):
        pt_, pkey = nxt("A")
        for k in range(nk):
            P.op("pe", lambda h, k=k: h.matmul(pt_[0:T, 0:cw], lhsT=lT[:, k, 0:T], rhs=Wb[:, k, c0:c0 + cw], start=(k == 0), stop=(k == nk - 1)),
                 r=[lkey, wkey], w=[pkey])
        return pt_, pkey

    wbig = sb([128, 8, 3584], BF16, name="wbig")
    memKT = sb([128, 8, 256], BF16, name="memKT")
    memV = sb([128, 2, D], BF16, name="memV")
    kvb = sb([128, D], BF16, name="kvb")

    def prep_mem_k(k_ap, kkey, nt_, KT, ktkey):
        P.op("dve", lambda h: h.tensor_copy(out=kvb[:], in_=k_ap), r=[kkey], w=["kvb"])
        for c in range(8):
            P.op("pe", lambda h, c=c: h.transpose(out=psT[:, c * 128:(c + 1) * 128], in_=kvb[:, c * 128:(c + 1) * 128], identity=ident_b[:]),
                 r=["kvb", "ident_b"], w=["psT"])
        P.op("act", lambda h: h.activation(out=KT[:, :, nt_ * 128:(nt_ + 1) * 128], in_=psT[:].rearrange("p (c t) -> p c t", t=128), func=AF.Copy),
             r=["psT"], w=[ktkey])

    load_weight(w_mk, wbig, D, gk="mkv", key="wbig")
    load_weight(w_mv, wbig[:, :, 1024:2048], D, gk="mkv", key="wbig")
    for nt_ in range(2):
        x_, xk = load_rows(mem[nt_ * 128:(nt_ + 1) * 128, :], 128)
        norm_T(x_[:], xk, 128)
        for which in range(2):
            for cb in range(2):
                p_, pk = linear(hT, "hT", 128, wbig, "wbig", which * 1024 + cb * 512, 512)
                P.op("dve", lambda h, p_=p_, cb=cb: h.tensor_copy(out=kvtmp[:, cb * 512:(cb + 1) * 512], in_=p_[:]), r=[pk], w=["xo"])
            outd = (mkp, mvp)[which]
            P.dma("sp", lambda h, outd=outd, nt_=nt_: h.dma_start(out=outd[nt_ * 128:(nt_ + 1) * 128, :], in_=kvtmp[:]), r=["xo"], w=["out_mkv"])
            if which == 0:
                prep_mem_k(kvtmp[:], "xo", nt_, memKT, "memKT")
            else:
                P.op("dve", lambda h, nt_=nt_: h.tensor_copy(out=memV[:, nt_, :], in_=kvtmp[:]), r=["xo"], w=["memV"])


    wout_b = sb([128, 8, D], BF16, name="wout_b")
    load_weight(w_in, wbig, 3584, gk="mix", key="wbig")
    load_weight(w_out, wout_b, D, key="wout_b")
    big2 = sb([128, 16640], BF16, name="big2")
    KT = big2[:, 0:8192].rearrange("p (c t) -> p c t", t=SEQ)
    Vaug = big2[:, 8192:8192 + 8448].rearrange("p (t g f) -> p t g f", g=4, f=132)
    P.op("pool", lambda h: h.memset(big2[:, 8192:16640], 1.0), w=["Vaug"])
    S = sb([128, 4, 128], name="S"); S_b = sb([128, 4, 128], BF16, name="S_b")
    P.op("dve", lambda h: h.memset(S[:], 0.0), w=["S"])
    P.op("dve", lambda h: h.memset(S_b[:], 0.0), w=["S_b"])
    q_hg = sb([128, 512], name="q_hg"); logf = sb([128, 512], name="logf"); kk = sb([128, 512], name="kk"); tmpf = sb([128, 512], name="tmpf")
    v_hg = sb([128, 512], name="v_hg"); v_hgb = sb([128, 512], BF16, name="v_hgb"); gs = sb([128, 512], name="gs")
    qr = sb([128, 512], name="qr"); kr = sb([128, 512], name="kr"); vr = sb([128, 512], name="vr")
    qrb = sb([128, 512], BF16, name="qrb"); krb = sb([128, 512], BF16, name="krb")
    r1 = sb([128, 256], name="r1"); r2 = sb([128, 256], name="r2")
    ebuf = sb([128, 512], name="ebuf"); qe = sb([128, 512], BF16, name="qe"); kinv = sb([128, 512], BF16, name="kinv"); kdec = sb([128, 512], BF16, name="kdec")
    gam = sb([128, 8], name="gam")
    qeT = sb([128, 4, 128], BF16, name="qeT"); qeF = sb([128, 4, 128], BF16, name="qeF"); kinvT = sb([128, 4, 128], BF16, name="kinvT"); kdT0 = sb([128, 4, 64], BF16, name="kdT0")
    AT = sb([128, 128], BF16, name="AT")
    qT1 = sb([128, 4, 128], BF16, name="qT1"); qT2 = sb([128, 4, 128], BF16, name="qT2"); qTm = [qT1, qT2]
    P.op("pool", lambda h: h.memset(qT1[:], 0.0), w=["qT"])
    P.op("pool", lambda h: h.memset(qT2[:], 0.0), w=["qT"])
    Eb = [sb([128, 512], BF16, name=f"Eb{i}") for i in range(2)]
    rl = sb([128, 4], name="rl"); od = sb([128, 512], name="od"); tda = sb([128, 128], name="tda")
    mix = lbt; mixb = sb([128, D], BF16, name="mixb"); mixT = sb([128, 8, 128], BF16, name="mixT")
    daon08 = sb([128, 128], name="daon08")
    P.op("dve", lambda h: h.tensor_scalar(out=daon08[:], in0=daon_b[:], scalar1=1.0 - LAM_INIT, scalar2=None, op0=ALU.mult), r=["daon_b"], w=["daon08"])

    def rope(p_, pk, T, dst, dkey, cosap, sinap):
        pv = p_[0:T, :].rearrange("p (g two f) -> p g two f", two=2, f=32)
        dv_ = dst[0:T, :].rearrange("p (g two f) -> p g two f", two=2, f=32)
        cb_ = cosap.unsqueeze(1).to_broadcast([T, 8, 32]); sb_ = sinap.unsqueeze(1).to_broadcast([T, 8, 32])
        r1v = r1[0:T, :].rearrange("p (g f) -> p g f", f=32); r2v = r2[0:T, :].rearrange("p (g f) -> p g f", f=32)
        P.op("dve", lambda h: h.tensor_tensor(out=r1v, in0=pv[:, :, 0, :], in1=cb_, op=ALU.mult), r=[pk, "cos_t", "coss_t"], w=["r1"])
        P.op("dve", lambda h: h.tensor_tensor(out=r2v, in0=pv[:, :, 1, :], in1=sb_, op=ALU.mult), r=[pk, "sin_t", "sins_t"], w=["r2"])
        P.op("dve", lambda h: h.tensor_tensor(out=dv_[:, :, 0, :], in0=r1v, in1=r2v, op=ALU.subtract), r=["r1", "r2"], w=[dkey])
        P.op("dve", lambda h: h.tensor_tensor(out=r1v, in0=pv[:, :, 0, :], in1=sb_, op=ALU.mult), r=[pk, dkey], w=["r1"])
        P.op("dve", lambda h: h.tensor_tensor(out=r2v, in0=pv[:, :, 1, :], in1=cb_, op=ALU.mult), r=[pk, dkey], w=["r2"])
        P.op("dve", lambda h: h.tensor_tensor(out=dv_[:, :, 1, :], in0=r1v, in1=r2v, op=ALU.add), r=["r1", "r2", dkey], w=[dkey])

    def mixer_inputs(T, cosap, sinap):
        for cb in range(7):
            p_, pk = linear(hT, "hT", T, wbig, "wbig", cb * 512, 512)
            pp = p_[0:T, :]
            if cb == 0:
                P.op("act", lambda h, pp=pp: h.activation(out=q_hg[0:T, :], in_=pp, func=AF.Copy), r=[pk], w=["q_hg"])
            elif cb == 1:
                P.op("act", lambda h, pp=pp: h.activation(out=tmpf[0:T, :], in_=pp, func=AF.Sigmoid), r=[pk], w=["tmpf"])
                P.op("dve", lambda h: h.tensor_tensor(out=tmpf[0:T, :], in0=tmpf[0:T, :], in1=oml_b[0:T, :], op=ALU.mult), r=["tmpf", "oml_b"], w=["tmpf"])
                P.op("dve", lambda h: h.tensor_tensor(out=tmpf[0:T, :], in0=tmpf[0:T, :], in1=lb_b[0:T, :], op=ALU.add), r=["tmpf", "lb_b"], w=["tmpf"])
                P.op("act", lambda h: h.activation(out=logf[0:T, :], in_=tmpf[0:T, :], func=AF.Ln), r=["tmpf"], w=["logf"])
                P.op("dve", lambda h: h.tensor_scalar(out=kk[0:T, :], in0=tmpf[0:T, :], scalar1=-1.0, scalar2=1.0, op0=ALU.mult, op1=ALU.add), r=["tmpf"], w=["kk"])
            elif cb == 2:
                P.op("act", lambda h, pp=pp: h.activation(out=v_hg[0:T, :], in_=pp, func=AF.Copy), r=[pk], w=["v_hg"])
                P.op("dve", lambda h: h.tensor_copy(out=v_hgb[0:T, :], in_=v_hg[0:T, :]), r=["v_hg"], w=["v_hgb"])
            elif cb == 3:
                P.op("act", lambda h, pp=pp: h.activation(out=gs[0:T, :], in_=pp, func=AF.Silu), r=[pk], w=["gs"])
                P.op("dve", lambda h: h.tensor_tensor(out=gs[0:T, :].rearrange("p (g f) -> p g f", f=128), in0=gs[0:T, :].rearrange("p (g f) -> p g f", f=128),
                                                      in1=hgon_b[0:T, :].unsqueeze(1).to_broadcast([T, 4, 128]), op=ALU.mult), r=["gs", "hgon_b"], w=["gs"])
            elif cb == 4:
                rope(p_, pk, T, qr, "qr", cosap, sinap)
                P.op("dve", lambda h: h.tensor_scalar(out=qrb[0:T, :], in0=qr[0:T, :], scalar1=0.125, scalar2=None, op0=ALU.mult), r=["qr"], w=["qrb"])
            elif cb == 5:
                rope(p_, pk, T, kr, "kr", cosap, sinap)
                P.op("pool", lambda h: h.tensor_copy(out=krb[0:T, :], in_=kr[0:T, :]), r=["kr"], w=["krb"])
            else:
                P.op("act", lambda h, pp=pp: h.activation(out=vr[0:T, :], in_=pp, func=AF.Copy), r=[pk], w=["vr"])

    def head_norm(src_ap_fn, skeys, T, dst_col0, scale_ap_fn, scale_key, col):
        for hh in range(4):
            P.op("act", lambda h, hh=hh: h.activation(out=junk[0:T, 0:128], in_=src_ap_fn(hh), func=AF.Square, accum_out=ssq[0:T, col + hh:col + hh + 1]), r=skeys, w=["junk", "ssq"])
        P.op("act", lambda h: h.activation(out=ssq[0:T, col:col + 4], in_=ssq[0:T, col:col + 4], func=AF.Sqrt, scale=1.0 / 128, bias=EPS), r=["ssq"], w=["ssq"])
        P.op("dve", lambda h: h.reciprocal(out=ssq[0:T, col:col + 4], in_=ssq[0:T, col:col + 4]), r=["ssq"], w=["ssq"])
        for hh in range(4):
            P.op("dve", lambda h, hh=hh: h.scalar_tensor_tensor(out=mix[0:T, dst_col0 + hh * 128:dst_col0 + (hh + 1) * 128], in0=src_ap_fn(hh), scalar=ssq[0:T, col + hh:col + hh + 1],
                                                                in1=scale_ap_fn(hh), op0=ALU.mult, op1=ALU.mult), r=skeys + ["ssq", scale_key], w=["mix"])

    def out_proj(T, x_ap, xkey, row0):
        P.op("act", lambda h: h.activation(out=mixb[0:T, :], in_=mix[0:T, :], func=AF.Copy), r=["mix"], w=["mixb"])
        transpose_to(mixT, "mixT", mixb, "mixb", T)
        for cb in range(2):
            p_, pk = linear(mixT, "mixT", T, wout_b, "wout_b", cb * 512, 512)
            P.op("dve", lambda h, p_=p_, cb=cb: h.tensor_tensor(out=xo[0:T, cb * 512:(cb + 1) * 512], in0=p_[0:T, :], in1=x_ap[:, cb * 512:(cb + 1) * 512], op=ALU.add), r=[pk, xkey], w=["xo"])
        P.dma("sp", lambda h: h.dma_start(out=xres[row0:row0 + T, :], in_=xo[0:T, :]), r=["xo"], w=[f"xres{row0 // 128}"])

    ssq8 = 0
    for i in range(NT if stage >= 1 else 0):
        x_, xk = load_rows(xp[i * 128:(i + 1) * 128, :], 128)
        norm_T(x_[:], xk, 128)
        mixer_inputs(128, cos_t[:, i, :], sin_t[:, i, :])
        P.dma("sp", lambda h, i=i: h.dma_start(out=kp[i * 128:(i + 1) * 128, :], in_=kr[:]), r=["kr"], w=["out_kp"])
        P.dma("sp", lambda h, i=i: h.dma_start(out=vp[i * 128:(i + 1) * 128, :], in_=vr[:]), r=["vr"], w=["out_vp"])
        P.op("pool", lambda h, i=i: h.tensor_copy(out=Vaug[:, i, :, 0:128], in_=vr[:].rearrange("p (g f) -> p g f", f=128)), r=["vr"], w=["Vaug"])
        if stage < 2:
            continue
        pb, pbk = nxt("A")
        P.op("pe", lambda h, pb=pb: h.matmul(pb[:], lhsT=u1[:], rhs=logf[:], start=True, stop=True), r=["u1", "logf"], w=[pbk])
        P.op("act", lambda h, pb=pb: h.activation(out=ebuf[:], in_=pb[:], func=AF.Exp), r=[pbk], w=["ebuf"])
        P.op("dve", lambda h: h.tensor_tensor(out=qe[:], in0=q_hg[:], in1=ebuf[:], op=ALU.mult), r=["q_hg", "ebuf"], w=["qe"])
        P.op("act", lambda h, pb=pb: h.activation(out=ebuf[:], in_=pb[:], func=AF.Exp, scale=-1.0), r=[pbk, "qe"], w=["ebuf"])
        P.op("dve", lambda h: h.tensor_tensor(out=kinv[:], in0=kk[:], in1=ebuf[:], op=ALU.mult), r=["kk", "ebuf"], w=["kinv"])
        pd, pdk = nxt("A")
        P.op("pe", lambda h, pd=pd: h.matmul(pd[:], lhsT=u2[:], rhs=logf[:], start=True, stop=True), r=["u2", "logf"], w=[pdk])
        P.op("act", lambda h, pd=pd: h.activation(out=ebuf[:], in_=pd[:], func=AF.Exp), r=[pdk, "kinv"], w=["ebuf"])
        P.op("dve", lambda h: h.tensor_tensor(out=kdec[:], in0=kk[:], in1=ebuf[:], op=ALU.mult), r=["kk", "ebuf"], w=["kdec"])
        for hh in range(4):
            P.op("pe", lambda h, hh=hh: h.matmul(psF[:, hh * 2:hh * 2 + 2], lhsT=logf[:, hh * 128:(hh + 1) * 128], rhs=ind[:], start=True, stop=True), r=["logf", "ind"], w=["psF"])
        P.op("act", lambda h: h.activation(out=gam[:], in_=psF[:, 0:8], func=AF.Exp), r=["psF"], w=["gam"])
        transpose_to(qeT, "qeT", qe, "qe", 128, nchunks=4)
        transpose_to(kinvT, "kinvT", kinv, "kinv", 128, nchunks=4)
        for hh in range(4):
            P.op("dve", lambda h, hh=hh: h.tensor_copy(out=qeF[:, hh, 0:64], in_=qeT[:, hh, 0:64]), r=["qeT"], w=["qeF"])
            P.op("dve", lambda h, hh=hh: h.tensor_scalar(out=qeF[:, hh, 64:128], in0=qeT[:, hh, 64:128], scalar1=gam[:, 2 * hh:2 * hh + 1], scalar2=None, op0=ALU.mult), r=["qeT", "gam"], w=["qeF"])
            P.op("dve", lambda h, hh=hh: h.tensor_scalar(out=kdT0[:, hh, :], in0=kinvT[:, hh, 0:64], scalar1=gam[:, 2 * hh:2 * hh + 1], scalar2=None, op0=ALU.mult), r=["kinvT", "gam"], w=["kdT0"])
        po, pok = nxt("O")
        for hh in range(4):
            pa, pak = nxt("S")
            P.op("pe", lambda h, hh=hh, pa=pa: h.matmul(pa[:, 0:128], lhsT=kinvT[:, hh, :], rhs=qeT[:, hh, :], start=True, stop=True), r=["kinvT", "qeT"], w=[pak])
            P.op("pe", lambda h, hh=hh, pa=pa: h.matmul(pa[0:64, 128:192], lhsT=kdT0[:, hh, :], rhs=qeT[:, hh, 64:128], start=True, stop=True), r=["kdT0", "qeT"], w=[pak])
            P.op("dve", lambda h, pa=pa: h.tensor_tensor(out=AT[:], in0=pa[:, 0:128], in1=am[:], op=ALU.mult), r=[pak, "am"], w=["AT"])
            P.op("dve", lambda h, pa=pa: h.tensor_copy(out=AT[0:64, 64:128], in_=pa[0:64, 128:192]), r=[pak, "AT"], w=["AT"])
            P.op("pe", lambda h, hh=hh, po=po: h.matmul(po[:, hh * 128:(hh + 1) * 128], lhsT=AT[:], rhs=v_hgb[:, hh * 128:(hh + 1) * 128], start=True, stop=False), r=["AT", "v_hgb"], w=[pok])
            P.op("pe", lambda h, hh=hh, po=po: h.matmul(po[:, hh * 128:(hh + 1) * 128], lhsT=qeF[:, hh, :], rhs=S_b[:, hh, :], start=False, stop=True), r=["qeF", "S_b"], w=[pok])
            pu, puk = nxt("A")
            P.op("pe", lambda h, hh=hh, pu=pu: h.matmul(pu[:, 0:128], lhsT=kdec[:, hh * 128:(hh + 1) * 128], rhs=v_hgb[:, hh * 128:(hh + 1) * 128], start=True, stop=True), r=["kdec", "v_hgb"], w=[puk])
            P.op("dve", lambda h, hh=hh, pu=pu: h.scalar_tensor_tensor(out=S[:, hh, :], in0=S[:, hh, :], scalar=gam[:, 2 * hh + 1:2 * hh + 2], in1=pu[:, 0:128], op0=ALU.mult, op1=ALU.add),
                 r=["S", "gam", puk], w=["S"])
        P.op("act", lambda h: h.activation(out=S_b[:], in_=S[:], func=AF.Copy), r=["S"], w=["S_b"])
        head_norm(lambda hh, po=po: po[:, hh * 128:(hh + 1) * 128], [pok], 128, 0, lambda hh: gs[:, hh * 128:(hh + 1) * 128], "gs", 0)
        if i == NT - 1:
            for hh in range(4):
                P.dma("sp", lambda h, hh=hh: h.dma_start(out=hsp[hh * 128:(hh + 1) * 128, :], in_=S[:, hh, :]), r=["S"], w=["out_hsp"])
        if stage < 3:
            continue
        for c_ in range(4):
            P.op("pe", lambda h, c_=c_: h.transpose(out=psT[:, c_ * 128:(c_ + 1) * 128], in_=qrb[:, c_ * 128:(c_ + 1) * 128], identity=ident_b[:]), r=["qrb", "ident_b"], w=["psT"])
        P.op("act", lambda h: h.activation(out=qT1[0:64, :, :], in_=psT[0:64, 0:512].rearrange("p (c t) -> p c t", t=128), func=AF.Copy), r=["psT"], w=["qT"])
        P.op("act", lambda h: h.activation(out=qT2[64:128, :, :], in_=psT[64:128, 0:512].rearrange("p (c t) -> p c t", t=128), func=AF.Copy), r=["psT", "qT"], w=["qT"])
        for c_ in range(4):
            P.op("pe", lambda h, c_=c_: h.transpose(out=psT[:, c_ * 128:(c_ + 1) * 128], in_=krb[:, c_ * 128:(c_ + 1) * 128], identity=ident_b[:]), r=["krb", "ident_b"], w=["psT"])
        P.op("act", lambda h, i=i: h.activation(out=KT[:, :, i * 128:(i + 1) * 128], in_=psT[:, 0:512].rearrange("p (c t) -> p c t", t=128), func=AF.Copy), r=["psT"], w=["KT"])
        for hp in range(2):
            for j in range(i + 1):
                pS, pSk = nxt("S")
                for hl in range(2):
                    for m_ in range(2):
                        P.op("pe", lambda h, hp=hp, hl=hl, j=j, m_=m_, pS=pS: h.matmul(pS[:, (hl * 2 + m_) * 128:(hl * 2 + m_ + 1) * 128], lhsT=KT[:, 2 * hp + hl, j * 128:(j + 1) * 128],
                                                                                      rhs=qTm[m_][:, 2 * hp + hl, :], start=True, stop=True), r=["KT", "qT"], w=[pSk])
                E_ = Eb[j & 1]; ek = f"Eb{j & 1}"
                P.op("act", lambda h, pS=pS, E_=E_: h.activation(out=E_[:], in_=pS[:, 0:512], func=AF.Exp), r=[pSk], w=[ek])
                if j == i:
                    P.op("dve", lambda h, E_=E_: h.tensor_tensor(out=E_[:].rearrange("p (m q) -> p m q", m=4), in0=E_[:].rearrange("p (m q) -> p m q", m=4),
                                                               in1=caus_b[:].unsqueeze(1).to_broadcast([128, 4, 128]), op=ALU.mult), r=[ek, "caus_b"], w=[ek])
                for hl in range(2):
                    for m_ in range(2):
                        acc_ = (psO, psA)[hl][m_]; acck_ = ("psO", "psA")[hl] + str(m_)
                        P.op("pe", lambda h, hp=hp, hl=hl, j=j, m_=m_, E_=E_, acc_=acc_: h.matmul(acc_[:, 0:130], lhsT=E_[:, (hl * 2 + m_) * 128:(hl * 2 + m_ + 1) * 128],
                                                                                      rhs=Vaug[:, j, 2 * hp + hl, 0:130], start=(j == 0), stop=(j == i)), r=[ek, "Vaug"], w=[acck_])
            for hl in range(2):
                hh = 2 * hp + hl
                a0_, a1_ = (psO, psA)[hl]
                k0_, k1_ = ("psO0", "psO1") if hl == 0 else ("psA0", "psA1")
                P.op("dve", lambda h, a0_=a0_: h.reciprocal(out=rl[:, 0:1], in_=a0_[:, 128:129]), r=[k0_, "rl"], w=["rl"])
                P.op("dve", lambda h, a1_=a1_: h.reciprocal(out=rl[:, 1:2], in_=a1_[:, 128:129]), r=[k1_, "rl"], w=["rl"])
                P.op("dve", lambda h: h.tensor_tensor(out=rl[:, 2:3], in0=rl[:, 1:2], in1=neglam[:], op=ALU.mult), r=["rl", "neglam"], w=["rl"])
                P.op("dve", lambda h, a0_=a0_: h.tensor_scalar(out=tda[:], in0=a0_[:, 0:128], scalar1=rl[:, 0:1], scalar2=None, op0=ALU.mult), r=[k0_, "rl"], w=["tda"])
                P.op("dve", lambda h, hh=hh, a1_=a1_: h.scalar_tensor_tensor(out=od[:, hh * 128:(hh + 1) * 128], in0=a1_[:, 0:128], scalar=rl[:, 2:3], in1=tda[:], op0=ALU.mult, op1=ALU.add),
                     r=[k1_, "rl", "tda"], w=["od"])
        head_norm(lambda hh: od[:, hh * 128:(hh + 1) * 128], ["od"], 128, 512, lambda hh: daon08[:], "daon08", 0)
        if stage < 4:
            continue
        out_proj(128, x_[:], xk, i * 128)


    if stage >= 4:
        x_, xk = load_rows(xs[0:NS, :], NS)
        norm_T(x_[0:NS, :], xk, NS)
        mixer_inputs(NS, coss_t[0:NS, :], sins_t[0:NS, :])
        P.dma("sp", lambda h: h.dma_start(out=ks[0:NS, :], in_=kr[0:NS, :]), r=["kr"], w=["out_ks"])
        P.dma("sp", lambda h: h.dma_start(out=vs[0:NS, :], in_=vr[0:NS, :]), r=["vr"], w=["out_vs"])
        smalls = sb([128, 64], name="smalls"); i4b = sb([128, 16], name="i4b"); qmask = sb([128, 64], name="qmask"); colsT = sb([128, 48], name="colsT")
        iota_c = sb([128, 1], name="iota_c"); cpos = sb([8, 4], name="cpos"); cneg = sb([8, 4], name="cneg"); Cm = sb([8, 4], name="Cm")
        ld(i4b[:], c_i4.partition_broadcast(128), ["i4b"]); ld(iota_c[:], c_iota, ["iota_c"]); ld(cpos[:], c_cpos, ["cpos"]); ld(cneg[:], c_cneg, ["cneg"])
        P.op("dve", lambda h: h.scalar_tensor_tensor(out=Cm[:], in0=cneg[:], scalar=neglam[0:8, 0:1], in1=cpos[:], op0=ALU.mult, op1=ALU.add), r=["cneg", "cpos", "neglam"], w=["Cm"])
        Ss = big2[:, 0:4096].bitcast(F32).rearrange("p (g v) -> p g v", v=128)
        s_all = big2[:, 4096:6144].bitcast(F32)
        Kt = [big2[:, 6144 + r_ * 2048:6144 + (r_ + 1) * 2048].bitcast(F32) for r_ in range(2)]
        Vt = [big2[:, 10240 + r_ * 2048:10240 + (r_ + 1) * 2048].bitcast(F32) for r_ in range(2)]
        prodt = [big2[:, 0:2048].bitcast(F32)]
        selT = kvb[:].bitcast(F32)[0:4, :].rearrange("p (b m) -> p b m", m=128)
        P.op("dve", lambda h: h.tensor_copy(out=selT, in_=ident_f[0:4, 0:4].unsqueeze(2).to_broadcast([4, 4, 128])), r=["ident_f", "KT", "Vaug"], w=["kvb"])
        P.dma("sp", lambda h: h.dma_start(out=Ss, in_=sh.rearrange("(g p) v -> p g v", p=128)), r=["KT", "Vaug"], w=["Ss"])
        if SL >= 1:
            P.op("act", lambda h: h.activation(out=tmpf[0:4, :], in_=logf[0:4, :], func=AF.Exp), r=["logf"], w=["tmpf"])
            for wi, (src, skey) in enumerate(((tmpf, "tmpf"), (kk, "kk"), (q_hg, "q_hg"))):
                for hh in range(4):
                    P.op("pe", lambda h, wi=wi, hh=hh, src=src: h.transpose(out=psF[:, (wi * 4 + hh) * 4:(wi * 4 + hh) * 4 + 4], in_=src[0:4, hh * 128:(hh + 1) * 128], identity=ident_f[0:4, 0:4]),
                         r=[skey, "ident_f"], w=["psF"])
            P.op("act", lambda h: h.activation(out=colsT[:], in_=psF[:, 0:48], func=AF.Copy), r=["psF"], w=["colsT"])
            for b in range(NS):
                pvb, pvbk = nxt("A")
                P.op("pe", lambda h, b=b, pvb=pvb: h.matmul(pvb[:, 0:512], lhsT=selT[:, b, :], rhs=v_hg[0:4, :], start=True, stop=True), r=["kvb", "v_hg"], w=[pvbk])
                for hh in range(4):
                    g_ = b * 4 + hh
                    P.op("dve", lambda h, hh=hh, b=b, pvb=pvb: h.tensor_scalar(out=tda[:], in0=pvb[:, hh * 128:(hh + 1) * 128], scalar1=colsT[:, 16 + hh * 4 + b:16 + hh * 4 + b + 1], scalar2=None, op0=ALU.mult),
                         r=[pvbk, "colsT"], w=["tda"])
                    P.op("dve", lambda h, hh=hh, b=b, g_=g_: h.scalar_tensor_tensor(out=Ss[:, g_, :], in0=Ss[:, g_, :], scalar=colsT[:, hh * 4 + b:hh * 4 + b + 1], in1=tda[:], op0=ALU.mult, op1=ALU.add),
                         r=["Ss", "colsT", "tda"], w=["Ss"])
            P.dma("sp", lambda h: h.dma_start(out=hss.rearrange("(g p) v -> p g v", p=128), in_=Ss), r=["Ss"], w=["out_hss"])
            qm4 = qmask[:].rearrange("p (a b c) -> p a b c", b=4, c=4)
            for hh in range(4):
                P.op("dve", lambda h, hh=hh: h.tensor_tensor(out=qm4[:, hh, :, :], in0=colsT[:, 32 + hh * 4:32 + hh * 4 + 4].unsqueeze(1).to_broadcast([128, 4, 4]),
                                                             in1=i4b[:].rearrange("p (a b) -> p a b", b=4), op=ALU.mult), r=["colsT", "i4b"], w=["qmask"])
            po, pok = nxt("O")
            for hh in range(4):
                for b in range(NS):
                    P.op("pe", lambda h, hh=hh, b=b, po=po: h.matmul(po[0:4, hh * 128:(hh + 1) * 128], lhsT=qm4[:, hh, b, :], rhs=Ss[:, b * 4 + hh, :], start=(b == 0), stop=(b == NS - 1)),
                         r=["qmask", "Ss"], w=[pok])
            head_norm(lambda hh, po=po: po[0:4, hh * 128:(hh + 1) * 128], [pok], NS, 0, lambda hh: gs[0:4, hh * 128:(hh + 1) * 128], "gs", 0)

        if SL >= 2:
            pti_t = sb([128, 512], I32, name="pti_t"); pti = pti_t[:]
            ptv = pt.rearrange("(o b) (g two) -> o two b g", o=1, two=2)
            for h2 in range(2):
                for b_ in range(NS):
                    P.dma("sp", lambda h, h2=h2, b_=b_: h.dma_start(out=pti[h2 * 64:(h2 + 1) * 64, b_ * 64:(b_ + 1) * 64], in_=ptv[:, h2, b_, :].partition_broadcast(64), allow_slow_non_contiguous=True), w=["ebuf"])
            P.op("dve", lambda h: h.tensor_copy(out=tmpf[:, 0:256], in_=pti[:, 0:256]), r=["ebuf"], w=["tmpf"])
            P.op("dve", lambda h: h.tensor_scalar(out=tmpf[:, 0:256], in0=tmpf[:, 0:256], scalar1=64.0, scalar2=iota_c[:, 0:1], op0=ALU.mult, op1=ALU.add), r=["tmpf", "iota_c"], w=["tmpf"])
            P.op("dve", lambda h: h.tensor_copy(out=pti[:, 0:256], in_=tmpf[:, 0:256]), r=["tmpf"], w=["ebuf"])
            qs = logf
            P.op("dve", lambda h: h.tensor_scalar(out=qs[0:4, :], in0=qr[0:4, :], scalar1=0.125, scalar2=None, op0=ALU.mult), r=["qr", "logf"], w=["logf"])
            P.op("dve", lambda h: h.tensor_tensor(out=kk[0:4, :], in0=qs[0:4, :], in1=kr[0:4, :], op=ALU.mult), r=["logf", "kr", "kk"], w=["kk"])
            P.op("dve", lambda h: h.tensor_reduce(out=smalls[0:4, 0:8], in_=kk[0:4, :].rearrange("p (m d) -> p m d", d=64), axis=AX.X, op=ALU.add), r=["kk"], w=["smalls"])
            P.op("act", lambda h: h.activation(out=smalls[0:4, 8:16], in_=smalls[0:4, 0:8], func=AF.Exp), r=["smalls"], w=["smalls"])
            bm_t = v_hg
            P.dma("sp", lambda h: h.dma_start(out=bm_t[0:4, :], in_=c_bm), r=["v_hg"], w=["v_hg"])
        if SL == 2:
            P.dma("sp", lambda h: h.dma_start(out=kp[0:128, :], in_=tmpf[:]), r=["tmpf"], w=["out_kp"])
            P.dma("sp", lambda h: h.dma_start(out=vp[0:128, :], in_=ebuf[:]), r=["ebuf"], w=["out_vp"])
        if SL >= 3:
            s_alls = [s_all, big2[:, 2048:4096].bitcast(F32)]
            sm = [smalls, sb([128, 64], name="smalls2")]
            pov = povk = None
            for b in range(NS + 1):
                kb, vb = b, b - 1
                if kb < NS:
                    sa = s_alls[kb % 2]; sak = f"s_all{kb % 2}"; smk = sm[kb % 2]; smkk = f"sm{kb % 2}"
                    pq, pqk = nxt("A")
                    P.op("pe", lambda h, kb=kb, pq=pq: h.matmul(pq[:, 0:512], lhsT=selT[:, kb, :], rhs=qs[0:4, :], start=True, stop=True), r=["kvb", "logf"], w=[pqk])
                    P.op("act", lambda h, pq=pq: h.activation(out=q_hg[:], in_=pq[:, 0:512], func=AF.Copy), r=[pqk], w=["q_hg"])
                if vb >= 0:
                    sv = s_alls[vb % 2]; svk = f"s_all{vb % 2}"; smv = sm[vb % 2]; smvk = f"sm{vb % 2}"
                    pov, povk = nxt("O")
                for g in range(NPAGES // 2):
                    r_ = g % 2
                    if kb < NS:
                        P.dma("pool", lambda h, kb=kb, g=g, r_=r_: h.indirect_dma_start(out=Kt[r_], out_offset=None, in_=ck, in_offset=bass.IndirectOffsetOnAxis(ap=pti[:, kb * 64 + g:kb * 64 + g + 1], axis=0)),
                              r=["ebuf"], w=[f"Kt{r_}"])
                        P.op("dve", lambda h, r_=r_: h.tensor_tensor(out=prodt[0].rearrange("p (t f) -> p t f", t=2), in0=Kt[r_].rearrange("p (t f) -> p t f", t=2),
                                                                 in1=q_hg[:].unsqueeze(1).to_broadcast([128, 2, 512]), op=ALU.mult), r=[f"Kt{r_}", "q_hg"], w=["prodt0", "Ss"])
                        P.op("dve", lambda h, g=g, sa=sa: h.tensor_reduce(out=sa[:, g * 16:(g + 1) * 16], in_=prodt[0].rearrange("p (m d) -> p m d", d=64), axis=AX.X, op=ALU.add), r=["prodt0"], w=[sak])
                    if vb >= 0:
                        P.dma("pool", lambda h, vb=vb, g=g, r_=r_: h.indirect_dma_start(out=Vt[r_], out_offset=None, in_=cv, in_offset=bass.IndirectOffsetOnAxis(ap=pti[:, vb * 64 + g:vb * 64 + g + 1], axis=0)),
                              r=["ebuf"], w=[f"Vt{r_}"])
                        for t2 in range(2):
                            P.op("pe", lambda h, g=g, t2=t2, r_=r_, pov=pov, sv=sv: h.matmul(pov[0:8, 0:512], lhsT=sv[:, g * 16 + t2 * 8:g * 16 + t2 * 8 + 8], rhs=Vt[r_][:, t2 * 512:(t2 + 1) * 512],
                                                                                        start=(g == 0 and t2 == 0), stop=False), r=[svk, f"Vt{r_}"], w=[povk])
                if vb >= 0:
                    P.op("pe", lambda h, pov=pov, smv=smv: h.matmul(pov[0:8, 0:512], lhsT=smv[0:4, 16:24], rhs=vr[0:4, :], start=False, stop=True), r=[smvk, "vr"], w=[povk])
                    pol, polk = nxt("S")
                    P.op("pe", lambda h, pol=pol, smv=smv: h.matmul(pol[0:8, 0:1], lhsT=smv[:, 24:32], rhs=ones_f[:, 0:1], start=True, stop=False), r=[smvk, "ones_f"], w=[polk])
                    P.op("pe", lambda h, pol=pol, smv=smv: h.matmul(pol[0:8, 0:1], lhsT=smv[0:4, 16:24], rhs=ones_f[0:4, 0:1], start=False, stop=True), r=[smvk, "ones_f"], w=[polk])
                    P.op("dve", lambda h, pol=pol, smv=smv: h.reciprocal(out=smv[0:8, 32:33], in_=pol[0:8, 0:1]), r=[polk], w=[smvk])
                    P.op("dve", lambda h, pov=pov, smv=smv: h.tensor_scalar(out=kk[0:8, :], in0=pov[0:8, 0:512], scalar1=smv[0:8, 32:33], scalar2=None, op0=ALU.mult), r=[povk, smvk], w=["kk"])
                    pfin, pfink = nxt("A")
                    P.op("pe", lambda h, pfin=pfin: h.matmul(pfin[0:4, 0:512], lhsT=Cm[0:8, 0:4], rhs=kk[0:8, :], start=True, stop=True), r=["Cm", "kk"], w=[pfink])
                    P.op("dve", lambda h, pfin=pfin: h.tensor_tensor(out=tmpf[0:4, :], in0=pfin[0:4, 0:512], in1=bm_t[0:4, :], op=ALU.mult), r=[pfink, "v_hg"], w=["tmpf"])
                    P.op("pe", lambda h, vb=vb: h.matmul(psF[0:4, 0:512], lhsT=i4b[0:4, vb * 4:(vb + 1) * 4], rhs=tmpf[0:4, :], start=(vb == 0), stop=(vb == NS - 1)), r=["i4b", "tmpf"], w=["psF"])
                if kb < NS:
                    P.op("act", lambda h, sa=sa: h.activation(out=sa, in_=sa, func=AF.Exp), r=[sak], w=[sak])
                    P.op("dve", lambda h, sa=sa, smk=smk: h.tensor_reduce(out=smk[:, 24:32], in_=sa.rearrange("p (j m) -> p m j", m=8), axis=AX.X, op=ALU.add), r=[sak], w=[smkk])
                    P.op("dve", lambda h, kb=kb, smk=smk: h.tensor_scalar(out=smk[0:4, 16:24], in0=smalls[0:4, 8:16], scalar1=ident_f[0:4, kb:kb + 1], scalar2=None, op0=ALU.mult), r=["smalls", "ident_f", smkk], w=[smkk])
        if SL >= 4:
            head_norm(lambda hh: psF[0:4, hh * 128:(hh + 1) * 128], ["psF"], NS, 512, lambda hh: daon08[0:4, :], "daon08", 0)
        x_, xk = load_rows(xs[0:NS, :], NS)
        out_proj(NS, x_[0:NS, :], xk, SEQ)

    if stage >= 5:
        load_weight(w_mq, wbig, D, gk="mq", key="wbig")
        load_weight(w_mo, wbig[:, :, 1024:2048], D, key="wbig")
        qTc = big2[:, 0:4096].rearrange("p (c t) -> p c t", t=512); oTc = big2[:, 4096:8192].rearrange("p (c t) -> p c t", t=512)
        ETc = big2[:, 8192:9216].rearrange("p (c t) -> p c t", t=512); hT4 = big2[:, 9216:13312].rearrange("p (c t) -> p c t", t=512); rlc = q_hg
        def cross_core(c0, n):
            for hh in range(4):
                for nt_ in range(2):
                    pS, pSk = nxt("S")
                    for c_ in range(2):
                        P.op("pe", lambda h, hh=hh, nt_=nt_, c_=c_, pS=pS: h.matmul(pS[:, 0:n], lhsT=memKT[:, hh * 2 + c_, nt_ * 128:(nt_ + 1) * 128], rhs=qTc[:, hh * 2 + c_, c0:c0 + n],
                                                                                  start=(c_ == 0), stop=(c_ == 1)), r=["memKT", "qTc"], w=[pSk])
                    P.op("act", lambda h, nt_=nt_, pS=pS: h.activation(out=ETc[:, nt_, 0:n], in_=pS[:, 0:n], func=AF.Exp), r=[pSk], w=["ETc"])
                for nt_ in range(2):
                    P.op("pe", lambda h, nt_=nt_: h.matmul(psF[:, 0:n], lhsT=ones_b[:], rhs=ETc[:, nt_, 0:n], start=(nt_ == 0), stop=(nt_ == 1)), r=["ones_b", "ETc"], w=["psF"])
                P.op("dve", lambda h: h.reciprocal(out=rlc[:, 0:n], in_=psF[:, 0:n]), r=["psF"], w=["q_hg"])
                for c_ in range(2):
                    po_, pok_ = nxt("O")
                    for nt_ in range(2):
                        P.op("pe", lambda h, hh=hh, nt_=nt_, c_=c_, po_=po_: h.matmul(po_[:, 0:n], lhsT=memV[:, nt_, hh * 256 + c_ * 128:hh * 256 + (c_ + 1) * 128], rhs=ETc[:, nt_, 0:n],
                                                                                    start=(nt_ == 0), stop=(nt_ == 1)), r=["memV", "ETc"], w=[pok_])
                    P.op("dve", lambda h, hh=hh, c_=c_, po_=po_: h.tensor_tensor(out=oTc[:, hh * 2 + c_, c0:c0 + n], in0=po_[:, 0:n], in1=rlc[:, 0:n], op=ALU.mult), r=[pok_, "q_hg"], w=["oTc"])

        def cross_q(T, src=None, skey="hT"):
            src = hT if src is None else src
            for oc in range(8):
                pq, pqk = nxt("A")
                for k in range(8):
                    P.op("pe", lambda h, k=k, oc=oc, pq=pq: h.matmul(pq[:, 0:T], lhsT=wbig[:, k, oc * 128:(oc + 1) * 128], rhs=src[:, k, 0:T], start=(k == 0), stop=(k == 7)), r=["wbig", skey], w=[pqk])
                P.op("dve", lambda h, oc=oc, pq=pq: h.tensor_scalar(out=qTc[:, oc, 0:T], in0=pq[:, 0:T], scalar1=1.0 / 16, scalar2=None, op0=ALU.mult), r=[pqk], w=["qTc"])

        def cross_out(T, x_, xk, row0, c0=0):
            for cb in range(2):
                p_, pk = linear(oTc[:, :, c0:c0 + T], "oTc", T, wbig, "wbig", 1024 + cb * 512, 512)
                P.op("dve", lambda h, p_=p_, cb=cb: h.tensor_tensor(out=xo[0:T, cb * 512:(cb + 1) * 512], in0=p_[0:T, :], in1=x_[0:T, cb * 512:(cb + 1) * 512], op=ALU.add), r=[pk, xk], w=["xo"])
            P.dma("sp", lambda h: h.dma_start(out=xres[row0:row0 + T, :], in_=xo[0:T, :]), r=["xo"], w=[f"xres{row0 // 128}"])

        for blk in range(NT // 4):
            for st_ in range(4):
                i = blk * 4 + st_
                x_, xk = load_rows(xres[i * 128:(i + 1) * 128, :], 128, rk=[f"xres{i}"])
                rstd_of(x_[0:128, :], 128, xk)
                P.op("dve", lambda h, x_=x_: h.tensor_scalar(out=hb[:, :], in0=x_[:, :], scalar1=ssq[:, 0:1], scalar2=None, op0=ALU.mult), r=[xk, "ssq"], w=["hb"])
                for c in range(8):
                    P.op("pe", lambda h, c=c: h.transpose(out=psT[:, c * 128:(c + 1) * 128], in_=hb[:, c * 128:(c + 1) * 128], identity=ident_b[:]), r=["hb", "ident_b"], w=["psT"])
                P.op("act", lambda h, st_=st_: h.activation(out=hT4[:, :, st_ * 128:(st_ + 1) * 128], in_=psT[:, 0:1024].rearrange("p (c t) -> p c t", t=128), func=AF.Copy), r=["psT"], w=["hT4"])
            cross_q(512, hT4, "hT4")
            cross_core(0, 512)
            for st_ in range(4):
                i = blk * 4 + st_
                x_, xk = load_rows(xres[i * 128:(i + 1) * 128, :], 128, rk=[f"xres{i}"])
                cross_out(128, x_, xk, i * 128, c0=st_ * 128)

        if CL >= 1:
            x_s, xk_s = load_rows(xres[SEQ:SEQ + NS, :], NS, rk=[f"xres{SEQ // 128}"])
            norm_T(x_s[0:NS, :], xk_s, NS)
            cross_q(NS)
            for b in range(NS):
                for nt_ in range(2):
                    kt_, ktk_ = load_rows(cmk[b * 256 + nt_ * 128:b * 256 + (nt_ + 1) * 128, :], 128)
                    prep_mem_k(kt_[:], ktk_, nt_, memKT, "memKT")
                    vt_, vtk_ = load_rows(cmv[b * 256 + nt_ * 128:b * 256 + (nt_ + 1) * 128, :], 128)
                    P.op("dve", lambda h, nt_=nt_, vt_=vt_: h.tensor_copy(out=memV[:, nt_, :], in_=vt_[:]), r=[vtk_], w=["memV"])
                if CL >= 3:
                    cross_core(b, 1)
            x_s, xk_s = load_rows(xres[SEQ:SEQ + NS, :], NS, rk=[f"xres{SEQ // 128}"])
            if CL >= 4:
                cross_out(NS, x_s, xk_s, SEQ)

    if stage >= 6:
        NTOK = SEQ + NS
        hTall = big2[:, 0:8 * NTOK].rearrange("p (k t) -> p k t", t=NTOK)
        hidT = wbig[:, 0:6, 2560:3072]
        sg = q_hg
        groups = [(0, 6), (6, 6), (12, 5), (17, 5)]
        ld(mix[:], g_fin.partition_broadcast(128), ["mix"])
        for i in range(NT + 1):
            T = 128 if i < NT else NS
            x_, xk = load_rows(xres[i * 128:i * 128 + T, :], T, rk=[f"xres{i}"])
            rstd_of(x_[0:T, :], T, xk)
            P.op("dve", lambda h, T=T, x_=x_: h.tensor_scalar(out=hb[0:T, :], in0=x_[0:T, :], scalar1=ssq[0:T, 0:1], scalar2=None, op0=ALU.mult), r=[xk, "ssq"], w=["hb"])
            for c in range(8):
                P.op("pe", lambda h, c=c, T=T: h.transpose(out=psT[:, c * 128:c * 128 + T], in_=hb[0:T, c * 128:(c + 1) * 128], identity=ident_b[0:T, 0:T]), r=["hb", "ident_b"], w=["psT"])
            P.op("act", lambda h, T=T, i=i: h.activation(out=hTall[:, :, i * 128:i * 128 + T], in_=psT[:, 0:1024].rearrange("p (c t) -> p c t", t=128)[:, :, 0:T], func=AF.Copy),
                 r=["psT"], w=["hTall"])
        blocks = [(0, 512), (512, 512), (1024, 512), (1536, 512), (SEQ, NS)]
        for gi, (f0, nf) in enumerate(groups):
            load_weight(w_gate, wbig[:, :, 0:768], nf * 128, gk="ffn", key="wbig", col0=f0 * 128)
            load_weight(w_up, wbig[:, :, 768:1536], nf * 128, gk="ffn", key="wbig", col0=f0 * 128)
            load_weight(w_down[f0 * 128:(f0 + nf) * 128, :], wbig[:, 0:nf, 1536:2560], D, key="wbig", rows=nf * 128)
            for (t0, n) in blocks:
                for f in range(nf):
                    pg, pgk = nxt("A")
                    for k in range(8):
                        P.op("pe", lambda h, k=k, f=f, pg=pg, t0=t0, n=n: h.matmul(pg[:, 0:n], lhsT=wbig[:, k, f * 128:(f + 1) * 128], rhs=hTall[:, k, t0:t0 + n], start=(k == 0), stop=(k == 7)),
                             r=["wbig", "hTall"], w=[pgk])
                    pu_, puk_ = nxt("S")
                    for k in range(8):
                        P.op("pe", lambda h, k=k, f=f, pu_=pu_, t0=t0, n=n: h.matmul(pu_[:, 0:n], lhsT=wbig[:, k, 768 + f * 128:768 + (f + 1) * 128], rhs=hTall[:, k, t0:t0 + n], start=(k == 0), stop=(k == 7)),
                             r=["wbig", "hTall"], w=[puk_])
                    P.op("act", lambda h, pg=pg, n=n: h.activation(out=sg[:, 0:n], in_=pg[:, 0:n], func=AF.Silu), r=[pgk], w=["q_hg"])
                    P.op("dve", lambda h, f=f, pu_=pu_, n=n: h.tensor_tensor(out=hidT[:, f, 0:n], in0=pu_[:, 0:n], in1=sg[:, 0:n], op=ALU.mult), r=[puk_, "q_hg"], w=["hidT"])
                for st_ in range((n + 127) // 128):
                    T = min(128, n - st_ * 128)
                    i = t0 // 128 + st_
                    ydst = yp[i * 128:(i + 1) * 128, :] if i < NT else ys[0:NS, :]
                    src = xres if gi == 0 else facc
                    x_, xk = load_rows(src[i * 128:i * 128 + T, :], T, rk=[f"xres{i}" if gi == 0 else f"facc{i}"])
                    for cb in range(2):
                        p_, pk = nxt("O")
                        for f in range(nf):
                            P.op("pe", lambda h, f=f, cb=cb, p_=p_, T=T, st_=st_: h.matmul(p_[0:T, :], lhsT=hidT[:, f, st_ * 128:st_ * 128 + T], rhs=wbig[:, f, 1536 + cb * 512:1536 + (cb + 1) * 512],
                                                                                          start=(f == 0), stop=(f == nf - 1)), r=["hidT", "wbig"], w=[pk])
                        P.op("dve", lambda h, p_=p_, cb=cb, x_=x_, T=T: h.tensor_tensor(out=xo[0:T, cb * 512:(cb + 1) * 512], in0=p_[0:T, :], in1=x_[0:T, cb * 512:(cb + 1) * 512], op=ALU.add), r=[pk, xk], w=["xo"])
                    if gi < len(groups) - 1:
                        P.dma("sp", lambda h, i=i, T=T: h.dma_start(out=facc[i * 128:i * 128 + T, :], in_=xo[0:T, :]), r=["xo"], w=[f"facc{i}"])
                    else:
                        rstd_of(xo[0:T, :], T, "xo")
                        P.op("dve", lambda h, T=T: h.scalar_tensor_tensor(out=xo[0:T, :], in0=xo[0:T, :], scalar=ssq[0:T, 0:1], in1=mix[0:T, :], op0=ALU.mult, op1=ALU.mult), r=["xo", "ssq", "mix"], w=["xo"])
                        P.dma("sp", lambda h, T=T, ydst=ydst: h.dma_start(out=ydst, in_=xo[0:T, :]), r=["xo"], w=["out_yp"])

    if stage in (4, 5):
        x_, xk = load_rows(xres[0:128, :], 128, rk=["xres0"])
        P.dma("sp", lambda h: h.dma_start(out=yp[0:128, :], in_=x_[:]), r=[xk], w=["out_yp"])
    sems = {e: es.enter_context(nc.semaphore("s_" + e)) for e in ("pe", "act", "dve", "pool")}
    dsems = {q: [es.enter_context(nc.semaphore(f"d_{q}{i}")) for i in range(RING)] for q in ("sp", "pool")}
    block = es.enter_context(nc.Block())

    @block.tensor
    def _(h):
        P.emit("pe", h, sems, dsems)

    @block.scalar
    def _(h):
        P.emit("act", h, sems, dsems)

    @block.vector
    def _(h):
        P.emit("dve", h, sems, dsems)

    @block.gpsimd
    def _(h):
        P.emit("pool", h, sems, dsems)

    @block.sync
    def _(h):
        P.emit("sp", h, sems, dsems)

    es.close()
    return nc


def make_consts():
    c = {}
    c["c_ident"] = np.eye(128, dtype=np.float32)
    s = np.arange(128)
    c["c_caus"] = (s[:, None] <= s[None, :]).astype(np.float32)
    same = (s[:, None] // 64) == (s[None, :] // 64)
    c["c_u1"] = ((s[:, None] <= s[None, :]) & same).astype(np.float32)
    c["c_u2"] = (s[:, None] > s[None, :]).astype(np.float32)
    am = ((s[:, None] <= s[None, :]) & same).astype(np.float32)
    c["c_am"] = am
    ind = np.zeros((128, 2), np.float32); ind[:64, 0] = 1.0; ind[:, 1] = 1.0
    c["c_ind"] = ind
    inv = (10000.0 ** (-np.arange(0, 64, 2, dtype=np.float32) / 64)).astype(np.float32)
    pos = np.arange(SEQ, dtype=np.float32)
    ang = (pos[:, None] * inv[None, :]).astype(np.float32)
    c["c_cos"] = np.cos(ang.astype(np.float64)).astype(np.float32)
    c["c_sin"] = np.sin(ang.astype(np.float64)).astype(np.float32)
    angs = (np.float32(16384.0) * inv[None, :]).astype(np.float32)
    c["c_coss"] = np.cos(angs.astype(np.float64)).astype(np.float32)
    c["c_sins"] = np.sin(angs.astype(np.float64)).astype(np.float32)
    c["c_i4"] = np.eye(4, dtype=np.float32).reshape(1, 16)
    c["c_iota"] = (np.arange(128) % 64).astype(np.float32).reshape(128, 1)
    cpos = np.zeros((8, 4), np.float32); cneg = np.zeros((8, 4), np.float32)
    for hh in range(4):
        cpos[2 * hh, hh] = 1.0; cneg[2 * hh + 1, hh] = 1.0
    c["c_cpos"] = cpos; c["c_cneg"] = cneg
    bm = np.zeros((4, 512), np.float32)
    for hh in range(4):
        bm[hh, hh * 128:(hh + 1) * 128] = 1.0
    c["c_bm"] = bm
    return c


def core_inputs(c, I, consts, nphys, ckf=None, cvf=None):
    f = np.ascontiguousarray
    gc = lambda v: f(np.asarray(v).reshape(8, 128).T)
    m = dict(consts)
    m.update(
        ck=ckf if ckf is not None else I["cache_k"].reshape(nphys * 64, 1024), cv=cvf if cvf is not None else I["cache_v"].reshape(nphys * 64, 1024),
        xp=f(I["x_prompt"][c]), xs=f(I["x_sample"][4 * c:4 * c + 4, 0]), mem=f(I["mem_prompt"][c]),
        cmk=f(I["cache_mem_k"][0, 4 * c:4 * c + 4].reshape(NS * 256, D)), cmv=f(I["cache_mem_v"][0, 4 * c:4 * c + 4].reshape(NS * 256, D)),
        sh=f(I["state_hgrn"][0, 4 * c:4 * c + 4].reshape(NS * 512, 128)), pt=f(I["page_table"][4 * c:4 * c + 4]),
        w_in=I["w_in"][0], w_out=I["w_out"][0], w_mq=I["w_mq"][0], w_mk=I["w_mk"][0], w_mv=I["w_mv"][0], w_mo=I["w_mo"][0],
        w_gate=I["w_gate"][0], w_up=I["w_up"][0], w_down=I["w_down"][0],
        g_mix=gc(I["norm_mix"][0]), g_mq=gc(I["norm_mem_q"][0]), g_mkv=gc(I["norm_mem_kv"][0]), g_ffn=gc(I["norm_ffn"][0]),
        g_fin=f(I["norm_final"].reshape(1, D)), hg_lb=f(I["hg_lb"]), hg_on=f(I["hg_onorm"].reshape(1, 128)), da_on=f(I["da_onorm"].reshape(1, 128)),
        da_lam=f(I["da_lambda"].reshape(1, 256)),
    )
    return m


def kernel(**I):
    I = {k: np.asarray(v) for k, v in I.items()}
    nphys = I["cache_k"].shape[1]
    consts = make_consts()
    nc = build(nphys)
    ckf = I["cache_k"].reshape(nphys * 64, 1024); cvf = I["cache_v"].reshape(nphys * 64, 1024)
    in_maps = [core_inputs(c, I, consts, nphys, ckf, cvf) for c in range(8)]
    res = run_bass_kernel_spmd(nc, in_maps, core_ids=list(range(8))).results
    cat = lambda k: np.stack([r[k] for r in res])
    y_prompt = cat("yp")
    y_sample = cat("ys").reshape(32, 1, D)
    hsp = cat("hsp").reshape(1, 8, 4, 128, 128)
    kp = cat("kp").reshape(1, 8, SEQ, 4, 128)
    vp = cat("vp").reshape(1, 8, SEQ, 4, 128)
    mkp = cat("mkp").reshape(1, 8, 256, 4, 256)
    mvp = cat("mvp").reshape(1, 8, 256, 4, 256)
    hss = cat("hss").reshape(1, 32, 4, 128, 128)
    ks = cat("ks").reshape(1, 32, 1, 4, 128)
    vs = cat("vs").reshape(1, 32, 1, 4, 128)
    return (y_prompt, y_sample, hsp, kp, vp, mkp, mvp, hss, ks, vs)
```
